# Optimizing a Trainium2 kernel written in Bass

```python
import jax, jax.numpy as jnp
from jax import lax
import numpy as np

D_MODEL = 1024
BATCH = 16
SEQ = 2048
DEPTH = 1

RMS_EPS = 1e-6
NEG_INF = -1e30

SSD_EXPAND = 2
SSD_D_INNER = SSD_EXPAND * D_MODEL
SSD_HEAD_DIM = 64
SSD_N_HEADS = SSD_D_INNER // SSD_HEAD_DIM
SSD_N_GROUPS = 8
SSD_D_STATE = 128
SSD_CONV = 4
SSD_CHUNK = 128
SSD_GN = SSD_N_GROUPS * SSD_D_STATE
SSD_CONV_DIM = SSD_D_INNER + 2 * SSD_GN

ATT_HEAD_DIM = 64
ATT_N_HEADS = 16
ATT_WIDTH = ATT_N_HEADS * ATT_HEAD_DIM
MOBA_BLOCK = 256
MOBA_TOPK = 3
ATT_Q_BLOCK = 128
ROPE_THETA = 500000.0
ROT_DIM = ATT_HEAD_DIM // 4

FFN_HIDDEN = 2816
FFN_CONV = 3

IN_SIZES = (SSD_D_INNER, SSD_CONV_DIM, SSD_N_HEADS, ATT_WIDTH, ATT_WIDTH, ATT_WIDTH, 2 * D_MODEL)
IN_DIM = int(sum(IN_SIZES))
IN_OFFSETS = tuple(int(v) for v in np.cumsum(IN_SIZES)[:-1])

kernel_name = "hybrid_ssd_moba_convffn_block"


def rmsnorm(x, w):
    xf = x.astype(jnp.float32)
    y = xf * lax.rsqrt(jnp.mean(xf * xf, axis=-1, keepdims=True) + RMS_EPS)
    return (y * w.astype(jnp.float32)).astype(x.dtype)


def causal_dwconv(x, w, b):
    k, c = w.shape
    y = lax.conv_general_dilated(x, w[:, None, :].astype(x.dtype), window_strides=(1,),
                                 padding=[(k - 1, 0)], dimension_numbers=('NWC', 'WIO', 'NWC'),
                                 feature_group_count=c)
    return y + b.astype(x.dtype)


def segsum(a):
    t = a.shape[-1]
    cs = jnp.cumsum(a, axis=-1)
    diff = cs[..., :, None] - cs[..., None, :]
    mask = jnp.tril(jnp.ones((t, t), dtype=bool))
    return jnp.where(mask, diff, -jnp.inf)


def ssd_scan(X, dt, A, Bm, Cm):
    b, s = X.shape[:2]
    c, l, g = s // SSD_CHUNK, SSD_CHUNK, SSD_N_GROUPS
    r = SSD_N_HEADS // g
    Xc = (X * dt[..., None]).reshape(b, c, l, g, r, SSD_HEAD_DIM)
    Adt = (dt * A).reshape(b, c, l, g, r).transpose(0, 3, 4, 1, 2)
    Bc = Bm.reshape(b, c, l, g, SSD_D_STATE)
    Cc = Cm.reshape(b, c, l, g, SSD_D_STATE)
    A_cs = jnp.cumsum(Adt, axis=-1)
    L = jnp.exp(segsum(Adt))
    CB = jnp.einsum('bclgn,bcsgn->bcgls', Cc, Bc)
    y_diag = jnp.einsum('bcgls,bgrcls,bcsgrp->bclgrp', CB, L, Xc)
    decay_states = jnp.exp(A_cs[..., -1:] - A_cs)
    states = jnp.einsum('bclgn,bgrcl,bclgrp->bcgrpn', Bc, decay_states, Xc)
    states = jnp.concatenate([jnp.zeros_like(states[:, :1]), states], axis=1)
    chunk_a = jnp.pad(A_cs[..., -1], ((0, 0), (0, 0), (0, 0), (1, 0)))
    decay_chunk = jnp.exp(segsum(chunk_a))
    states = jnp.einsum('bgrzc,bcgrpn->bzgrpn', decay_chunk, states)[:, :-1]
    y_off = jnp.einsum('bclgn,bcgrpn,bgrcl->bclgrp', Cc, states, jnp.exp(A_cs))
    return (y_diag + y_off).reshape(b, s, SSD_N_HEADS, SSD_HEAD_DIM)


def ssd_mixer(z, xbc_raw, dt_raw, conv_w, conv_b, dt_bias, a_log, d_skip, norm_w):
    b, s, _ = z.shape
    xbc = jax.nn.silu(causal_dwconv(xbc_raw, conv_w, conv_b))
    xs, Bm, Cm = jnp.split(xbc, [SSD_D_INNER, SSD_D_INNER + SSD_GN], axis=-1)
    dt = jax.nn.softplus(dt_raw.astype(jnp.float32) + dt_bias.astype(jnp.float32))
    A = -jnp.exp(a_log.astype(jnp.float32))
    X = xs.reshape(b, s, SSD_N_HEADS, SSD_HEAD_DIM).astype(jnp.float32)
    y = ssd_scan(X, dt, A,
                 Bm.reshape(b, s, SSD_N_GROUPS, SSD_D_STATE).astype(jnp.float32),
                 Cm.reshape(b, s, SSD_N_GROUPS, SSD_D_STATE).astype(jnp.float32))
    y = (y + X * d_skip.astype(jnp.float32)[:, None]).reshape(b, s, SSD_D_INNER)
    y = y * jax.nn.silu(z.astype(jnp.float32))
    yg = y.reshape(b, s, SSD_N_GROUPS, SSD_D_INNER // SSD_N_GROUPS)
    yg = yg * lax.rsqrt(jnp.mean(yg * yg, axis=-1, keepdims=True) + RMS_EPS)
    y = yg.reshape(b, s, SSD_D_INNER) * norm_w.astype(jnp.float32)
    return y.astype(z.dtype)


def partial_rope(t, pos):
    inv_freq = jnp.float32(ROPE_THETA) ** (-jnp.arange(0, ROT_DIM, 2, dtype=jnp.float32) / ROT_DIM)
    ang = pos[:, None] * inv_freq[None, :]
    ang = jnp.concatenate([ang, ang], axis=-1)
    cos = jnp.cos(ang)[None, :, None, :].astype(t.dtype)
    sin = jnp.sin(ang)[None, :, None, :].astype(t.dtype)
    rot, rest = t[..., :ROT_DIM], t[..., ROT_DIM:]
    r1, r2 = rot[..., :ROT_DIM // 2], rot[..., ROT_DIM // 2:]
    rot_half = jnp.concatenate([-r2, r1], axis=-1)
    return jnp.concatenate([rot * cos + rot_half * sin, rest], axis=-1)


def moba_attention(q, k, v):
    b, s, h, hd = q.shape
    scale = hd ** -0.5
    q, k, v = (t.transpose(0, 2, 1, 3) for t in (q, k, v))
    nb = -(-s // MOBA_BLOCK)
    s_pad = nb * MOBA_BLOCK
    kp = jnp.pad(k, ((0, 0), (0, 0), (0, s_pad - s), (0, 0)))
    vp = jnp.pad(v, ((0, 0), (0, 0), (0, s_pad - s), (0, 0)))
    k_blocks = kp.reshape(b, h, nb, MOBA_BLOCK, hd)
    v_blocks = vp.reshape(b, h, nb, MOBA_BLOCK, hd)
    k_mean = jnp.mean(k_blocks.astype(jnp.float32), axis=3)
    gate = jnp.einsum('bhsd,bhnd->bhsn', q.astype(jnp.float32), k_mean)
    q_blk = jnp.arange(s) // MOBA_BLOCK
    past = jnp.arange(nb)[None, :] < q_blk[:, None]
    gate = jnp.where(past[None, None], gate, -jnp.inf)
    n_sel = min(MOBA_TOPK, nb)
    _, sel_idx = lax.top_k(gate, n_sel)
    nq = s // ATT_Q_BLOCK
    gather_blocks = jax.vmap(lambda blocks, ix: blocks[ix])

    def attend(i):
        bi, ci = i // nq, i % nq
        q0 = ci * ATT_Q_BLOCK
        blk = q0 // MOBA_BLOCK
        qc = lax.dynamic_slice_in_dim(q[bi], q0, ATT_Q_BLOCK, axis=1).astype(jnp.float32)
        idx = lax.dynamic_slice_in_dim(sel_idx[bi], q0, ATT_Q_BLOCK, axis=1)
        k_sel = gather_blocks(k_blocks[bi], idx).astype(jnp.float32)
        v_sel = gather_blocks(v_blocks[bi], idx).astype(jnp.float32)
        s_sel = jnp.einsum('hqd,hqjkd->hqjk', qc, k_sel) * scale
        valid = (jnp.arange(n_sel) < blk)[None, None, :, None]
        s_sel = jnp.where(valid, s_sel, NEG_INF)
        k_own = lax.dynamic_slice_in_dim(kp[bi], blk * MOBA_BLOCK, MOBA_BLOCK, axis=1).astype(jnp.float32)
        v_own = lax.dynamic_slice_in_dim(vp[bi], blk * MOBA_BLOCK, MOBA_BLOCK, axis=1).astype(jnp.float32)
        s_own = jnp.einsum('hqd,hkd->hqk', qc, k_own) * scale
        causal = (blk * MOBA_BLOCK + jnp.arange(MOBA_BLOCK))[None, :] <= (q0 + jnp.arange(ATT_Q_BLOCK))[:, None]
        s_own = jnp.where(causal[None], s_own, NEG_INF)
        scores = jnp.concatenate([s_sel.reshape(h, ATT_Q_BLOCK, n_sel * MOBA_BLOCK), s_own], axis=-1)
        p = jax.nn.softmax(scores, axis=-1)
        p_sel = p[..., :n_sel * MOBA_BLOCK].reshape(h, ATT_Q_BLOCK, n_sel, MOBA_BLOCK)
        p_own = p[..., n_sel * MOBA_BLOCK:]
        out = jnp.einsum('hqjk,hqjkd->hqd', p_sel, v_sel) + jnp.einsum('hqk,hkd->hqd', p_own, v_own)
        return out.astype(v.dtype)

    outs = lax.map(attend, jnp.arange(b * nq))
    outs = outs.reshape(b, nq, h, ATT_Q_BLOCK, hd).transpose(0, 1, 3, 2, 4)
    return outs.reshape(b, s, h * hd)


def hybrid_layer(x, norm1_w, w_in, b_gate, ssd_conv_w, ssd_conv_b, ssd_dt_bias, ssd_a_log, ssd_d,
                 ssd_norm_w, w_ssd_proj, w_att_proj, w_out, norm2_w, w_ffn_up, ffn_conv_w, ffn_conv_b,
                 w_ffn_down):
    b, s, _ = x.shape
    hn = rmsnorm(x, norm1_w)
    proj = hn @ w_in
    z, xbc_raw, dt_raw, q, k, v, g_raw = jnp.split(proj, IN_OFFSETS, axis=-1)
    y_ssd = ssd_mixer(z, xbc_raw, dt_raw, ssd_conv_w, ssd_conv_b, ssd_dt_bias, ssd_a_log, ssd_d, ssd_norm_w)
    pos = jnp.arange(s, dtype=jnp.float32)
    q = partial_rope(q.reshape(b, s, ATT_N_HEADS, ATT_HEAD_DIM), pos)
    k = partial_rope(k.reshape(b, s, ATT_N_HEADS, ATT_HEAD_DIM), pos)
    y_att = moba_attention(q, k, v.reshape(b, s, ATT_N_HEADS, ATT_HEAD_DIM))
    gates = jax.nn.sigmoid((g_raw + b_gate).astype(jnp.float32)).astype(x.dtype)
    g_ssd, g_att = jnp.split(gates, 2, axis=-1)
    merged = g_ssd * (y_ssd @ w_ssd_proj) + g_att * (y_att @ w_att_proj)
    x = x + merged @ w_out
    h2 = rmsnorm(x, norm2_w)
    a, gb = jnp.split(h2 @ w_ffn_up, 2, axis=-1)
    a = causal_dwconv(a, ffn_conv_w, ffn_conv_b)
    return x + (jax.nn.silu(a) * gb) @ w_ffn_down


def setup_inputs(seed: int = 0) -> dict:
    key = jax.random.key(seed)
    ks = jax.random.split(key, 20)
    f32 = jnp.float32

    def nrm(k, shape, scale):
        return jax.random.normal(k, shape, f32) * scale

    L = DEPTH
    dt0 = jnp.exp(jax.random.uniform(ks[5], (L, SSD_N_HEADS), f32, np.log(1e-3), np.log(1e-1)))
    return {
        "x": nrm(ks[0], (BATCH, SEQ, D_MODEL), 1.0),
        "norm1_w": 1.0 + nrm(ks[1], (L, D_MODEL), 0.02),
        "w_in": nrm(ks[2], (L, D_MODEL, IN_DIM), D_MODEL ** -0.5),
        "b_gate": nrm(ks[3], (L, 2 * D_MODEL), 0.01),
        "ssd_conv_w": nrm(ks[4], (L, SSD_CONV, SSD_CONV_DIM), SSD_CONV ** -0.5),
        "ssd_conv_b": nrm(ks[6], (L, SSD_CONV_DIM), 0.01),
        "ssd_dt_bias": dt0 + jnp.log(-jnp.expm1(-dt0)),
        "ssd_a_log": jnp.log(jax.random.uniform(ks[7], (L, SSD_N_HEADS), f32, 1.0, 16.0)),
        "ssd_d": 1.0 + nrm(ks[8], (L, SSD_N_HEADS), 0.01),
        "ssd_norm_w": 1.0 + nrm(ks[9], (L, SSD_D_INNER), 0.02),
        "w_ssd_proj": nrm(ks[10], (L, SSD_D_INNER, D_MODEL), SSD_D_INNER ** -0.5),
        "w_att_proj": nrm(ks[11], (L, ATT_WIDTH, D_MODEL), ATT_WIDTH ** -0.5),
        "w_out": nrm(ks[12], (L, D_MODEL, D_MODEL), D_MODEL ** -0.5),
        "norm2_w": 1.0 + nrm(ks[13], (L, D_MODEL), 0.02),
        "w_ffn_up": nrm(ks[14], (L, D_MODEL, 2 * FFN_HIDDEN), D_MODEL ** -0.5),
        "ffn_conv_w": nrm(ks[15], (L, FFN_CONV, FFN_HIDDEN), FFN_CONV ** -0.5),
        "ffn_conv_b": nrm(ks[16], (L, FFN_HIDDEN), 0.01),
        "w_ffn_down": nrm(ks[17], (L, FFN_HIDDEN, D_MODEL), FFN_HIDDEN ** -0.5),
        "norm_f_w": 1.0 + nrm(ks[18], (D_MODEL,), 0.02),
    }


def reference(x, norm1_w, w_in, b_gate, ssd_conv_w, ssd_conv_b, ssd_dt_bias, ssd_a_log, ssd_d, ssd_norm_w,
              w_ssd_proj, w_att_proj, w_out, norm2_w, w_ffn_up, ffn_conv_w, ffn_conv_b, w_ffn_down, norm_f_w):
    for l in range(DEPTH):
        x = hybrid_layer(x, norm1_w[l], w_in[l], b_gate[l], ssd_conv_w[l], ssd_conv_b[l], ssd_dt_bias[l],
                         ssd_a_log[l], ssd_d[l], ssd_norm_w[l], w_ssd_proj[l], w_att_proj[l], w_out[l],
                         norm2_w[l], w_ffn_up[l], ffn_conv_w[l], ffn_conv_b[l], w_ffn_down[l])
    return rmsnorm(x, norm_f_w)
```

```python
import numpy as np
from contextlib import ExitStack
import concourse.bass as bass
import concourse.mybir as mybir
from concourse.bass_utils import run_bass_kernel_spmd

F32 = mybir.dt.float32
BF16 = mybir.dt.bfloat16
AF = mybir.ActivationFunctionType
ALU = mybir.AluOpType
AX = mybir.AxisListType

ENGS = ("pe", "act", "dve", "pool", "sp")
T = 2048
D = 1024
NT = 16
EPS = 1e-6
OZ, OX, OB, OC, ODT, OQ, OK_, OV, OG = 0, 2048, 4096, 5120, 6144, 6176, 7200, 8224, 9248
CW0, CB0, FW0, FB0, BG0, SN0, DP0, DTB, ALG, NPP = 0, 128, 160, 226, 248, 264, 280, 296, 297, 298
NCF = 642
NCB = 640
NEGV = -30000.0
DEBUG = {}


class Sched:
    def __init__(self, nc, n_dma_sems=32):
        self.nc = nc
        self.ops = {e: [] for e in ENGS}
        self.cnt = {e: 0 for e in ENGS}
        self.sem = {}
        self.last_w = {}
        self.readers = {}
        self.seen = {e: {} for e in ENGS}
        self.n_dma_sems = n_dma_sems
        self.dma_cnt = {}
        self.dma_rr = 0

    def open(self, stack):
        for e in ENGS:
            self.sem[e] = stack.enter_context(self.nc.semaphore("s_" + e))
        for i in range(self.n_dma_sems):
            nm = "dma%d" % i
            self.sem[nm] = stack.enter_context(self.nc.semaphore("s_" + nm))
            self.dma_cnt[nm] = 0

    def _deps(self, eng, reads, writes):
        deps = {}

        def add(tok, same_ok):
            if tok is None:
                return
            s, v = tok
            if s == eng and not same_ok:
                return
            if deps.get(s, 0) < v:
                deps[s] = v

        for k in reads:
            add(self.last_w.get(k), True)
        for k in writes:
            add(self.last_w.get(k), True)
            for t in self.readers.get(k, ()):
                add(t, False)
        out = []
        seen = self.seen[eng]
        for s, v in deps.items():
            if seen.get(s, 0) < v:
                seen[s] = v
                out.append((s, v))
        return out

    def _commit(self, tok, reads, writes):
        for k in writes:
            self.last_w[k] = tok
            self.readers[k] = []
        for k in reads:
            if k not in writes:
                self.readers.setdefault(k, []).append(tok)

    def op(self, eng, fn, reads=(), writes=()):
        waits = self._deps(eng, reads, writes)
        self.cnt[eng] += 1
        tok = (eng, self.cnt[eng])
        if eng == "pe":
            self.seen[eng][eng] = self.cnt[eng]
        self.ops[eng].append((waits, fn, (eng, 1)))
        self._commit(tok, reads, writes)
        return tok

    def dma(self, queue, fn, reads=(), writes=()):
        waits = self._deps(queue, reads, writes)
        name = "dma%d" % (self.dma_rr % self.n_dma_sems)
        self.dma_rr += 1
        self.dma_cnt[name] += 16
        tok = (name, self.dma_cnt[name])
        prev = self.dma_cnt[name] - 16
        if prev > 0 and self.seen[queue].get(name, 0) < prev:
            self.seen[queue][name] = prev
            waits = waits + [(name, prev)]
        self.ops[queue].append((waits, fn, (name, 16)))
        self._commit(tok, reads, writes)
        return tok

    def barrier(self):
        for e in ENGS:
            waits = []
            for s in ENGS:
                v = self.cnt[s]
                if v > 0 and self.seen[e].get(s, 0) < v:
                    self.seen[e][s] = v
                    waits.append((s, v))
            for nm, v in self.dma_cnt.items():
                if v > 0 and self.seen[e].get(nm, 0) < v:
                    self.seen[e][nm] = v
                    waits.append((nm, v))
            if waits:
                self.ops[e].append((waits, None, None))
        self.last_w = {}
        self.readers = {}

    def emit(self):
        nc = self.nc
        sem = self.sem

        def replay(e, name):
            for waits, fn, inc in self.ops[name]:
                for s, v in waits:
                    e.wait_ge(sem[s], v)
                if fn is None:
                    continue
                ins = fn(e)
                ins.then_inc(sem[inc[0]], inc[1])

        with nc.Block() as block:
            @block.tensor
            def _(e):
                replay(e, "pe")

            @block.scalar
            def _(e):
                replay(e, "act")

            @block.vector
            def _(e):
                replay(e, "dve")

            @block.gpsimd
            def _(e):
                replay(e, "pool")

            @block.sync
            def _(e):
                replay(e, "sp")


class PhasesA:
    def phase_norm1(self, sq):
        nc = self.nc
        with ExitStack() as ph:
            sb = lambda n, s, d: ph.enter_context(nc.sbuf_tensor("%s_%d" % (n, sq), s, d))
            nwb = sb("n1wb", [128, D], F32)
            xt = [sb("n1x%d" % i, [128, D], F32) for i in range(3)]
            xs = [sb("n1s%d" % i, [128, D], BF16) for i in range(3)]
            junk = sb("n1j", [128, D], BF16)
            ss = sb("n1ss", [128, 16], F32)
            r1 = sb("n1r1", [128, 16], F32)
            rstd = sb("n1rs", [128, 16], F32)
            self.dma("sp", nwb[:], self.nw[0:1, :].partition_broadcast(128), [], ["n1wb"])
            for t in range(NT):
                b = t % 3
                self.dma("sp", xt[b][:], self.x[sq, t * 128:(t + 1) * 128, :], [], [("n1x", b)])
                self.act(junk[:], xt[b][:], AF.Square, [("n1x", b)], ["n1j", ("n1ss", t)], accum=ss[:, t:t + 1])
                self.act(r1[:, t:t + 1], ss[:, t:t + 1], AF.Sqrt, [("n1ss", t)], [("n1r1", t)], bias=self.epsc[:, 0:1], scale=1.0 / D)
                self.recip(rstd[:, t:t + 1], r1[:, t:t + 1], [("n1r1", t)], [("n1rs", t)])
                self.stt(xs[b][:], xt[b][:], rstd[:, t:t + 1], nwb[:], ALU.mult, ALU.mult,
                         [("n1x", b), ("n1rs", t), "n1wb"], [("n1s", b)])
                pbk = t % 4
                pb = self.bankb(pbk)
                for j in range(8):
                    self.tr(pb[:, j * 128:(j + 1) * 128], xs[b][:, j * 128:(j + 1) * 128], self.ident,
                            [("n1s", b), "cb"], ["ps%d" % pbk])
                self.evac(self.hnT[:, :, t * 128:(t + 1) * 128], pb.rearrange("p (k c) -> p k c", k=8),
                          ["ps%d" % pbk], ["hnT"])
            self.dbg("hnT", self.hnT[:], sq, ["hnT"])

    def phase_dt_ssd(self, sq):
        nc = self.nc
        S = self.S
        with ExitStack() as ph:
            sb = lambda n, s, d: ph.enter_context(nc.sbuf_tensor("%s_%d" % (n, sq), s, d))
            selg = sb("selg", [128, 8192], BF16)
            self.dma("pool", selg[:], self.selg_d, [], ["selg"])
            cs_cat = sb("cs_cat", [128, T], BF16)
            f_cat = sb("f_cat", [128, T], BF16)
            self.memset(cs_cat[:], 0.0, [], ["cscat"])
            self.memset(f_cat[:], 0.0, [], ["fcat"])
            ecs_tok = sb("ecs_tok", [128, 16, 32], F32)
            w_tok = sb("w_tok", [128, 16, 32], F32)
            dec_rep = sb("dec_rep", [128, 16, 32], F32)
            with ExitStack() as t1s:
                sb1 = lambda n, s, d: t1s.enter_context(nc.sbuf_tensor("%s_%d" % (n, sq), s, d))
                A = [sb1("dtA%d" % i, [64, T], F32) for i in range(4)]
                mask = sb1("dtmask", [64, T], F32)
                hi64 = sb1("dthi64", [64, T], BF16)
                dec = sb1("dtdec", [32, 16], F32)
                decx = sb1("dtdecx", [32, 16, 32], F32)
                negA = sb1("dtnegA", [64, 1], F32)
                wdt = sb1("wdt", [128, 8, 64], BF16)
                self.dma("pool", wdt[:, :, 0:32], self.w_in_r[:, :, ODT:ODT + 32], [], ["wdt"])
                self.dma("pool", wdt[:, :, 32:64], self.w_in_r[:, :, ODT:ODT + 32], [], ["wdt"])
                for tb in range(4):
                    for k in range(8):
                        self.mm(self.bank(tb)[0:64, :], wdt[:, k, :], self.hnT[:, k, tb * 512:(tb + 1) * 512],
                                k == 0, k == 7, ["wdt", "hnT"], ["ps%d" % tb])
                    self.act(A[0][:, tb * 512:(tb + 1) * 512], self.bank(tb)[0:64, :], AF.Exp, ["ps%d" % tb, "pp"], ["A0"],
                             bias=self.pp[0:64, DTB:DTB + 1])
                self.act(A[1][:], A[0][:], AF.Ln, ["A0"], ["A1"], bias=self.onec[0:64, 0:1])
                self.act(A[2][:], A[1][:], AF.Ln, ["A1"], ["A2"])
                self.act(negA[:], self.pp[0:64, ALG:ALG + 1], AF.Exp, ["pp"], ["negA"])
                self.ts(A[0][:], A[1][:], negA[:, 0:1], -1.0, ALU.mult, ALU.mult, ["A1", "negA", "A0"], ["A0"])
                self.memset(mask[:], 1.0, [], ["mask"])
                self.memset(mask[:].rearrange("p (c l) -> p c l", l=128)[:, :, 0:1], 0.0, ["mask"], ["mask"])
                S.op("dve", lambda e: e.tensor_tensor_scan(out=A[3][:], data0=mask[:], data1=A[0][:], initial=0.0,
                                                           op0=ALU.mult, op1=ALU.add), reads=["mask", "A0"], writes=["A3"])
                self.tt(A[2][:], A[2][:], A[3][:], ALU.subtract, ["A2", "A3"], ["A2"])
                self.act(A[1][:], A[3][:], AF.Exp, ["A3", "A1"], ["A1"])
                A3v = A[3][:].rearrange("p (c l) -> p c l", l=128)
                self.tt(A[0][:].rearrange("p (c l) -> p c l", l=128), A[2][:].rearrange("p (c l) -> p c l", l=128),
                        A3v[:, :, 127:128].to_broadcast([64, 16, 128]), ALU.add, ["A2", "A3", "A0"], ["A0"])
                self.act(A[0][:], A[0][:], AF.Exp, ["A0"], ["A0"])
                self.act(dec[:], A3v[0:32, :, 127], AF.Exp, ["A3"], ["dec"])
                for (src, dst, nm) in ((A[3], cs_cat, "cscat"), (A[2], f_cat, "fcat")):
                    self.vcopy(hi64[:], src[:], ["A3", "A2", "hi64"], ["hi64"])
                    self.vcopy(dst[0:32, :], hi64[0:32, :], ["hi64"], [nm])
                    self.tt(dst[32:64, :], src[32:64, :], hi64[32:64, :], ALU.subtract, ["A3", "A2", "hi64"], [nm])
                id32 = self.ident_f[0:32, 0:32]
                for c in range(16):
                    self.tr(self.bank(4)[:, c * 32:(c + 1) * 32], A[1][0:32, c * 128:(c + 1) * 128], id32, ["A1", "cf"], ["ps4"])
                    self.tr(self.bank(5)[:, c * 32:(c + 1) * 32], A[0][0:32, c * 128:(c + 1) * 128], id32, ["A0", "cf"], ["ps5"])
                self.vcopy(ecs_tok[:].rearrange("p c h -> p (c h)"), self.bank(4), ["ps4"], ["ecs_tok"])
                self.vcopy(w_tok[:].rearrange("p c h -> p (c h)"), self.bank(5), ["ps5"], ["w_tok"])
                self.tt(decx[:], dec[:].unsqueeze(2).to_broadcast([32, 16, 32]),
                        id32.unsqueeze(1).to_broadcast([32, 16, 32]), ALU.mult, ["dec", "cf"], ["decx"])
                self.mm(self.bank(6), self.ones_f[0:32, :], decx[:].rearrange("p c h -> p (c h)"), True, True,
                        ["ones_f", "decx"], ["ps6"])
                self.vcopy(dec_rep[:].rearrange("p c h -> p (c h)"), self.bank(6), ["ps6"], ["dec_rep"])
                self.dbg("ecs_tok", ecs_tok[:], sq, ["ecs_tok"])
                self.dbg("w_tok", w_tok[:], sq, ["w_tok"])
                self.dbg("dec_rep", dec_rep[:], sq, ["dec_rep"])
                S.barrier()
            with ExitStack() as gs:
                sbg = lambda n, s, d: gs.enter_context(nc.sbuf_tensor("%s_%d" % (n, sq), s, d))
                wx = [sbg("wx%d" % i, [128, 8, 512], BF16) for i in range(2)]
                wz = [sbg("wz%d" % i, [128, 8, 256], BF16) for i in range(2)]
                raw = [sbg("raw%d" % i, [128, 3 + T], BF16) for i in range(2)]
                xc = sbg("xc", [128, 4, T], BF16)
                xbt = sbg("xbt", [128, 16, 384], BF16)
                yT = sbg("yT", [128, 2, T], BF16)
                diag = sbg("diag", [128, 16, 128], BF16)
                L2 = [sbg("L2_%d" % i, [128, 512], F32) for i in range(2)]
                M2 = [sbg("M2_%d" % i, [128, 512], BF16) for i in range(2)]
                Xw_all = sbg("Xw_all", [128, 16, 256], BF16)
                t1 = sbg("t1", [128, 256], F32)
                ych = [sbg("ych%d" % i, [128, 256], F32) for i in range(2)]
                Sst = sbg("Sst", [128, 256], F32)
                Sd = sbg("Sd", [128, 256], F32)
                STs = sbg("STs", [128, 256], F32)
                Sbf = sbg("Sbf", [128, 256], BF16)
                sz = [sbg("sz%d" % i, [128, 512], BF16) for i in range(2)]
                sq_ = [sbg("sqq%d" % i, [128, 512], BF16) for i in range(2)]
                rr = [sbg("rr%d" % i, [128, 512], F32) for i in range(4)]
                for i in range(2):
                    self.memset(raw[i][:, 0:3], 0.0, [], [("raw", i)])

                def load_w(g):
                    wb = g % 2
                    self.dma("pool", wx[wb][:, :, 0:256], self.w_in_r[:, :, OX + g * 256:OX + (g + 1) * 256], [], [("wx", wb, 0)])
                    self.dma("pool", wx[wb][:, :, 256:384], self.w_in_r[:, :, OB + g * 128:OB + (g + 1) * 128], [], [("wx", wb, 1)])
                    self.dma("pool", wx[wb][:, :, 384:512], self.w_in_r[:, :, OC + g * 128:OC + (g + 1) * 128], [], [("wx", wb, 2)])
                    self.dma("pool", wz[wb][:], self.w_in_r[:, :, OZ + g * 256:OZ + (g + 1) * 256], [], [("wz", wb)])

                load_w(0)
                load_w(1)
                def prologue_a(g):
                    wb = g % 2
                    chs = [2 * g, 2 * g + 1, 16 + g, 24 + g]
                    for cc in range(4):
                        for k in range(4):
                            col = CW0 + chs[cc] * 4 + k
                            self.S.op("act", (lambda o_, c_: lambda e: e.mul(out=o_, in_=self.ident, mul=self.pp[:, c_:c_ + 1]))(diag[:, cc * 4 + k, :], col),
                                      reads=["cb", "pp"], writes=[("diag", cc)])
                    for cc in range(4):
                        rb = cc % 2
                        ch = chs[cc]
                        for tb in range(4):
                            bk = tb
                            for k in range(8):
                                self.mm(self.bank(bk), wx[wb][:, k, cc * 128:(cc + 1) * 128],
                                        self.hnT[:, k, tb * 512:(tb + 1) * 512], k == 0, k == 7,
                                        [("wx", wb, max(0, cc - 1)), "hnT"], ["ps%d" % bk])
                            self.evac(raw[rb][:, 3 + tb * 512:3 + (tb + 1) * 512], self.bank(bk), ["ps%d" % bk], [("raw", rb)])
                        if cc == 3:
                            prologue_b(g)
                        for tb in range(4):
                            bk = 4 + tb
                            for k in range(4):
                                self.mm(self.bank(bk), diag[:, cc * 4 + k, :], raw[rb][:, tb * 512 + k:tb * 512 + k + 512],
                                        k == 0, k == 3, [("diag", cc), ("raw", rb)], ["ps%d" % bk])
                            self.act(xc[:, cc, tb * 512:(tb + 1) * 512], self.bank(bk), AF.Silu, ["ps%d" % bk, "pp"],
                                     [("xc", cc)], bias=self.pp[:, CB0 + ch:CB0 + ch + 1])
                    if g == 0:
                        self.dbg("xc0", xc[:], sq, [("xc", i) for i in range(4)])
                def prologue_b(g):
                    wb = g % 2
                    chs = [2 * g, 2 * g + 1, 16 + g, 24 + g]
                    for c in range(16):
                        bk = c % 4
                        pb = self.bankb(bk)
                        for cc in range(3):
                            self.tr(pb[:, cc * 128:(cc + 1) * 128], xc[:, cc, c * 128:(c + 1) * 128], self.ident,
                                    [("xc", cc), "cb"], ["ps%d" % bk])
                        self.evac(xbt[:, c, :], pb[:, 0:384], ["ps%d" % bk], [("xbt", c // 4)])
                        if c % 4 == 3:
                            q4 = c // 4
                            self.tt(Xw_all[:, 4 * q4:4 * q4 + 4, :].rearrange("p c (h d) -> p c h d", h=4),
                                    xbt[:, 4 * q4:4 * q4 + 4, 0:256].rearrange("p c (h d) -> p c h d", h=4),
                                    w_tok[:, 4 * q4:4 * q4 + 4, 4 * g:4 * g + 4].unsqueeze(3).to_broadcast([128, 4, 4, 64]), ALU.mult,
                                    [("xbt", q4), "w_tok"], [("Xw", q4)], eng="pool")
                def chunkloop(g):
                    wb = g % 2
                    chs = [2 * g, 2 * g + 1, 16 + g, 24 + g]
                    def front(c):
                        p2 = c % 2
                        Db = self.bank(p2)
                        CBb = self.bank(2 + p2)
                        cs = slice(c * 128, (c + 1) * 128)
                        kD = "ps%d" % p2
                        kC = "ps%d" % (2 + p2)
                        self.mm(Db, self.ident, self.neg4, True, False, ["cb"], [kD])
                        for h in range(4):
                            j = 4 * g + h
                            self.mm(Db[:, h * 128:(h + 1) * 128], selg[:, j * 128:(j + 1) * 128], cs_cat[:, cs],
                                    False, False, ["selg", "cscat"], [kD])
                        self.mm(Db, f_cat[:, cs], selg[:, 4096 + g * 512:4096 + (g + 1) * 512], False, True, ["selg", "fcat"], [kD])
                        self.mm(CBb[:, 0:128], xc[:, 2, cs], xc[:, 3, cs], True, True, [("xc", 2), ("xc", 3)], [kC])
                        self.act(L2[p2][:], Db, AF.Exp, [kD], [("L2", p2)])
                        self.tt(M2[p2][:].rearrange("p (h l) -> p h l", h=4), L2[p2][:].rearrange("p (h l) -> p h l", h=4),
                                CBb[:, 0:128].unsqueeze(1).to_broadcast([128, 4, 128]), ALU.mult,
                                [("L2", p2), kC], [("M2", p2)])

                    def mid(c):
                        p2 = c % 2
                        cs = slice(c * 128, (c + 1) * 128)
                        Y1 = self.bank(4)
                        Y2 = self.bank(5)
                        STb = self.bank(6)
                        for h in range(4):
                            self.mm(Y1[:, h * 64:(h + 1) * 64], M2[p2][:, h * 128:(h + 1) * 128], xbt[:, c, h * 64:(h + 1) * 64],
                                    True, True, [("M2", p2), ("xbt", c // 4)], ["ps4"])
                        if c > 0:
                            self.mm(Y2[:, 0:256], xc[:, 3, cs], Sbf[:], True, True, [("xc", 3), "Sbf"], ["ps5"])
                        if c < 15:
                            self.mm(STb[:, 0:256], xbt[:, c, 256:384], Xw_all[:, c, :], True, True, [("xbt", c // 4), ("Xw", c // 4)], ["ps6"])
                        if c > 0:
                            self.tt(t1[:].rearrange("p (h d) -> p h d", h=4), Y2[:, 0:256].rearrange("p (h d) -> p h d", h=4),
                                    ecs_tok[:, c, 4 * g:4 * g + 4].unsqueeze(2).to_broadcast([128, 4, 64]), ALU.mult,
                                    ["ps5", "ecs_tok"], ["t1"])
                            self.tt(ych[p2][:], t1[:], Y1[:, 0:256], ALU.add, ["t1", "ps4"], [("ych", p2)])
                        else:
                            self.vcopy(ych[p2][:], Y1[:, 0:256], ["ps4"], [("ych", p2)])
                        if c < 15:
                            if c == 0:
                                self.acopy(Sst[:], STb[:, 0:256], ["ps6"], ["Sst"])
                            else:
                                self.acopy(STs[:], STb[:, 0:256], ["ps6"], ["STs"])
                                self.tt(Sd[:].rearrange("p (h d) -> p h d", h=4), Sst[:].rearrange("p (h d) -> p h d", h=4),
                                        dec_rep[:, c, 4 * g:4 * g + 4].unsqueeze(2).to_broadcast([128, 4, 64]), ALU.mult,
                                        ["Sst", "dec_rep"], ["Sd"], eng="pool")
                                self.tt(Sst[:], Sd[:], STs[:], ALU.add, ["Sd", "STs"], ["Sst"], eng="pool")
                            self.acopy(Sbf[:], Sst[:], ["Sst"], ["Sbf"])

                    def back(c):
                        p2 = c % 2
                        cs = slice(c * 128, (c + 1) * 128)
                        YT = self.bank(7)
                        for cc in range(2):
                            self.tr(YT[:, cc * 128:(cc + 1) * 128], ych[p2][:, cc * 128:(cc + 1) * 128], self.ident_f,
                                    [("ych", p2), "cf"], ["ps7"])
                        for cc in range(2):
                            ch = 2 * g + cc
                            self.stt(yT[:, cc, cs], xc[:, cc, cs], self.pp[:, DP0 + ch:DP0 + ch + 1], YT[:, cc * 128:(cc + 1) * 128],
                                     ALU.mult, ALU.add, [("xc", cc), "pp", "ps7"], [("yT", cc)])

                    front(0)
                    for c in range(16):
                        if c + 1 < 16:
                            front(c + 1)
                        mid(c)
                        if c >= 1:
                            back(c - 1)
                    back(15)
                    if g == 0:
                        self.dbg("yT0", yT[:], sq, [("yT", 0), ("yT", 1)])
                def zgate(g):
                    wb = g % 2
                    chs = [2 * g, 2 * g + 1, 16 + g, 24 + g]
                    for tb in range(4):
                        ts_ = slice(tb * 512, (tb + 1) * 512)
                        for cc in range(2):
                            bk = (tb * 2 + cc) % 4
                            i2 = (tb * 2 + cc) % 2
                            for k in range(8):
                                self.mm(self.bank(bk), wz[wb][:, k, cc * 128:(cc + 1) * 128], self.hnT[:, k, ts_], k == 0, k == 7,
                                        [("wz", wb), "hnT"], ["ps%d" % bk])
                            self.act(sz[i2][:], self.bank(bk), AF.Silu, ["ps%d" % bk], [("sz", i2)])
                            self.tt(yT[:, cc, ts_], yT[:, cc, ts_], sz[i2][:], ALU.mult, [("yT", cc), ("sz", i2)], [("yT", cc)], eng="pool")
                def normpart(g):
                    wb = g % 2
                    chs = [2 * g, 2 * g + 1, 16 + g, 24 + g]
                    for tb in range(4):
                        ts_ = slice(tb * 512, (tb + 1) * 512)
                        bk = 4 + tb
                        for cc in range(2):
                            i2 = cc
                            self.tt(sq_[i2][:], yT[:, cc, ts_], yT[:, cc, ts_], ALU.mult, [("yT", cc)], [("sq", i2)], eng="pool")
                            self.mm(self.bank(bk), self.ones_bf[:], sq_[i2][:], cc == 0, cc == 1, ["ones_bf", ("sq", i2)], ["ps%d" % bk])
                    for tb in range(4):
                        ts_ = slice(tb * 512, (tb + 1) * 512)
                        self.act(rr[tb][:], self.bank(4 + tb), AF.Ln, ["ps%d" % (4 + tb)], [("rr", tb)], bias=self.epsc[:, 0:1], scale=1.0 / 256)
                        self.act(rr[tb][:], rr[tb][:], AF.Exp, [("rr", tb)], [("rr", tb)], scale=-0.5)
                        for cc in range(2):
                            ch = 2 * g + cc
                            self.stt(yT[:, cc, ts_], yT[:, cc, ts_], self.pp[:, SN0 + ch:SN0 + ch + 1], rr[tb][:], ALU.mult, ALU.mult,
                                     [("yT", cc), "pp", ("rr", tb)], [("yT", cc)])
                    if g == 0:
                        self.dbg("yn0", yT[:], sq, [("yT", 0), ("yT", 1)])
                    for cc in range(2):
                        self.dma("sp", self.yn_d[2 * g + cc], yT[:, cc, :], [("yT", cc)], [("yn_d", 2 * g + cc)])
                prologue_a(0)
                for g in range(8):
                    chunkloop(g)
                    zgate(g)
                    if g + 1 < 8:
                        prologue_a(g + 1)
                    if g + 2 < 8:
                        load_w(g + 2)
                    normpart(g)
                S.barrier()

    def phase_ssdproj(self, sq):
        nc = self.nc
        with ExitStack() as ph:
            sb = lambda n, s, d: ph.enter_context(nc.sbuf_tensor("%s_%d" % (n, sq), s, d))
            ynT = sb("ynT", [128, 16, T], BF16)
            wsp = [sb("wsp%d" % i, [128, 16, 512], BF16) for i in range(2)]
            wg = [sb("wgs%d" % i, [128, 8, 512], BF16) for i in range(2)]
            gsb = [sb("gsb%d" % i, [128, 512], BF16) for i in range(2)]
            for j in range(16):
                self.dma("sp", ynT[:, j, :], self.yn_d[j], [], [("ynT", j)])
            w_ssd_r = self.w_ssd.rearrange("(k p) n -> p k n", p=128)
            for dh in range(2):
                self.dma("pool", wsp[dh][:, 0:8, :], w_ssd_r[:, 0:8, dh * 512:(dh + 1) * 512], [], [("wsp", dh)])
                self.dma("pool", wsp[dh][:, 8:16, :], w_ssd_r[:, 8:16, dh * 512:(dh + 1) * 512], [], [("wsp", dh)])
                self.dma("pool", wg[dh][:], self.w_in_r[:, :, OG + dh * 512:OG + (dh + 1) * 512], [], [("wgs", dh)])
            it = 0
            for dh in range(2):
                for j in range(4):
                    dch = dh * 4 + j
                    for tb in range(4):
                        ts_ = slice(tb * 512, (tb + 1) * 512)
                        bP = (it % 4) * 2
                        bG = bP + 1
                        i2 = it % 2
                        it += 1
                        if dh == 0 and j == 0:
                            if tb == 0:
                                for tb2 in range(4):
                                    ts2 = slice(tb2 * 512, (tb2 + 1) * 512)
                                    for k in range(8):
                                        self.mm(self.bank(tb2 * 2 + 1), wg[dh][:, k, 0:128], self.hnT[:, k, ts2], k == 0, k == 7,
                                                [("wgs", dh), "hnT"], ["ps%d" % (tb2 * 2 + 1)])
                                for k in range(16):
                                    for tb2 in range(4):
                                        ts2 = slice(tb2 * 512, (tb2 + 1) * 512)
                                        self.mm(self.bank(tb2 * 2), wsp[dh][:, k, 0:128], ynT[:, k, ts2], k == 0, k == 15,
                                                [("wsp", dh), ("ynT", k)], ["ps%d" % (tb2 * 2)])
                        else:
                            for k in range(16):
                                self.mm(self.bank(bP), wsp[dh][:, k, j * 128:(j + 1) * 128], ynT[:, k, ts_], k == 0, k == 15,
                                        [("wsp", dh), ("ynT", k)], ["ps%d" % bP])
                            for k in range(8):
                                self.mm(self.bank(bG), wg[dh][:, k, j * 128:(j + 1) * 128], self.hnT[:, k, ts_], k == 0, k == 7,
                                        [("wgs", dh), "hnT"], ["ps%d" % bG])
                        self.act(gsb[i2][:], self.bank(bG), AF.Sigmoid, ["ps%d" % bG, "pp"], [("gsb", i2)],
                                 bias=self.pp[:, BG0 + dch:BG0 + dch + 1])
                        self.tt(self.m1[:, dch, ts_], gsb[i2][:], self.bank(bP), ALU.mult, [("gsb", i2), "ps%d" % bP], ["m1"])
            self.dbg("m1s", self.m1[:], sq, ["m1"])


class PhasesB:
    def phase_att(self, sq):
        nc = self.nc
        S = self.S
        with ExitStack() as ph:
            sb = lambda n, s, d: ph.enter_context(nc.sbuf_tensor("%s_%d" % (n, sq), s, d))
            yattT = sb("yattT", [128, 8, T], BF16)
            with ExitStack() as a1:
                sba = lambda n, s, d: a1.enter_context(nc.sbuf_tensor("%s_%d" % (n, sq), s, d))
                wqk = [sba("wqk%d" % i, [128, 8, 512], BF16) for i in range(2)]
                wv = [sba("wv%d" % i, [128, 8, 256], BF16) for i in range(2)]
                qk_tok = sba("qk_tok", [128, 16, 512], BF16)
                v_aug = sba("v_aug", [128, 16, 4, 66], BF16)
                qTm = sba("qTm", [128, 2, 2, T], BF16)
                kT = sba("kT", [128, 2, T], BF16)
                kms = sba("kms", [128, 2, 8], F32)
                km_hi = sba("km_hi", [128, 2, 8], BF16)
                km_lo = sba("km_lo", [128, 2, 8], BF16)
                PT = [sba("PT%d" % i, [128, 512], BF16) for i in range(5)]
                yat = sba("yat", [128, 16, 256], BF16)
                g8all = sba("g8all", [128, 256], F32)
                m8all = sba("m8all", [128, 32, 8], F32)
                selA = sba("selA", [128, 16, 4, 8], F32)
                tmp = [sba("atmp%d" % i, [128, 7 * 65], F32) for i in range(2)]
                red = [sba("ared%d" % i, [128, 65], F32) for i in range(2)]
                tot = [sba("atot%d" % i, [128, 65], F32) for i in range(2)]
                rinv = [sba("arinv%d" % i, [128, 1], F32) for i in range(2)]
                rt = [sba("rt%d" % i, [128, 8, 8], F32) for i in range(4)]
                self.memset(v_aug[:].rearrange("p a b c -> p (a b c)"), 1.0, [], ["v_aug"])

                def load_w(hg):
                    wb = hg % 2
                    self.dma("pool", wqk[wb][:, :, 0:256], self.w_in_r[:, :, OQ + hg * 256:OQ + (hg + 1) * 256], [], [("wqk", wb)])
                    self.dma("pool", wqk[wb][:, :, 256:512], self.w_in_r[:, :, OK_ + hg * 256:OK_ + (hg + 1) * 256], [], [("wqk", wb)])
                    self.dma("pool", wv[wb][:], self.w_in_r[:, :, OV + hg * 256:OV + (hg + 1) * 256], [], [("wv", wb)])

                load_w(0)
                pti = 0
                sri = 0
                aci = 0
                for hg in range(4):
                    wb = hg % 2
                    if hg + 1 < 4:
                        load_w(hg + 1)
                    def emit_tr(t):
                        tl = slice(t * 128, (t + 1) * 128)
                        bk = 4 + t % 4
                        pb = self.bankb(bk)
                        for i in range(4):
                            self.tr(pb[:, i * 128:(i + 1) * 128], qk_tok[:, t, i * 128:(i + 1) * 128], self.ident,
                                    [("qk_tok", t), "cb"], ["ps%d" % bk])
                        qsrc = pb[:, 0:256].rearrange("p (c l) -> p c l", c=2)
                        ksrc = pb[:, 256:512].rearrange("p (c l) -> p c l", c=2)
                        if bk in (4, 5):
                            for par in range(2):
                                self.S.op("act", (lambda o_, i_, c_: lambda e: e.mul(out=o_, in_=i_, mul=self.cf[:, c_:c_ + 1]))(qTm[:, par, :, tl], qsrc, 384 + par),
                                          reads=["ps%d" % bk, "cf"], writes=[("qT", par)])
                            self.acopy(kT[:, :, tl], ksrc, ["ps%d" % bk], ["kT"])
                        else:
                            for par in range(2):
                                self.ts(qTm[:, par, :, tl], qsrc, self.cf[:, 384 + par:385 + par], None, ALU.mult, None, ["ps%d" % bk, "cf"], [("qT", par)])
                            self.vcopy(kT[:, :, tl], ksrc, ["ps%d" % bk], ["kT"])
                    for t in range(NT):
                        tl = slice(t * 128, (t + 1) * 128)
                        bq = (t % 2) * 2
                        bv = bq + 1
                        for k in range(8):
                            self.mm(self.bank(bq), self.hnT[:, k, tl], wqk[wb][:, k, :], k == 0, k == 7, ["hnT", ("wqk", wb)], ["ps%d" % bq])
                        for k in range(8):
                            self.mm(self.bank(bv)[:, 0:256], self.hnT[:, k, tl], wv[wb][:, k, :], k == 0, k == 7, ["hnT", ("wv", wb)], ["ps%d" % bv])
                        Pv = self.bank(bq).rearrange("p (h d) -> p h d", h=8)
                        O = qk_tok[:, t, :].rearrange("p (h d) -> p h d", h=8)
                        cosb = self.cf[:, t * 8:(t + 1) * 8].unsqueeze(1).to_broadcast([128, 8, 8])
                        sinb = self.cf[:, 128 + t * 8:128 + (t + 1) * 8].unsqueeze(1).to_broadcast([128, 8, 8])
                        kq = "ps%d" % bq
                        self.vcopy(O[:, :, 16:64], Pv[:, :, 16:64], [kq], [("qk_tok", t)])
                        self.tt(rt[0][:], Pv[:, :, 0:8], cosb, ALU.mult, [kq, "cf"], ["rt0"])
                        self.tt(rt[1][:], Pv[:, :, 8:16], sinb, ALU.mult, [kq, "cf"], ["rt1"])
                        self.tt(rt[2][:], Pv[:, :, 8:16], cosb, ALU.mult, [kq, "cf"], ["rt2"])
                        self.tt(rt[3][:], Pv[:, :, 0:8], sinb, ALU.mult, [kq, "cf"], ["rt3"])
                        self.tt(O[:, :, 0:8], rt[0][:], rt[1][:], ALU.subtract, ["rt0", "rt1"], [("qk_tok", t)])
                        self.tt(O[:, :, 8:16], rt[2][:], rt[3][:], ALU.add, ["rt2", "rt3"], [("qk_tok", t)])
                        self.acopy(v_aug[:, t, :, 0:64], self.bank(bv)[:, 0:256].rearrange("p (h d) -> p h d", h=4),
                                   ["ps%d" % bv], ["v_aug"])
                        if t >= 1:
                            emit_tr(t - 1)
                    emit_tr(NT - 1)
                    if self.att_lvl == -1:
                        S.barrier()
                        return
                    if self.att_lvl == -2:
                        continue
                    if hg == 0:
                        self.dbg("kT0", kT[:], sq, ["kT"])
                    if "K" not in self.att_skip:
                        S.op("dve", lambda e: e.tensor_reduce(out=kms[:], in_=kT[:].rearrange("p c (n j) -> p c n j", j=256),
                                                              axis=AX.X, op=ALU.add), reads=["kT"], writes=["kms"])
                        self.vcopy(km_hi[:], kms[:], ["kms"], ["km_hi"])
                        self.tt(km_lo[:], kms[:], km_hi[:], ALU.subtract, ["kms", "km_hi"], ["km_lo"])
                    if self.att_lvl >= 1:
                        Gps = self.bank(7)
                        first = True
                        for t in range(8, NT):
                            tl = slice(t * 128, (t + 1) * 128)
                            for h in range(4):
                                c = h // 2
                                o_ = Gps[:, (t - 8) * 32 + h * 8:(t - 8) * 32 + (h + 1) * 8]
                                self.mm(o_, qTm[:, h % 2, c, tl], km_hi[:, c, :], first, False, [("qT", h % 2), "km_hi"], ["ps7"])
                                first = False
                                self.mm(o_, qTm[:, h % 2, c, tl], km_lo[:, c, :], False, (t == NT - 1 and h == 3), [("qT", h % 2), "km_lo"], ["ps7"])
                        self.tt(g8all[:], Gps[:, 0:256], self.cf[:, 386:642], ALU.add, ["ps7", "cf"], ["g8"])
                        for idx in range(32):
                            S.op("dve", (lambda i_: (lambda e: e.max(out=m8all[:, i_, :], in_=g8all[:, i_ * 8:(i_ + 1) * 8])))(idx), reads=["g8"], writes=["m8"])
                        self.tt(selA[:, 8:16, :, :].rearrange("p t h n -> p (t h) n"), g8all[:].rearrange("p (g n) -> p g n", n=8),
                                m8all[:, :, 2:3].to_broadcast([128, 32, 8]), ALU.is_ge, ["g8", "m8"], ["selA"])
                    if hg == 0:
                        self.dbg("selA", selA[:], sq, ["selA"])
                    chunks_l = []
                    t_order = []
                    for i_ in range(NT // 2):
                        t_order += [i_, NT - 1 - i_]
                    for h in (range(4) if self.att_lvl >= 2 else []):
                        for t in t_order:
                            nk = t + 1
                            cl = [(k0_, min(k0_ + 4, nk)) for k0_ in range(0, nk, 4)]
                            for ci, (k0, k1) in enumerate(cl):
                                chunks_l.append((h, t, k0, k1, ci == len(cl) - 1))
                    unit_ai = {}

                    def emit_qk(idx):
                        h, t, k0, k1, last = chunks_l[idx]
                        c = h // 2
                        tl = slice(t * 128, (t + 1) * 128)
                        si = idx % 4
                        pi = idx % 5
                        sreg = self.bank(si)
                        ks = ["ps%d" % si]
                        for kt in range(k0, k1):
                            col = (kt - k0) * 128
                            self.mm(sreg[:, col:col + 128], kT[:, c, kt * 128:(kt + 1) * 128], qTm[:, h % 2, c, tl],
                                    True, kt != t, ["kT", ("qT", h % 2)], ks)
                            if kt == t:
                                self.mm(sreg[:, col:col + 128], self.ident, self.neg4[:, 0:128], False, True, ["cb"], ks)
                        n = (k1 - k0) * 128
                        self.act(PT[pi][:, 0:n], sreg[:, 0:n], AF.Exp, ks, [("PT", pi)], scale=0.125)

                    def emit_pv(idx):
                        h, t, k0, k1, last = chunks_l[idx]
                        blk = t // 2
                        pi = idx % 5
                        if (h, t) not in unit_ai:
                            unit_ai[(h, t)] = len(unit_ai) % 2
                        ai = unit_ai[(h, t)]
                        acc0 = self.bank(4 + 2 * ai)
                        acc1 = self.bank(5 + 2 * ai)
                        ka0 = "ps%d" % (4 + 2 * ai)
                        ka1 = "ps%d" % (5 + 2 * ai)
                        for kt in range(k0, k1):
                            col = (kt - k0) * 128
                            if blk <= 3:
                                o, st_, sp_, kk = acc0[:, 0:65], kt == 0, kt == t, ka0
                            else:
                                n_ = kt // 2
                                if n_ < blk:
                                    o, st_, sp_, kk = acc0[:, n_ * 65:(n_ + 1) * 65], kt % 2 == 0, kt % 2 == 1, ka0
                                else:
                                    o, st_, sp_, kk = acc1[:, 0:65], kt == 2 * blk, kt == t, ka1
                            self.mm(o, PT[pi][:, col:col + 128], v_aug[:, kt, h, 0:65], st_, sp_, [("PT", pi), "v_aug"], [kk])
                        if not last:
                            return
                        yo = yat[:, t, h * 64:(h + 1) * 64]
                        if blk <= 3:
                            self.recip(rinv[ai][:], acc0[:, 64:65], [ka0], [("rinv", ai)])
                            self.ts(yo, acc0[:, 0:64], rinv[ai][:, 0:1], None, ALU.mult, None, [ka0, ("rinv", ai)], ["yat"])
                        else:
                            self.tt(tmp[ai][:, 0:blk * 65].rearrange("p (n d) -> p n d", n=blk),
                                    acc0[:, 0:blk * 65].rearrange("p (n d) -> p n d", n=blk),
                                    selA[:, t, h, 0:blk].unsqueeze(2).to_broadcast([128, blk, 65]), ALU.mult,
                                    [ka0, "selA"], [("atmp", ai)])
                            S.op("dve", (lambda a_, b_: (lambda e: e.tensor_reduce(
                                out=red[a_][:], in_=tmp[a_][:, 0:b_ * 65].rearrange("p (n d) -> p d n", n=b_), axis=AX.X, op=ALU.add)))(ai, blk),
                                reads=[("atmp", ai)], writes=[("ared", ai)])
                            self.tt(tot[ai][:], red[ai][:], acc1[:, 0:65], ALU.add, [("ared", ai), ka1], [("atot", ai)])
                            self.recip(rinv[ai][:], tot[ai][:, 64:65], [("atot", ai)], [("rinv", ai)])
                            self.ts(yo, tot[ai][:, 0:64], rinv[ai][:, 0:1], None, ALU.mult, None, [("atot", ai), ("rinv", ai)], ["yat"])

                    DEP = 3
                    for idx in range(len(chunks_l)):
                        emit_qk(idx)
                        if idx >= DEP:
                            emit_pv(idx - DEP)
                    for idx in range(max(0, len(chunks_l) - DEP), len(chunks_l)):
                        emit_pv(idx)
                    for t in (range(NT) if "Y" not in self.att_skip else []):
                        tl = slice(t * 128, (t + 1) * 128)
                        bk = t % 4
                        pb = self.bankb(bk)
                        for i in range(2):
                            self.tr(pb[:, i * 128:(i + 1) * 128], yat[:, t, i * 128:(i + 1) * 128], self.ident, ["yat", "cb"], ["ps%d" % bk])
                        self.evac(yattT[:, 2 * hg:2 * hg + 2, tl], pb[:, 0:256].rearrange("p (c l) -> p c l", c=2), ["ps%d" % bk], ["yattT"])
                S.barrier()
            self.dbg("yattT", yattT[:], sq, ["yattT"])
            if "M" in self.att_skip:
                return
            with ExitStack() as a2:
                sbm = lambda n, s, d: a2.enter_context(nc.sbuf_tensor("%s_%d" % (n, sq), s, d))
                watt = [sbm("watt%d" % i, [128, 8, 512], BF16) for i in range(2)]
                wg = [sbm("wga%d" % i, [128, 8, 512], BF16) for i in range(2)]
                gsb = [sbm("gsa%d" % i, [128, 512], BF16) for i in range(2)]
                tmpm = [sbm("tmpm%d" % i, [128, 512], BF16) for i in range(2)]
                w_att_r = self.w_att.rearrange("(k p) n -> p k n", p=128)
                for dh in range(2):
                    self.dma("pool", watt[dh][:], w_att_r[:, :, dh * 512:(dh + 1) * 512], [], [("watt", dh)])
                    self.dma("pool", wg[dh][:], self.w_in_r[:, :, OG + 1024 + dh * 512:OG + 1024 + (dh + 1) * 512], [], [("wga", dh)])
                it = 0
                for dh in range(2):
                    for j in range(4):
                        dch = dh * 4 + j
                        for tb in range(4):
                            ts_ = slice(tb * 512, (tb + 1) * 512)
                            bP = (it % 4) * 2
                            bG = bP + 1
                            i2 = it % 2
                            it += 1
                            for k in range(8):
                                self.mm(self.bank(bP), watt[dh][:, k, j * 128:(j + 1) * 128], yattT[:, k, ts_], k == 0, k == 7,
                                        [("watt", dh), "yattT"], ["ps%d" % bP])
                            for k in range(8):
                                self.mm(self.bank(bG), wg[dh][:, k, j * 128:(j + 1) * 128], self.hnT[:, k, ts_], k == 0, k == 7,
                                        [("wga", dh), "hnT"], ["ps%d" % bG])
                            self.act(gsb[i2][:], self.bank(bG), AF.Sigmoid, ["ps%d" % bG, "pp"], [("gsa", i2)],
                                     bias=self.pp[:, BG0 + 8 + dch:BG0 + 8 + dch + 1])
                            self.tt(tmpm[i2][:], gsb[i2][:], self.bank(bP), ALU.mult, [("gsa", i2), "ps%d" % bP], [("tmpm", i2)])
                            self.tt(self.m1[:, dch, ts_], self.m1[:, dch, ts_], tmpm[i2][:], ALU.add, ["m1", ("tmpm", i2)], ["m1"], eng="pool")
                self.dbg("m1", self.m1[:], sq, ["m1"])
                S.barrier()

    def phase_ffn(self, sq):
        nc = self.nc
        S = self.S
        h2T = self.m1
        with ExitStack() as ph:
            sb = lambda n, s, d: ph.enter_context(nc.sbuf_tensor("%s_%d" % (n, sq), s, d))
            xn = sb("xn", [128, 16, D], F32)
            with ExitStack() as s1:
                sb1 = lambda n, s, d: s1.enter_context(nc.sbuf_tensor("%s_%d" % (n, sq), s, d))
                wout = sb1("wout", [128, 8, D], BF16)
                nwb = sb1("n2wb", [128, D], F32)
                xt = [sb1("xt%d" % i, [128, D], F32) for i in range(3)]
                h2 = [sb1("h2_%d" % i, [128, D], BF16) for i in range(2)]
                junk = sb1("junk2", [128, D], BF16)
                ss = sb1("ss2", [128, 16], F32)
                r1 = sb1("r12", [128, 16], F32)
                rstd = sb1("rstd2", [128, 16], F32)
                w_out_r = self.w_out.rearrange("(k p) n -> p k n", p=128)
                for dh in range(2):
                    self.dma("pool", wout[:, :, dh * 512:(dh + 1) * 512], w_out_r[:, :, dh * 512:(dh + 1) * 512], [], [("wout", dh)])
                self.dma("sp", nwb[:], self.nw[1:2, :].partition_broadcast(128), [], ["n2wb"])
                def partA(t):
                    tl = slice(t * 128, (t + 1) * 128)
                    b = t % 2
                    x3 = t % 3
                    self.dma("sp" if t % 2 == 0 else "pool", xt[x3][:], self.x[sq, tl, :], [], [("xt", x3)])
                    reg = self.PSD[b]
                    ks = ["ps%d" % (2 * b), "ps%d" % (2 * b + 1)]
                    for dh in range(2):
                        for k in range(8):
                            self.mm(reg[:, dh * 512:(dh + 1) * 512], self.m1[:, k, tl], wout[:, k, dh * 512:(dh + 1) * 512], k == 0, k == 7,
                                    [("m1", t), ("wout", dh)], [ks[dh]])
                    self.tt(xn[:, t, :], xt[x3][:], reg[:], ALU.add, [("xt", x3)] + ks, [("xn", t)])
                    self.act(junk[:], xn[:, t, :], AF.Square, [("xn", t)], ["junk2", ("ss2", t)], accum=ss[:, t:t + 1])
                    self.act(r1[:, t:t + 1], ss[:, t:t + 1], AF.Sqrt, [("ss2", t)], [("r12", t)], bias=self.epsc[:, 0:1], scale=1.0 / D)
                    self.recip(rstd[:, t:t + 1], r1[:, t:t + 1], [("r12", t)], [("rstd2", t)])
                    self.stt(h2[b][:], xn[:, t, :], rstd[:, t:t + 1], nwb[:], ALU.mult, ALU.mult,
                             [("xn", t), ("rstd2", t), "n2wb"], [("h2", b)])

                def partB(t):
                    tl = slice(t * 128, (t + 1) * 128)
                    b = t % 2
                    pb = self.bankb(4 + b)
                    for j in range(8):
                        self.tr(pb[:, j * 128:(j + 1) * 128], h2[b][:, j * 128:(j + 1) * 128], self.ident, [("h2", b), "cb"], ["ps%d" % (4 + b)])
                    self.evac(h2T[:, :, tl], pb.rearrange("p (k c) -> p k c", k=8), ["ps%d" % (4 + b)], [("m1", t)])

                partA(0)
                for t in range(1, NT):
                    partA(t)
                    partB(t - 1)
                partB(NT - 1)
                self.dbg("xn0", xn[:, 0:8, :], sq, [("xn", i) for i in range(8)])
                S.barrier()
            if self.upto < 5:
                return
            with ExitStack() as s2:
                sb2 = lambda n, s, d: s2.enter_context(nc.sbuf_tensor("%s_%d" % (n, sq), s, d))
                uT = sb2("uT", [128, 22, 1024], BF16)
                wa = [sb2("wa%d" % i, [128, 8, 256], BF16) for i in range(2)]
                wb_ = [sb2("wb%d" % i, [128, 8, 256], BF16) for i in range(2)]
                wd = [sb2("wd%d" % i, [128, 22, 256], BF16) for i in range(2)]
                nfwb = sb2("nfwb", [128, D], F32)
                araw = [sb2("araw%d" % i, [128, 1026], BF16) for i in range(2)]
                sa = [sb2("sa%d" % i, [128, 512], BF16) for i in range(2)]
                fdiag = [sb2("fdiag%d" % i, [128, 3, 128], BF16) for i in range(2)]
                outt = [sb2("outt%d" % i, [128, D], F32) for i in range(2)]
                junk = sb2("junk3", [128, D], BF16)
                ss = sb2("ss3", [128, 16], F32)
                r1 = sb2("r13", [128, 16], F32)
                rstd = sb2("rstd3", [128, 16], F32)
                w_up_r = self.w_up.rearrange("(k p) n -> p k n", p=128)
                w_dn_r = self.w_dn.rearrange("(k p) n -> p k n", p=128)
                self.dma("sp", nfwb[:], self.nw[2:3, :].partition_broadcast(128), [], ["nfwb"])
                wcnt = [0]

                def load_up(jg):
                    i = wcnt[0] % 2
                    wcnt[0] += 1
                    self.dma("pool", wa[i][:], w_up_r[:, :, jg * 256:(jg + 1) * 256], [], [("wa", i)])
                    self.dma("pool", wb_[i][:], w_up_r[:, :, 2816 + jg * 256:2816 + (jg + 1) * 256], [], [("wb", i)])
                    return i

                dcnt = [0]

                def load_dn(dq):
                    i = dcnt[0] % 2
                    dcnt[0] += 1
                    self.dma("pool", wd[i][:], w_dn_r[:, :, dq * 256:(dq + 1) * 256], [], [("wd", i)])
                    return i

                nxt_up = load_up(0)
                for hf in range(2):
                    tok0 = hf * 1024
                    dn_idx = {}
                    for jg in range(11):
                        wi = nxt_up
                        if jg + 1 < 11:
                            nxt_up = load_up(jg + 1)
                        else:
                            dn_idx[0] = load_dn(0)
                            dn_idx[1] = load_dn(1)
                            if hf == 0:
                                nxt_up = load_up(0)
                        for jj in range(2):
                            j = jg * 2 + jj
                            rb = j % 2
                            for k in range(3):
                                col = FW0 + j * 3 + k
                                self.S.op("act", (lambda o_, c_: lambda e: e.mul(out=o_, in_=self.ident, mul=self.pp[:, c_:c_ + 1]))(fdiag[rb][:, k, :], col),
                                          reads=["cb", "pp"], writes=[("fdiag", rb)])
                            if hf == 0:
                                self.memset(araw[rb][:, 0:2], 0.0, [], [("araw", rb)])
                            else:
                                self.vcopy(araw[rb][:, 0:2], self.carry[:, j, :], ["carry"], [("araw", rb)], eng="pool")
                            for tb in range(2):
                                ts_ = slice(tok0 + tb * 512, tok0 + (tb + 1) * 512)
                                for k in range(8):
                                    self.mm(self.bank(tb), wa[wi][:, k, jj * 128:(jj + 1) * 128], h2T[:, k, ts_], k == 0, k == 7,
                                            [("wa", wi), "h2T"], ["ps%d" % tb])
                                for k in range(8):
                                    self.mm(self.bank(2 + tb), wb_[wi][:, k, jj * 128:(jj + 1) * 128], h2T[:, k, ts_], k == 0, k == 7,
                                            [("wb", wi), "h2T"], ["ps%d" % (2 + tb)])
                                self.evac(araw[rb][:, 2 + tb * 512:2 + (tb + 1) * 512], self.bank(tb), ["ps%d" % tb], [("araw", rb)])
                            if hf == 0:
                                self.vcopy(self.carry[:, j, :], araw[rb][:, 1024:1026], [("araw", rb)], ["carry"], eng="pool")
                            for tb in range(2):
                                us_ = slice(tb * 512, (tb + 1) * 512)
                                for k in range(3):
                                    self.mm(self.bank(4 + tb), fdiag[rb][:, k, :], araw[rb][:, tb * 512 + k:tb * 512 + k + 512], k == 0, k == 2,
                                            [("fdiag", rb), ("araw", rb)], ["ps%d" % (4 + tb)])
                                self.act(sa[tb][:], self.bank(4 + tb), AF.Silu, ["ps%d" % (4 + tb), "pp"], [("sa", tb)],
                                         bias=self.pp[:, FB0 + j:FB0 + j + 1])
                                self.tt(uT[:, j, us_], sa[tb][:], self.bank(2 + tb), ALU.mult, [("sa", tb), "ps%d" % (2 + tb)], ["uT"])
                    for dq in range(4):
                        if dq >= 2:
                            dn_idx[dq] = load_dn(dq)
                        di = dn_idx[dq]
                        for tl_ in range(8):
                            t = hf * 8 + tl_
                            bk = 6 + tl_ % 2
                            for j in range(22):
                                self.mm(self.bank(bk)[:, 0:256], uT[:, j, tl_ * 128:(tl_ + 1) * 128], wd[di][:, j, :], j == 0, j == 21,
                                        ["uT", ("wd", di)], ["ps%d" % bk])
                            self.tt(xn[:, t, dq * 256:(dq + 1) * 256], xn[:, t, dq * 256:(dq + 1) * 256], self.bank(bk)[:, 0:256], ALU.add,
                                    [("xn", t), "ps%d" % bk], [("xn", t)])
                    for tl_ in range(8):
                        t = hf * 8 + tl_
                        b = t % 2
                        self.act(junk[:], xn[:, t, :], AF.Square, [("xn", t)], ["junk3", ("ss3", t)], accum=ss[:, t:t + 1])
                        self.act(r1[:, t:t + 1], ss[:, t:t + 1], AF.Sqrt, [("ss3", t)], [("r13", t)], bias=self.epsc[:, 0:1], scale=1.0 / D)
                        self.recip(rstd[:, t:t + 1], r1[:, t:t + 1], [("r13", t)], [("rstd3", t)])
                        self.stt(outt[b][:], xn[:, t, :], rstd[:, t:t + 1], nfwb[:], ALU.mult, ALU.mult,
                                 [("xn", t), ("rstd3", t), "nfwb"], [("outt", b)])
                        self.dma("sp", self.out[sq, t * 128:(t + 1) * 128, :], outt[b][:], [("outt", b)], [("out", t)])
                S.barrier()


class KB(PhasesA, PhasesB):
    def __init__(self, nc, nseq, dbg_names):
        self.nc = nc
        self.nseq = nseq
        self.S = Sched(nc)
        self.dbg_names = dbg_names
        self.dbg_t = {}
        self.evac_rr = 0

    def mm(self, out, lhsT, rhs, start, stop, r, w):
        self.S.op("pe", lambda e: e.matmul(out, lhsT=lhsT, rhs=rhs, start=start, stop=stop, skip_group_check=True), reads=r, writes=w)

    def tr(self, out, in_, ident, r, w):
        self.S.op("pe", lambda e: e.transpose(out=out, in_=in_, identity=ident), reads=r, writes=w)

    def act(self, out, in_, func, r, w, bias=None, scale=1.0, accum=None):
        kw = {}
        if bias is not None:
            kw["bias"] = bias
        if accum is not None:
            kw["accum_out"] = accum
        self.S.op("act", lambda e: e.activation(out=out, in_=in_, func=func, scale=scale, **kw), reads=r, writes=w)

    def acopy(self, out, in_, r, w):
        self.S.op("act", lambda e: e.copy(out=out, in_=in_), reads=r, writes=w)

    def vcopy(self, out, in_, r, w, eng="dve"):
        self.S.op(eng, lambda e: e.tensor_copy(out=out, in_=in_), reads=r, writes=w)

    def evac(self, out, in_, r, w):
        self.evac_rr += 1
        if self.evac_rr % 2:
            self.acopy(out, in_, r, w)
        else:
            self.vcopy(out, in_, r, w)

    def tt(self, out, in0, in1, op, r, w, eng="dve"):
        self.S.op(eng, lambda e: e.tensor_tensor(out=out, in0=in0, in1=in1, op=op), reads=r, writes=w)

    def ts(self, out, in0, s1, s2, op0, op1, r, w, eng="dve"):
        if s2 is None:
            self.S.op(eng, lambda e: e.tensor_scalar(out=out, in0=in0, scalar1=s1, scalar2=None, op0=op0), reads=r, writes=w)
        else:
            self.S.op(eng, lambda e: e.tensor_scalar(out=out, in0=in0, scalar1=s1, scalar2=s2, op0=op0, op1=op1), reads=r, writes=w)

    def stt(self, out, in0, scalar, in1, op0, op1, r, w):
        self.S.op("dve", lambda e: e.scalar_tensor_tensor(out=out, in0=in0, scalar=scalar, in1=in1, op0=op0, op1=op1), reads=r, writes=w)

    def recip(self, out, in_, r, w):
        self.S.op("dve", lambda e: e.reciprocal(out=out, in_=in_), reads=r, writes=w)

    def memset(self, ap, val, r, w, eng="pool"):
        self.S.op(eng, lambda e: e.memset(ap, val), reads=r, writes=w)

    def dma(self, q, out, in_, r, w):
        self.S.dma(q, lambda e: e.dma_start(out=out, in_=in_), reads=r, writes=w)

    def bank(self, i):
        return self.PSD[i // 2][:, (i % 2) * 512:(i % 2) * 512 + 512]

    def bankb(self, i):
        return self.PSD[i // 2][:].bitcast(BF16)[:, (i % 2) * 1024:(i % 2) * 1024 + 1024]

    def dbg(self, name, ap, sq, r):
        if sq != 0 or name not in self.dbg_names:
            return
        shape = list(ap.shape)
        dt_ = ap.dtype
        t = self.nc.dram_tensor("dbg_" + name, shape, dt_, kind="ExternalOutput").ap()
        self.dbg_t[name] = t
        self.dma("sp", t, ap, r, ["dbgout_" + name])

    def build(self):
        nc = self.nc
        S = self.S
        nseq = self.nseq
        self.x = nc.dram_tensor("x", [nseq, T, D], F32, kind="ExternalInput").ap()
        self.w_in = nc.dram_tensor("w_in", [D, 11296], F32, kind="ExternalInput").ap()
        self.w_ssd = nc.dram_tensor("w_ssd", [2048, D], F32, kind="ExternalInput").ap()
        self.w_att = nc.dram_tensor("w_att", [D, D], F32, kind="ExternalInput").ap()
        self.w_out = nc.dram_tensor("w_out", [D, D], F32, kind="ExternalInput").ap()
        self.w_up = nc.dram_tensor("w_up", [D, 5632], F32, kind="ExternalInput").ap()
        self.w_dn = nc.dram_tensor("w_dn", [2816, D], F32, kind="ExternalInput").ap()
        self.nw = nc.dram_tensor("nw", [3, D], F32, kind="ExternalInput").ap()
        self.pp_d = nc.dram_tensor("pp", [128, NPP], F32, kind="ExternalInput").ap()
        self.cf_d = nc.dram_tensor("cf", [128, NCF], F32, kind="ExternalInput").ap()
        self.cb_d = nc.dram_tensor("cb", [128, NCB], F32, kind="ExternalInput").ap()
        self.selg_d = nc.dram_tensor("selg", [128, 8192], F32, kind="ExternalInput").ap()
        self.out = nc.dram_tensor("out", [nseq, T, D], F32, kind="ExternalOutput").ap()
        self.yn_d = nc.dram_tensor("yn_scr", [16, 128, T], BF16, kind="Internal").ap()
        self.w_in_r = self.w_in.rearrange("(k p) n -> p k n", p=128)

        with ExitStack() as st:
            S.open(st)
            self.PSD = [st.enter_context(nc.psum_tensor("psd%d" % i, [128, 1024], F32)) for i in range(4)]
            sb = lambda n, s, d: st.enter_context(nc.sbuf_tensor(n, s, d))
            self.pp = sb("pp_sb", [128, NPP], F32)
            self.cf = sb("cf_sb", [128, NCF], F32)
            self.cbp = sb("cb_sb", [128, NCB], BF16)
            self.ones_bf = sb("ones_bf", [128, 128], BF16)
            self.ones_f = sb("ones_f", [128, 128], F32)
            self.dma("sp", self.pp[:], self.pp_d, [], ["pp"])
            self.dma("sp", self.cf[:], self.cf_d, [], ["cf"])
            self.dma("pool", self.cbp[:], self.cb_d, [], ["cb"])
            self.memset(self.ones_bf[:], 1.0, [], ["ones_bf"])
            self.memset(self.ones_f[:], 1.0, [], ["ones_f"])
            self.epsc = sb("epsc", [128, 1], F32)
            self.onec = sb("onec", [128, 1], F32)
            self.carry = sb("carry", [128, 22, 2], BF16)
            self.memset(self.epsc[:], EPS, [], ["epsc"])
            self.memset(self.onec[:], 1.0, [], ["onec"])
            self.ident = self.cbp[:, 0:128]
            self.neg4 = self.cbp[:, 128:640]
            self.ident_f = self.cf[:, 256:384]
            self.CONST = ["pp", "cf", "cb", "ones_bf", "ones_f"]

            for sq in range(nseq):
                with ExitStack() as sst:
                    self.m1 = sst.enter_context(nc.sbuf_tensor("m1_%d" % sq, [128, 8, T], BF16))
                    with ExitStack() as hst:
                        self.hnT = hst.enter_context(nc.sbuf_tensor("hnT%d" % sq, [128, 8, T], BF16))
                        self.phase_norm1(sq)
                        S.barrier()
                        if self.upto >= 2 and not SKIP_SSD:
                            self.phase_dt_ssd(sq)
                            S.barrier()
                            self.phase_ssdproj(sq)
                            S.barrier()
                        if self.upto >= 3:
                            self.phase_att(sq)
                            S.barrier()
                    if self.upto >= 4:
                        self.phase_ffn(sq)
                        S.barrier()
                S.barrier()
            S.emit()
        return nc


ATT_LVL = 2
ATT_SKIP = ""
SKIP_SSD = False
def _consts():
    cf = np.zeros((128, NCF), np.float32)
    inv_freq = (np.float32(500000.0) ** (-np.arange(0, 16, 2, dtype=np.float32) / np.float32(16))).astype(np.float32)
    pos = np.arange(T, dtype=np.float32)
    ang = (pos[:, None] * inv_freq[None, :]).astype(np.float32)
    cos = np.cos(ang).astype(np.float32)
    sin = np.sin(ang).astype(np.float32)
    cf[:, 0:128] = cos.reshape(16, 128, 8).transpose(1, 0, 2).reshape(128, 128)
    cf[:, 128:256] = sin.reshape(16, 128, 8).transpose(1, 0, 2).reshape(128, 128)
    cf[:, 256:384] = np.eye(128, dtype=np.float32)
    cf[0:64, 384] = 1.0
    cf[64:128, 385] = 1.0
    gm = np.zeros((8, 4, 8), np.float32)
    for t in range(8, 16):
        gm[t - 8, :, (t // 2):] = -1.0e30
    cf[:, 386:642] = gm.reshape(1, 256)
    cb = np.zeros((128, NCB), np.float32)
    cb[:, 0:128] = np.eye(128, dtype=np.float32)
    s = np.arange(128)[:, None]
    l = np.arange(128)[None, :]
    neg = np.where(s > l, NEGV, 0.0).astype(np.float32)
    cb[:, 128:640] = np.tile(neg, (1, 4))
    selg = np.zeros((128, 8192), np.float32)
    for j in range(32):
        selg[j, j * 128:(j + 1) * 128] = 1.0
        selg[32 + j, j * 128:(j + 1) * 128] = 1.0
    for g in range(8):
        for h in range(4):
            selg[4 * g + h, 4096 + g * 512 + h * 128:4096 + g * 512 + (h + 1) * 128] = 1.0
            selg[32 + 4 * g + h, 4096 + g * 512 + h * 128:4096 + g * 512 + (h + 1) * 128] = 1.0
    return cf, cb, selg


def _layout(inputs):
    f = lambda a: np.ascontiguousarray(np.asarray(a, dtype=np.float32))
    pp = np.zeros((128, NPP), np.float32)
    cw = f(inputs["ssd_conv_w"])[0]
    pp[:, CW0:CW0 + 128] = cw.reshape(4, 32, 128).transpose(2, 1, 0).reshape(128, 128)
    pp[:, CB0:CB0 + 32] = f(inputs["ssd_conv_b"])[0].reshape(32, 128).T
    fw = f(inputs["ffn_conv_w"])[0]
    pp[:, FW0:FW0 + 66] = fw.reshape(3, 22, 128).transpose(2, 1, 0).reshape(128, 66)
    pp[:, FB0:FB0 + 22] = f(inputs["ffn_conv_b"])[0].reshape(22, 128).T
    pp[:, BG0:BG0 + 16] = f(inputs["b_gate"])[0].reshape(16, 128).T
    pp[:, SN0:SN0 + 16] = f(inputs["ssd_norm_w"])[0].reshape(16, 128).T
    pp[:, DP0:DP0 + 16] = np.repeat(f(inputs["ssd_d"])[0], 64).reshape(16, 128).T
    pp[0:32, DTB] = f(inputs["ssd_dt_bias"])[0]
    pp[0:32, ALG] = f(inputs["ssd_a_log"])[0]
    pp[32:64, DTB] = f(inputs["ssd_dt_bias"])[0]
    pp[32:64, ALG] = f(inputs["ssd_a_log"])[0]
    nw = np.stack([f(inputs["norm1_w"])[0], f(inputs["norm2_w"])[0], f(inputs["norm_f_w"])])
    cf, cb, selg = _consts()
    base = {
        "w_in": f(inputs["w_in"])[0], "w_ssd": f(inputs["w_ssd_proj"])[0], "w_att": f(inputs["w_att_proj"])[0],
        "w_out": f(inputs["w_out"])[0], "w_up": f(inputs["w_ffn_up"])[0], "w_dn": f(inputs["w_ffn_down"])[0],
        "nw": nw, "pp": pp, "cf": cf, "cb": cb, "selg": selg,
    }
    return base


def build_program(nseq=2, dbg_names=(), upto=5):
    nc = bass.Bass("TRN2", target_bir_lowering=False)
    kb = KB(nc, nseq, dbg_names)
    kb.upto = upto
    kb.att_lvl = ATT_LVL
    kb.att_skip = ATT_SKIP
    kb.build()
    return nc, kb


def kernel(**inputs):
    n = 8
    x = np.ascontiguousarray(np.asarray(inputs["x"], dtype=np.float32))
    base = _layout(inputs)
    nseq = x.shape[0] // n
    nc, kb = build_program(nseq=nseq)
    in_maps = []
    for i in range(n):
        m = dict(base)
        m["x"] = np.ascontiguousarray(x[i * nseq:(i + 1) * nseq])
        in_maps.append(m)
    res = run_bass_kernel_spmd(nc, in_maps, core_ids=list(range(n)))
    return np.concatenate([np.asarray(r["out"], dtype=np.float32) for r in res.results], axis=0)
```

```python
import numpy as np
from contextlib import ExitStack
import concourse.bass as bass
import concourse.mybir as mybir
from concourse.bass_utils import run_bass_kernel_spmd

F32 = mybir.dt.float32
BF16 = mybir.dt.bfloat16
AF = mybir.ActivationFunctionType
ALU = mybir.AluOpType
AX = mybir.AxisListType

ENGS = ("pe", "act", "dve", "pool", "sp")
T = 2048
D = 1024
NT = 16
EPS = 1e-6
OZ, OX, OB, OC, ODT, OQ, OK_, OV, OG = 0, 2048, 4096, 5120, 6144, 6176, 7200, 8224, 9248
CW0, CB0, FW0, FB0, BG0, SN0, DP0, DTB, ALG, NPP = 0, 128, 160, 226, 248, 264, 280, 296, 297, 298
NCF = 642
NCB = 640
NEGV = -30000.0
DEBUG = {}


class Sched:
    def __init__(self, nc, n_dma_sems=32):
        self.nc = nc
        self.ops = {e: [] for e in ENGS}
        self.cnt = {e: 0 for e in ENGS}
        self.sem = {}
        self.last_w = {}
        self.readers = {}
        self.seen = {e: {} for e in ENGS}
        self.n_dma_sems = n_dma_sems
        self.dma_cnt = {}
        self.dma_rr = 0

    def open(self, stack):
        for e in ENGS:
            self.sem[e] = stack.enter_context(self.nc.semaphore("s_" + e))
        for i in range(self.n_dma_sems):
            nm = "dma%d" % i
            self.sem[nm] = stack.enter_context(self.nc.semaphore("s_" + nm))
            self.dma_cnt[nm] = 0

    def _deps(self, eng, reads, writes):
        deps = {}

        def add(tok, same_ok):
            if tok is None:
                return
            s, v = tok
            if s == eng and not same_ok:
                return
            if deps.get(s, 0) < v:
                deps[s] = v

        for k in reads:
            add(self.last_w.get(k), True)
        for k in writes:
            add(self.last_w.get(k), True)
            for t in self.readers.get(k, ()):
                add(t, False)
        out = []
        seen = self.seen[eng]
        for s, v in deps.items():
            if seen.get(s, 0) < v:
                seen[s] = v
                out.append((s, v))
        return out

    def _commit(self, tok, reads, writes):
        for k in writes:
            self.last_w[k] = tok
            self.readers[k] = []
        for k in reads:
            if k not in writes:
                self.readers.setdefault(k, []).append(tok)

    def op(self, eng, fn, reads=(), writes=()):
        waits = self._deps(eng, reads, writes)
        self.cnt[eng] += 1
        tok = (eng, self.cnt[eng])
        if eng == "pe":
            self.seen[eng][eng] = self.cnt[eng]
        self.ops[eng].append((waits, fn, (eng, 1)))
        self._commit(tok, reads, writes)
        return tok

    def dma(self, queue, fn, reads=(), writes=()):
        waits = self._deps(queue, reads, writes)
        name = "dma%d" % (self.dma_rr % self.n_dma_sems)
        self.dma_rr += 1
        self.dma_cnt[name] += 16
        tok = (name, self.dma_cnt[name])
        prev = self.dma_cnt[name] - 16
        if prev > 0 and self.seen[queue].get(name, 0) < prev:
            self.seen[queue][name] = prev
            waits = waits + [(name, prev)]
        self.ops[queue].append((waits, fn, (name, 16)))
        self._commit(tok, reads, writes)
        return tok

    def barrier(self):
        for e in ENGS:
            waits = []
            for s in ENGS:
                v = self.cnt[s]
                if v > 0 and self.seen[e].get(s, 0) < v:
                    self.seen[e][s] = v
                    waits.append((s, v))
            for nm, v in self.dma_cnt.items():
                if v > 0 and self.seen[e].get(nm, 0) < v:
                    self.seen[e][nm] = v
                    waits.append((nm, v))
            if waits:
                self.ops[e].append((waits, None, None))
        self.last_w = {}
        self.readers = {}

    def emit(self):
        nc = self.nc
        sem = self.sem

        def replay(e, name):
            for waits, fn, inc in self.ops[name]:
                for s, v in waits:
                    e.wait_ge(sem[s], v)
                if fn is None:
                    continue
                ins = fn(e)
                ins.then_inc(sem[inc[0]], inc[1])

        with nc.Block() as block:
            @block.tensor
            def _(e):
                replay(e, "pe")

            @block.scalar
            def _(e):
                replay(e, "act")

            @block.vector
            def _(e):
                replay(e, "dve")

            @block.gpsimd
            def _(e):
                replay(e, "pool")

            @block.sync
            def _(e):
                replay(e, "sp")


class PhasesA:
    def phase_norm1(self, sq):
        nc = self.nc
        with ExitStack() as ph:
            sb = lambda n, s, d: ph.enter_context(nc.sbuf_tensor("%s_%d" % (n, sq), s, d))
            nwb = sb("n1wb", [128, D], F32)
            xt = [sb("n1x%d" % i, [128, D], F32) for i in range(3)]
            xs = [sb("n1s%d" % i, [128, D], BF16) for i in range(3)]
            junk = sb("n1j", [128, D], BF16)
            ss = sb("n1ss", [128, 16], F32)
            r1 = sb("n1r1", [128, 16], F32)
            rstd = sb("n1rs", [128, 16], F32)
            self.dma("sp", nwb[:], self.nw[0:1, :].partition_broadcast(128), [], ["n1wb"])
            for t in range(NT):
                b = t % 3
                self.dma("sp", xt[b][:], self.x[sq, t * 128:(t + 1) * 128, :], [], [("n1x", b)])
                self.act(junk[:], xt[b][:], AF.Square, [("n1x", b)], ["n1j", ("n1ss", t)], accum=ss[:, t:t + 1])
                self.act(r1[:, t:t + 1], ss[:, t:t + 1], AF.Sqrt, [("n1ss", t)], [("n1r1", t)], bias=self.epsc[:, 0:1], scale=1.0 / D)
                self.recip(rstd[:, t:t + 1], r1[:, t:t + 1], [("n1r1", t)], [("n1rs", t)])
                self.stt(xs[b][:], xt[b][:], rstd[:, t:t + 1], nwb[:], ALU.mult, ALU.mult,
                         [("n1x", b), ("n1rs", t), "n1wb"], [("n1s", b)])
                pbk = t % 4
                pb = self.bankb(pbk)
                for j in range(8):
                    self.tr(pb[:, j * 128:(j + 1) * 128], xs[b][:, j * 128:(j + 1) * 128], self.ident,
                            [("n1s", b), "cb"], ["ps%d" % pbk])
                self.evac(self.hnT[:, :, t * 128:(t + 1) * 128], pb.rearrange("p (k c) -> p k c", k=8),
                          ["ps%d" % pbk], ["hnT"])
            self.dbg("hnT", self.hnT[:], sq, ["hnT"])

    def phase_dt_ssd(self, sq):
        nc = self.nc
        S = self.S
        with ExitStack() as ph:
            sb = lambda n, s, d: ph.enter_context(nc.sbuf_tensor("%s_%d" % (n, sq), s, d))
            selg = sb("selg", [128, 8192], BF16)
            self.dma("pool", selg[:], self.selg_d, [], ["selg"])
            cs_cat = sb("cs_cat", [128, T], BF16)
            f_cat = sb("f_cat", [128, T], BF16)
            self.memset(cs_cat[:], 0.0, [], ["cscat"])
            self.memset(f_cat[:], 0.0, [], ["fcat"])
            ecs_tok = sb("ecs_tok", [128, 16, 32], F32)
            w_tok = sb("w_tok", [128, 16, 32], F32)
            dec_rep = sb("dec_rep", [128, 16, 32], F32)
            with ExitStack() as t1s:
                sb1 = lambda n, s, d: t1s.enter_context(nc.sbuf_tensor("%s_%d" % (n, sq), s, d))
                A = [sb1("dtA%d" % i, [64, T], F32) for i in range(4)]
                mask = sb1("dtmask", [64, T], F32)
                hi64 = sb1("dthi64", [64, T], BF16)
                dec = sb1("dtdec", [32, 16], F32)
                decx = sb1("dtdecx", [32, 16, 32], F32)
                negA = sb1("dtnegA", [64, 1], F32)
                wdt = sb1("wdt", [128, 8, 64], BF16)
                self.dma("pool", wdt[:, :, 0:32], self.w_in_r[:, :, ODT:ODT + 32], [], ["wdt"])
                self.dma("pool", wdt[:, :, 32:64], self.w_in_r[:, :, ODT:ODT + 32], [], ["wdt"])
                for tb in range(4):
                    for k in range(8):
                        self.mm(self.bank(tb)[0:64, :], wdt[:, k, :], self.hnT[:, k, tb * 512:(tb + 1) * 512],
                                k == 0, k == 7, ["wdt", "hnT"], ["ps%d" % tb])
                    self.act(A[0][:, tb * 512:(tb + 1) * 512], self.bank(tb)[0:64, :], AF.Exp, ["ps%d" % tb, "pp"], ["A0"],
                             bias=self.pp[0:64, DTB:DTB + 1])
                self.act(A[1][:], A[0][:], AF.Ln, ["A0"], ["A1"], bias=self.onec[0:64, 0:1])
                self.act(A[2][:], A[1][:], AF.Ln, ["A1"], ["A2"])
                self.act(negA[:], self.pp[0:64, ALG:ALG + 1], AF.Exp, ["pp"], ["negA"])
                self.ts(A[0][:], A[1][:], negA[:, 0:1], -1.0, ALU.mult, ALU.mult, ["A1", "negA", "A0"], ["A0"])
                self.memset(mask[:], 1.0, [], ["mask"])
                self.memset(mask[:].rearrange("p (c l) -> p c l", l=128)[:, :, 0:1], 0.0, ["mask"], ["mask"])
                S.op("dve", lambda e: e.tensor_tensor_scan(out=A[3][:], data0=mask[:], data1=A[0][:], initial=0.0,
                                                           op0=ALU.mult, op1=ALU.add), reads=["mask", "A0"], writes=["A3"])
                self.tt(A[2][:], A[2][:], A[3][:], ALU.subtract, ["A2", "A3"], ["A2"])
                self.act(A[1][:], A[3][:], AF.Exp, ["A3", "A1"], ["A1"])
                A3v = A[3][:].rearrange("p (c l) -> p c l", l=128)
                self.tt(A[0][:].rearrange("p (c l) -> p c l", l=128), A[2][:].rearrange("p (c l) -> p c l", l=128),
                        A3v[:, :, 127:128].to_broadcast([64, 16, 128]), ALU.add, ["A2", "A3", "A0"], ["A0"])
                self.act(A[0][:], A[0][:], AF.Exp, ["A0"], ["A0"])
                self.act(dec[:], A3v[0:32, :, 127], AF.Exp, ["A3"], ["dec"])
                for (src, dst, nm) in ((A[3], cs_cat, "cscat"), (A[2], f_cat, "fcat")):
                    self.vcopy(hi64[:], src[:], ["A3", "A2", "hi64"], ["hi64"])
                    self.vcopy(dst[0:32, :], hi64[0:32, :], ["hi64"], [nm])
                    self.tt(dst[32:64, :], src[32:64, :], hi64[32:64, :], ALU.subtract, ["A3", "A2", "hi64"], [nm])
                id32 = self.ident_f[0:32, 0:32]
                for c in range(16):
                    self.tr(self.bank(4)[:, c * 32:(c + 1) * 32], A[1][0:32, c * 128:(c + 1) * 128], id32, ["A1", "cf"], ["ps4"])
                    self.tr(self.bank(5)[:, c * 32:(c + 1) * 32], A[0][0:32, c * 128:(c + 1) * 128], id32, ["A0", "cf"], ["ps5"])
                self.vcopy(ecs_tok[:].rearrange("p c h -> p (c h)"), self.bank(4), ["ps4"], ["ecs_tok"])
                self.vcopy(w_tok[:].rearrange("p c h -> p (c h)"), self.bank(5), ["ps5"], ["w_tok"])
                self.tt(decx[:], dec[:].unsqueeze(2).to_broadcast([32, 16, 32]),
                        id32.unsqueeze(1).to_broadcast([32, 16, 32]), ALU.mult, ["dec", "cf"], ["decx"])
                self.mm(self.bank(6), self.ones_f[0:32, :], decx[:].rearrange("p c h -> p (c h)"), True, True,
                        ["ones_f", "decx"], ["ps6"])
                self.vcopy(dec_rep[:].rearrange("p c h -> p (c h)"), self.bank(6), ["ps6"], ["dec_rep"])
                self.dbg("ecs_tok", ecs_tok[:], sq, ["ecs_tok"])
                self.dbg("w_tok", w_tok[:], sq, ["w_tok"])
                self.dbg("dec_rep", dec_rep[:], sq, ["dec_rep"])
                S.barrier()
            with ExitStack() as gs:
                sbg = lambda n, s, d: gs.enter_context(nc.sbuf_tensor("%s_%d" % (n, sq), s, d))
                wx = [sbg("wx%d" % i, [128, 8, 512], BF16) for i in range(2)]
                wz = [sbg("wz%d" % i, [128, 8, 256], BF16) for i in range(2)]
                raw = [sbg("raw%d" % i, [128, 3 + T], BF16) for i in range(2)]
                xc = sbg("xc", [128, 4, T], BF16)
                xbt = sbg("xbt", [128, 16, 384], BF16)
                yT = sbg("yT", [128, 2, T], BF16)
                diag = sbg("diag", [128, 16, 128], BF16)
                L2 = [sbg("L2_%d" % i, [128, 512], F32) for i in range(2)]
                M2 = [sbg("M2_%d" % i, [128, 512], BF16) for i in range(2)]
                Xw_all = sbg("Xw_all", [128, 16, 256], BF16)
                t1 = sbg("t1", [128, 256], F32)
                ych = [sbg("ych%d" % i, [128, 256], F32) for i in range(2)]
                Sst = sbg("Sst", [128, 256], F32)
                Sd = sbg("Sd", [128, 256], F32)
                STs = sbg("STs", [128, 256], F32)
                Sbf = sbg("Sbf", [128, 256], BF16)
                sz = [sbg("sz%d" % i, [128, 512], BF16) for i in range(2)]
                sq_ = [sbg("sqq%d" % i, [128, 512], BF16) for i in range(2)]
                rr = [sbg("rr%d" % i, [128, 512], F32) for i in range(4)]
                for i in range(2):
                    self.memset(raw[i][:, 0:3], 0.0, [], [("raw", i)])

                def load_w(g):
                    wb = g % 2
                    self.dma("pool", wx[wb][:, :, 0:256], self.w_in_r[:, :, OX + g * 256:OX + (g + 1) * 256], [], [("wx", wb, 0)])
                    self.dma("pool", wx[wb][:, :, 256:384], self.w_in_r[:, :, OB + g * 128:OB + (g + 1) * 128], [], [("wx", wb, 1)])
                    self.dma("pool", wx[wb][:, :, 384:512], self.w_in_r[:, :, OC + g * 128:OC + (g + 1) * 128], [], [("wx", wb, 2)])
                    self.dma("pool", wz[wb][:], self.w_in_r[:, :, OZ + g * 256:OZ + (g + 1) * 256], [], [("wz", wb)])

                load_w(0)
                load_w(1)
                def prologue_a(g):
                    wb = g % 2
                    chs = [2 * g, 2 * g + 1, 16 + g, 24 + g]
                    for cc in range(4):
                        for k in range(4):
                            col = CW0 + chs[cc] * 4 + k
                            self.S.op("act", (lambda o_, c_: lambda e: e.mul(out=o_, in_=self.ident, mul=self.pp[:, c_:c_ + 1]))(diag[:, cc * 4 + k, :], col),
                                      reads=["cb", "pp"], writes=[("diag", cc)])
                    for cc in range(4):
                        rb = cc % 2
                        ch = chs[cc]
                        for tb in range(4):
                            bk = tb
                            for k in range(8):
                                self.mm(self.bank(bk), wx[wb][:, k, cc * 128:(cc + 1) * 128],
                                        self.hnT[:, k, tb * 512:(tb + 1) * 512], k == 0, k == 7,
                                        [("wx", wb, max(0, cc - 1)), "hnT"], ["ps%d" % bk])
                            self.evac(raw[rb][:, 3 + tb * 512:3 + (tb + 1) * 512], self.bank(bk), ["ps%d" % bk], [("raw", rb)])
                        for tb in range(4):
                            bk = 4 + tb
                            for k in range(4):
                                self.mm(self.bank(bk), diag[:, cc * 4 + k, :], raw[rb][:, tb * 512 + k:tb * 512 + k + 512],
                                        k == 0, k == 3, [("diag", cc), ("raw", rb)], ["ps%d" % bk])
                            self.act(xc[:, cc, tb * 512:(tb + 1) * 512], self.bank(bk), AF.Silu, ["ps%d" % bk, "pp"],
                                     [("xc", cc)], bias=self.pp[:, CB0 + ch:CB0 + ch + 1])
                    if g == 0:
                        self.dbg("xc0", xc[:], sq, [("xc", i) for i in range(4)])
                def prologue_b(g):
                    wb = g % 2
                    chs = [2 * g, 2 * g + 1, 16 + g, 24 + g]
                    for c in range(16):
                        bk = c % 4
                        pb = self.bankb(bk)
                        for cc in range(3):
                            self.tr(pb[:, cc * 128:(cc + 1) * 128], xc[:, cc, c * 128:(c + 1) * 128], self.ident,
                                    [("xc", cc), "cb"], ["ps%d" % bk])
                        self.evac(xbt[:, c, :], pb[:, 0:384], ["ps%d" % bk], [("xbt", c // 4)])
                        if c % 4 == 3:
                            q4 = c // 4
                            self.tt(Xw_all[:, 4 * q4:4 * q4 + 4, :].rearrange("p c (h d) -> p c h d", h=4),
                                    xbt[:, 4 * q4:4 * q4 + 4, 0:256].rearrange("p c (h d) -> p c h d", h=4),
                                    w_tok[:, 4 * q4:4 * q4 + 4, 4 * g:4 * g + 4].unsqueeze(3).to_broadcast([128, 4, 4, 64]), ALU.mult,
                                    [("xbt", q4), "w_tok"], [("Xw", q4)], eng="pool")
                def chunkloop(g):
                    wb = g % 2
                    chs = [2 * g, 2 * g + 1, 16 + g, 24 + g]
                    def front(c):
                        p2 = c % 2
                        Db = self.bank(p2)
                        CBb = self.bank(2 + p2)
                        cs = slice(c * 128, (c + 1) * 128)
                        kD = "ps%d" % p2
                        kC = "ps%d" % (2 + p2)
                        self.mm(Db, self.ident, self.neg4, True, False, ["cb"], [kD])
                        for h in range(4):
                            j = 4 * g + h
                            self.mm(Db[:, h * 128:(h + 1) * 128], selg[:, j * 128:(j + 1) * 128], cs_cat[:, cs],
                                    False, False, ["selg", "cscat"], [kD])
                        self.mm(Db, f_cat[:, cs], selg[:, 4096 + g * 512:4096 + (g + 1) * 512], False, True, ["selg", "fcat"], [kD])
                        self.mm(CBb[:, 0:128], xc[:, 2, cs], xc[:, 3, cs], True, True, [("xc", 2), ("xc", 3)], [kC])
                        self.act(L2[p2][:], Db, AF.Exp, [kD], [("L2", p2)])
                        self.tt(M2[p2][:].rearrange("p (h l) -> p h l", h=4), L2[p2][:].rearrange("p (h l) -> p h l", h=4),
                                CBb[:, 0:128].unsqueeze(1).to_broadcast([128, 4, 128]), ALU.mult,
                                [("L2", p2), kC], [("M2", p2)])

                    def mid(c):
                        p2 = c % 2
                        cs = slice(c * 128, (c + 1) * 128)
                        Y1 = self.bank(4)
                        Y2 = self.bank(5)
                        STb = self.bank(6)
                        for h in range(4):
                            self.mm(Y1[:, h * 64:(h + 1) * 64], M2[p2][:, h * 128:(h + 1) * 128], xbt[:, c, h * 64:(h + 1) * 64],
                                    True, True, [("M2", p2), ("xbt", c // 4)], ["ps4"])
                        if c > 0:
                            self.mm(Y2[:, 0:256], xc[:, 3, cs], Sbf[:], True, True, [("xc", 3), "Sbf"], ["ps5"])
                        if c < 15:
                            self.mm(STb[:, 0:256], xbt[:, c, 256:384], Xw_all[:, c, :], True, True, [("xbt", c // 4), ("Xw", c // 4)], ["ps6"])
                        if c > 0:
                            self.tt(t1[:].rearrange("p (h d) -> p h d", h=4), Y2[:, 0:256].rearrange("p (h d) -> p h d", h=4),
                                    ecs_tok[:, c, 4 * g:4 * g + 4].unsqueeze(2).to_broadcast([128, 4, 64]), ALU.mult,
                                    ["ps5", "ecs_tok"], ["t1"])
                            self.tt(ych[p2][:], t1[:], Y1[:, 0:256], ALU.add, ["t1", "ps4"], [("ych", p2)])
                        else:
                            self.vcopy(ych[p2][:], Y1[:, 0:256], ["ps4"], [("ych", p2)])
                        if c < 15:
                            if c == 0:
                                self.acopy(Sst[:], STb[:, 0:256], ["ps6"], ["Sst"])
                            else:
                                self.acopy(STs[:], STb[:, 0:256], ["ps6"], ["STs"])
                                self.tt(Sd[:].rearrange("p (h d) -> p h d", h=4), Sst[:].rearrange("p (h d) -> p h d", h=4),
                                        dec_rep[:, c, 4 * g:4 * g + 4].unsqueeze(2).to_broadcast([128, 4, 64]), ALU.mult,
                                        ["Sst", "dec_rep"], ["Sd"], eng="pool")
                                self.tt(Sst[:], Sd[:], STs[:], ALU.add, ["Sd", "STs"], ["Sst"], eng="pool")
                            self.acopy(Sbf[:], Sst[:], ["Sst"], ["Sbf"])

                    def back(c):
                        p2 = c % 2
                        cs = slice(c * 128, (c + 1) * 128)
                        YT = self.bank(7)
                        for cc in range(2):
                            self.tr(YT[:, cc * 128:(cc + 1) * 128], ych[p2][:, cc * 128:(cc + 1) * 128], self.ident_f,
                                    [("ych", p2), "cf"], ["ps7"])
                        for cc in range(2):
                            ch = 2 * g + cc
                            self.stt(yT[:, cc, cs], xc[:, cc, cs], self.pp[:, DP0 + ch:DP0 + ch + 1], YT[:, cc * 128:(cc + 1) * 128],
                                     ALU.mult, ALU.add, [("xc", cc), "pp", "ps7"], [("yT", cc)])

                    front(0)
                    for c in range(16):
                        if c + 1 < 16:
                            front(c + 1)
                        mid(c)
                        if c >= 1:
                            back(c - 1)
                    back(15)
                    if g == 0:
                        self.dbg("yT0", yT[:], sq, [("yT", 0), ("yT", 1)])
                def zgate(g):
                    wb = g % 2
                    chs = [2 * g, 2 * g + 1, 16 + g, 24 + g]
                    for tb in range(4):
                        ts_ = slice(tb * 512, (tb + 1) * 512)
                        for cc in range(2):
                            bk = (tb * 2 + cc) % 4
                            i2 = (tb * 2 + cc) % 2
                            for k in range(8):
                                self.mm(self.bank(bk), wz[wb][:, k, cc * 128:(cc + 1) * 128], self.hnT[:, k, ts_], k == 0, k == 7,
                                        [("wz", wb), "hnT"], ["ps%d" % bk])
                            self.act(sz[i2][:], self.bank(bk), AF.Silu, ["ps%d" % bk], [("sz", i2)])
                            self.tt(yT[:, cc, ts_], yT[:, cc, ts_], sz[i2][:], ALU.mult, [("yT", cc), ("sz", i2)], [("yT", cc)], eng="pool")
                def normpart(g):
                    wb = g % 2
                    chs = [2 * g, 2 * g + 1, 16 + g, 24 + g]
                    for tb in range(4):
                        ts_ = slice(tb * 512, (tb + 1) * 512)
                        bk = 4 + tb
                        for cc in range(2):
                            i2 = cc
                            self.tt(sq_[i2][:], yT[:, cc, ts_], yT[:, cc, ts_], ALU.mult, [("yT", cc)], [("sq", i2)], eng="pool")
                            self.mm(self.bank(bk), self.ones_bf[:], sq_[i2][:], cc == 0, cc == 1, ["ones_bf", ("sq", i2)], ["ps%d" % bk])
                    for tb in range(4):
                        ts_ = slice(tb * 512, (tb + 1) * 512)
                        self.act(rr[tb][:], self.bank(4 + tb), AF.Ln, ["ps%d" % (4 + tb)], [("rr", tb)], bias=self.epsc[:, 0:1], scale=1.0 / 256)
                        self.act(rr[tb][:], rr[tb][:], AF.Exp, [("rr", tb)], [("rr", tb)], scale=-0.5)
                        for cc in range(2):
                            ch = 2 * g + cc
                            self.stt(yT[:, cc, ts_], yT[:, cc, ts_], self.pp[:, SN0 + ch:SN0 + ch + 1], rr[tb][:], ALU.mult, ALU.mult,
                                     [("yT", cc), "pp", ("rr", tb)], [("yT", cc)])
                    if g == 0:
                        self.dbg("yn0", yT[:], sq, [("yT", 0), ("yT", 1)])
                    for cc in range(2):
                        self.dma("sp", self.yn_d[2 * g + cc], yT[:, cc, :], [("yT", cc)], [("yn_d", 2 * g + cc)])
                prologue_a(0)
                prologue_b(0)
                for g in range(8):
                    chunkloop(g)
                    zgate(g)
                    if g + 1 < 8:
                        prologue_a(g + 1)
                    if g + 2 < 8:
                        load_w(g + 2)
                    normpart(g)
                    if g + 1 < 8:
                        prologue_b(g + 1)
                S.barrier()

    def phase_ssdproj(self, sq):
        nc = self.nc
        with ExitStack() as ph:
            sb = lambda n, s, d: ph.enter_context(nc.sbuf_tensor("%s_%d" % (n, sq), s, d))
            ynT = sb("ynT", [128, 16, T], BF16)
            wsp = [sb("wsp%d" % i, [128, 16, 512], BF16) for i in range(2)]
            wg = [sb("wgs%d" % i, [128, 8, 512], BF16) for i in range(2)]
            gsb = [sb("gsb%d" % i, [128, 512], BF16) for i in range(2)]
            for j in range(16):
                self.dma("sp", ynT[:, j, :], self.yn_d[j], [], [("ynT", j)])
            w_ssd_r = self.w_ssd.rearrange("(k p) n -> p k n", p=128)
            for dh in range(2):
                self.dma("pool", wsp[dh][:, 0:8, :], w_ssd_r[:, 0:8, dh * 512:(dh + 1) * 512], [], [("wsp", dh)])
                self.dma("pool", wsp[dh][:, 8:16, :], w_ssd_r[:, 8:16, dh * 512:(dh + 1) * 512], [], [("wsp", dh)])
                self.dma("pool", wg[dh][:], self.w_in_r[:, :, OG + dh * 512:OG + (dh + 1) * 512], [], [("wgs", dh)])
            it = 0
            for dh in range(2):
                for j in range(4):
                    dch = dh * 4 + j
                    for tb in range(4):
                        ts_ = slice(tb * 512, (tb + 1) * 512)
                        bP = (it % 4) * 2
                        bG = bP + 1
                        i2 = it % 2
                        it += 1
                        if dh == 0 and j == 0:
                            if tb == 0:
                                for tb2 in range(4):
                                    ts2 = slice(tb2 * 512, (tb2 + 1) * 512)
                                    for k in range(8):
                                        self.mm(self.bank(tb2 * 2 + 1), wg[dh][:, k, 0:128], self.hnT[:, k, ts2], k == 0, k == 7,
                                                [("wgs", dh), "hnT"], ["ps%d" % (tb2 * 2 + 1)])
                                for k in range(16):
                                    for tb2 in range(4):
                                        ts2 = slice(tb2 * 512, (tb2 + 1) * 512)
                                        self.mm(self.bank(tb2 * 2), wsp[dh][:, k, 0:128], ynT[:, k, ts2], k == 0, k == 15,
                                                [("wsp", dh), ("ynT", k)], ["ps%d" % (tb2 * 2)])
                        else:
                            for k in range(16):
                                self.mm(self.bank(bP), wsp[dh][:, k, j * 128:(j + 1) * 128], ynT[:, k, ts_], k == 0, k == 15,
                                        [("wsp", dh), ("ynT", k)], ["ps%d" % bP])
                            for k in range(8):
                                self.mm(self.bank(bG), wg[dh][:, k, j * 128:(j + 1) * 128], self.hnT[:, k, ts_], k == 0, k == 7,
                                        [("wgs", dh), "hnT"], ["ps%d" % bG])
                        self.act(gsb[i2][:], self.bank(bG), AF.Sigmoid, ["ps%d" % bG, "pp"], [("gsb", i2)],
                                 bias=self.pp[:, BG0 + dch:BG0 + dch + 1])
                        self.tt(self.m1[:, dch, ts_], gsb[i2][:], self.bank(bP), ALU.mult, [("gsb", i2), "ps%d" % bP], ["m1"])
            self.dbg("m1s", self.m1[:], sq, ["m1"])


class PhasesB:
    def phase_att(self, sq):
        nc = self.nc
        S = self.S
        with ExitStack() as ph:
            sb = lambda n, s, d: ph.enter_context(nc.sbuf_tensor("%s_%d" % (n, sq), s, d))
            yattT = sb("yattT", [128, 8, T], BF16)
            with ExitStack() as a1:
                sba = lambda n, s, d: a1.enter_context(nc.sbuf_tensor("%s_%d" % (n, sq), s, d))
                wqk = [sba("wqk%d" % i, [128, 8, 512], BF16) for i in range(2)]
                wv = [sba("wv%d" % i, [128, 8, 256], BF16) for i in range(2)]
                qk_tok = sba("qk_tok", [128, 16, 512], BF16)
                v_aug = sba("v_aug", [128, 16, 4, 66], BF16)
                qTm = sba("qTm", [128, 2, 2, T], BF16)
                kT = sba("kT", [128, 2, T], BF16)
                kms = sba("kms", [128, 2, 8], F32)
                km_hi = sba("km_hi", [128, 2, 8], BF16)
                km_lo = sba("km_lo", [128, 2, 8], BF16)
                PT = [sba("PT%d" % i, [128, 512], BF16) for i in range(5)]
                yat = sba("yat", [128, 16, 256], BF16)
                g8all = sba("g8all", [128, 256], F32)
                m8all = sba("m8all", [128, 32, 8], F32)
                selA = sba("selA", [128, 16, 4, 8], F32)
                tmp = [sba("atmp%d" % i, [128, 7 * 65], F32) for i in range(2)]
                red = [sba("ared%d" % i, [128, 65], F32) for i in range(2)]
                tot = [sba("atot%d" % i, [128, 65], F32) for i in range(2)]
                rinv = [sba("arinv%d" % i, [128, 1], F32) for i in range(2)]
                rt = [sba("rt%d" % i, [128, 8, 8], F32) for i in range(4)]
                self.memset(v_aug[:].rearrange("p a b c -> p (a b c)"), 1.0, [], ["v_aug"])

                def load_w(hg):
                    wb = hg % 2
                    self.dma("pool", wqk[wb][:, :, 0:256], self.w_in_r[:, :, OQ + hg * 256:OQ + (hg + 1) * 256], [], [("wqk", wb)])
                    self.dma("pool", wqk[wb][:, :, 256:512], self.w_in_r[:, :, OK_ + hg * 256:OK_ + (hg + 1) * 256], [], [("wqk", wb)])
                    self.dma("pool", wv[wb][:], self.w_in_r[:, :, OV + hg * 256:OV + (hg + 1) * 256], [], [("wv", wb)])

                def emit_ytr(hg_, t):
                    tl = slice(t * 128, (t + 1) * 128)
                    bk = 4 + t % 4
                    pb = self.bankb(bk)
                    for i in range(2):
                        self.tr(pb[:, i * 128:(i + 1) * 128], yat[:, t, i * 128:(i + 1) * 128], self.ident, ["yat", "cb"], ["ps%d" % bk])
                    if bk in (4, 5):
                        self.acopy(yattT[:, 2 * hg_:2 * hg_ + 2, tl], pb[:, 0:256].rearrange("p (c l) -> p c l", c=2), ["ps%d" % bk], ["yattT"])
                    else:
                        self.vcopy(yattT[:, 2 * hg_:2 * hg_ + 2, tl], pb[:, 0:256].rearrange("p (c l) -> p c l", c=2), ["ps%d" % bk], ["yattT"])

                load_w(0)
                pti = 0
                sri = 0
                aci = 0
                for hg in range(4):
                    wb = hg % 2
                    if hg + 1 < 4:
                        load_w(hg + 1)
                    def emit_tr(t):
                        tl = slice(t * 128, (t + 1) * 128)
                        bk = 4 + t % 4
                        pb = self.bankb(bk)
                        for i in range(4):
                            self.tr(pb[:, i * 128:(i + 1) * 128], qk_tok[:, t, i * 128:(i + 1) * 128], self.ident,
                                    [("qk_tok", t), "cb"], ["ps%d" % bk])
                        qsrc = pb[:, 0:256].rearrange("p (c l) -> p c l", c=2)
                        ksrc = pb[:, 256:512].rearrange("p (c l) -> p c l", c=2)
                        if bk in (4, 5):
                            for par in range(2):
                                self.S.op("act", (lambda o_, i_, c_: lambda e: e.mul(out=o_, in_=i_, mul=self.cf[:, c_:c_ + 1]))(qTm[:, par, :, tl], qsrc, 384 + par),
                                          reads=["ps%d" % bk, "cf"], writes=[("qT", par)])
                            self.acopy(kT[:, :, tl], ksrc, ["ps%d" % bk], ["kT"])
                        else:
                            for par in range(2):
                                self.ts(qTm[:, par, :, tl], qsrc, self.cf[:, 384 + par:385 + par], None, ALU.mult, None, ["ps%d" % bk, "cf"], [("qT", par)])
                            self.vcopy(kT[:, :, tl], ksrc, ["ps%d" % bk], ["kT"])
                    for t in range(NT):
                        tl = slice(t * 128, (t + 1) * 128)
                        bq = (t % 2) * 2
                        bv = bq + 1
                        for k in range(8):
                            self.mm(self.bank(bq), self.hnT[:, k, tl], wqk[wb][:, k, :], k == 0, k == 7, ["hnT", ("wqk", wb)], ["ps%d" % bq])
                        for k in range(8):
                            self.mm(self.bank(bv)[:, 0:256], self.hnT[:, k, tl], wv[wb][:, k, :], k == 0, k == 7, ["hnT", ("wv", wb)], ["ps%d" % bv])
                        Pv = self.bank(bq).rearrange("p (h d) -> p h d", h=8)
                        O = qk_tok[:, t, :].rearrange("p (h d) -> p h d", h=8)
                        cosb = self.cf[:, t * 8:(t + 1) * 8].unsqueeze(1).to_broadcast([128, 8, 8])
                        sinb = self.cf[:, 128 + t * 8:128 + (t + 1) * 8].unsqueeze(1).to_broadcast([128, 8, 8])
                        kq = "ps%d" % bq
                        self.vcopy(O[:, :, 16:64], Pv[:, :, 16:64], [kq], [("qk_tok", t)])
                        self.tt(rt[0][:], Pv[:, :, 0:8], cosb, ALU.mult, [kq, "cf"], ["rt0"])
                        self.tt(rt[1][:], Pv[:, :, 8:16], sinb, ALU.mult, [kq, "cf"], ["rt1"])
                        self.tt(rt[2][:], Pv[:, :, 8:16], cosb, ALU.mult, [kq, "cf"], ["rt2"])
                        self.tt(rt[3][:], Pv[:, :, 0:8], sinb, ALU.mult, [kq, "cf"], ["rt3"])
                        self.tt(O[:, :, 0:8], rt[0][:], rt[1][:], ALU.subtract, ["rt0", "rt1"], [("qk_tok", t)])
                        self.tt(O[:, :, 8:16], rt[2][:], rt[3][:], ALU.add, ["rt2", "rt3"], [("qk_tok", t)])
                        self.acopy(v_aug[:, t, :, 0:64], self.bank(bv)[:, 0:256].rearrange("p (h d) -> p h d", h=4),
                                   ["ps%d" % bv], ["v_aug"])
                        if t >= 1:
                            emit_tr(t - 1)
                        if hg >= 1:
                            emit_ytr(hg - 1, t)
                    emit_tr(NT - 1)
                    if self.att_lvl == -1:
                        S.barrier()
                        return
                    if self.att_lvl == -2:
                        continue
                    if hg == 0:
                        self.dbg("kT0", kT[:], sq, ["kT"])
                    if "K" not in self.att_skip:
                        S.op("dve", lambda e: e.tensor_reduce(out=kms[:], in_=kT[:].rearrange("p c (n j) -> p c n j", j=256),
                                                              axis=AX.X, op=ALU.add), reads=["kT"], writes=["kms"])
                        self.vcopy(km_hi[:], kms[:], ["kms"], ["km_hi"])
                        self.tt(km_lo[:], kms[:], km_hi[:], ALU.subtract, ["kms", "km_hi"], ["km_lo"])
                    if self.att_lvl >= 1:
                        Gps = self.bank(7)
                        first = True
                        for t in range(8, NT):
                            tl = slice(t * 128, (t + 1) * 128)
                            for h in range(4):
                                c = h // 2
                                o_ = Gps[:, (t - 8) * 32 + h * 8:(t - 8) * 32 + (h + 1) * 8]
                                self.mm(o_, qTm[:, h % 2, c, tl], km_hi[:, c, :], first, False, [("qT", h % 2), "km_hi"], ["ps7"])
                                first = False
                                self.mm(o_, qTm[:, h % 2, c, tl], km_lo[:, c, :], False, (t == NT - 1 and h == 3), [("qT", h % 2), "km_lo"], ["ps7"])
                        self.tt(g8all[:], Gps[:, 0:256], self.cf[:, 386:642], ALU.add, ["ps7", "cf"], ["g8"])
                        for idx in range(32):
                            S.op("dve", (lambda i_: (lambda e: e.max(out=m8all[:, i_, :], in_=g8all[:, i_ * 8:(i_ + 1) * 8])))(idx), reads=["g8"], writes=["m8"])
                        self.tt(selA[:, 8:16, :, :].rearrange("p t h n -> p (t h) n"), g8all[:].rearrange("p (g n) -> p g n", n=8),
                                m8all[:, :, 2:3].to_broadcast([128, 32, 8]), ALU.is_ge, ["g8", "m8"], ["selA"])
                    if hg == 0:
                        self.dbg("selA", selA[:], sq, ["selA"])
                    chunks_l = []
                    t_order = []
                    for i_ in range(NT // 2):
                        t_order += [i_, NT - 1 - i_]
                    for h in (range(4) if self.att_lvl >= 2 else []):
                        for t in t_order:
                            nk = t + 1
                            cl = [(k0_, min(k0_ + 4, nk)) for k0_ in range(0, nk, 4)]
                            for ci, (k0, k1) in enumerate(cl):
                                chunks_l.append((h, t, k0, k1, ci == len(cl) - 1))
                    unit_ai = {}

                    def emit_qk(idx):
                        h, t, k0, k1, last = chunks_l[idx]
                        c = h // 2
                        tl = slice(t * 128, (t + 1) * 128)
                        si = idx % 4
                        pi = idx % 5
                        sreg = self.bank(si)
                        ks = ["ps%d" % si]
                        for kt in range(k0, k1):
                            col = (kt - k0) * 128
                            self.mm(sreg[:, col:col + 128], kT[:, c, kt * 128:(kt + 1) * 128], qTm[:, h % 2, c, tl],
                                    True, kt != t, ["kT", ("qT", h % 2)], ks)
                            if kt == t:
                                self.mm(sreg[:, col:col + 128], self.ident, self.neg4[:, 0:128], False, True, ["cb"], ks)
                        n = (k1 - k0) * 128
                        self.act(PT[pi][:, 0:n], sreg[:, 0:n], AF.Exp, ks, [("PT", pi)], scale=0.125)

                    def emit_pv(idx):
                        h, t, k0, k1, last = chunks_l[idx]
                        blk = t // 2
                        pi = idx % 5
                        if (h, t) not in unit_ai:
                            unit_ai[(h, t)] = len(unit_ai) % 2
                        ai = unit_ai[(h, t)]
                        acc0 = self.bank(4 + 2 * ai)
                        acc1 = self.bank(5 + 2 * ai)
                        ka0 = "ps%d" % (4 + 2 * ai)
                        ka1 = "ps%d" % (5 + 2 * ai)
                        for kt in range(k0, k1):
                            col = (kt - k0) * 128
                            if blk <= 3:
                                o, st_, sp_, kk = acc0[:, 0:65], kt == 0, kt == t, ka0
                            else:
                                n_ = kt // 2
                                if n_ < blk:
                                    o, st_, sp_, kk = acc0[:, n_ * 65:(n_ + 1) * 65], kt % 2 == 0, kt % 2 == 1, ka0
                                else:
                                    o, st_, sp_, kk = acc1[:, 0:65], kt == 2 * blk, kt == t, ka1
                            self.mm(o, PT[pi][:, col:col + 128], v_aug[:, kt, h, 0:65], st_, sp_, [("PT", pi), "v_aug"], [kk])
                        if not last:
                            return
                        yo = yat[:, t, h * 64:(h + 1) * 64]
                        if blk <= 3:
                            self.recip(rinv[ai][:], acc0[:, 64:65], [ka0], [("rinv", ai)])
                            self.ts(yo, acc0[:, 0:64], rinv[ai][:, 0:1], None, ALU.mult, None, [ka0, ("rinv", ai)], ["yat"])
                        else:
                            self.tt(tmp[ai][:, 0:blk * 65].rearrange("p (n d) -> p n d", n=blk),
                                    acc0[:, 0:blk * 65].rearrange("p (n d) -> p n d", n=blk),
                                    selA[:, t, h, 0:blk].unsqueeze(2).to_broadcast([128, blk, 65]), ALU.mult,
                                    [ka0, "selA"], [("atmp", ai)])
                            S.op("dve", (lambda a_, b_: (lambda e: e.tensor_reduce(
                                out=red[a_][:], in_=tmp[a_][:, 0:b_ * 65].rearrange("p (n d) -> p d n", n=b_), axis=AX.X, op=ALU.add)))(ai, blk),
                                reads=[("atmp", ai)], writes=[("ared", ai)])
                            self.tt(tot[ai][:], red[ai][:], acc1[:, 0:65], ALU.add, [("ared", ai), ka1], [("atot", ai)])
                            self.recip(rinv[ai][:], tot[ai][:, 64:65], [("atot", ai)], [("rinv", ai)])
                            self.ts(yo, tot[ai][:, 0:64], rinv[ai][:, 0:1], None, ALU.mult, None, [("atot", ai), ("rinv", ai)], ["yat"])

                    DEP = 3
                    for idx in range(len(chunks_l)):
                        emit_qk(idx)
                        if idx >= DEP:
                            emit_pv(idx - DEP)
                    for idx in range(max(0, len(chunks_l) - DEP), len(chunks_l)):
                        emit_pv(idx)
                    if hg == 3:
                        for t in range(NT):
                            emit_ytr(hg, t)
                S.barrier()
            self.dbg("yattT", yattT[:], sq, ["yattT"])
            if "M" in self.att_skip:
                return
            with ExitStack() as a2:
                sbm = lambda n, s, d: a2.enter_context(nc.sbuf_tensor("%s_%d" % (n, sq), s, d))
                watt = [sbm("watt%d" % i, [128, 8, 512], BF16) for i in range(2)]
                wg = [sbm("wga%d" % i, [128, 8, 512], BF16) for i in range(2)]
                gsb = [sbm("gsa%d" % i, [128, 512], BF16) for i in range(2)]
                tmpm = [sbm("tmpm%d" % i, [128, 512], BF16) for i in range(2)]
                w_att_r = self.w_att.rearrange("(k p) n -> p k n", p=128)
                for dh in range(2):
                    self.dma("pool", watt[dh][:], w_att_r[:, :, dh * 512:(dh + 1) * 512], [], [("watt", dh)])
                    self.dma("pool", wg[dh][:], self.w_in_r[:, :, OG + 1024 + dh * 512:OG + 1024 + (dh + 1) * 512], [], [("wga", dh)])
                it = 0
                for dh in range(2):
                    for j in range(4):
                        dch = dh * 4 + j
                        for tb in range(4):
                            ts_ = slice(tb * 512, (tb + 1) * 512)
                            bP = (it % 4) * 2
                            bG = bP + 1
                            i2 = it % 2
                            it += 1
                            for k in range(8):
                                self.mm(self.bank(bP), watt[dh][:, k, j * 128:(j + 1) * 128], yattT[:, k, ts_], k == 0, k == 7,
                                        [("watt", dh), "yattT"], ["ps%d" % bP])
                            for k in range(8):
                                self.mm(self.bank(bG), wg[dh][:, k, j * 128:(j + 1) * 128], self.hnT[:, k, ts_], k == 0, k == 7,
                                        [("wga", dh), "hnT"], ["ps%d" % bG])
                            self.act(gsb[i2][:], self.bank(bG), AF.Sigmoid, ["ps%d" % bG, "pp"], [("gsa", i2)],
                                     bias=self.pp[:, BG0 + 8 + dch:BG0 + 8 + dch + 1])
                            self.tt(tmpm[i2][:], gsb[i2][:], self.bank(bP), ALU.mult, [("gsa", i2), "ps%d" % bP], [("tmpm", i2)])
                            self.tt(self.m1[:, dch, ts_], self.m1[:, dch, ts_], tmpm[i2][:], ALU.add, ["m1", ("tmpm", i2)], ["m1"], eng="pool")
                self.dbg("m1", self.m1[:], sq, ["m1"])
                S.barrier()

    def phase_ffn(self, sq):
        nc = self.nc
        S = self.S
        h2T = self.m1
        with ExitStack() as ph:
            sb = lambda n, s, d: ph.enter_context(nc.sbuf_tensor("%s_%d" % (n, sq), s, d))
            xn = sb("xn", [128, 16, D], F32)
            with ExitStack() as s1:
                sb1 = lambda n, s, d: s1.enter_context(nc.sbuf_tensor("%s_%d" % (n, sq), s, d))
                wout = sb1("wout", [128, 8, D], BF16)
                nwb = sb1("n2wb", [128, D], F32)
                xt = [sb1("xt%d" % i, [128, D], F32) for i in range(3)]
                h2 = [sb1("h2_%d" % i, [128, D], BF16) for i in range(2)]
                junk = sb1("junk2", [128, D], BF16)
                ss = sb1("ss2", [128, 16], F32)
                r1 = sb1("r12", [128, 16], F32)
                rstd = sb1("rstd2", [128, 16], F32)
                w_out_r = self.w_out.rearrange("(k p) n -> p k n", p=128)
                for dh in range(2):
                    self.dma("pool", wout[:, :, dh * 512:(dh + 1) * 512], w_out_r[:, :, dh * 512:(dh + 1) * 512], [], [("wout", dh)])
                self.dma("sp", nwb[:], self.nw[1:2, :].partition_broadcast(128), [], ["n2wb"])
                def partA(t):
                    tl = slice(t * 128, (t + 1) * 128)
                    b = t % 2
                    x3 = t % 3
                    self.dma("sp" if t % 2 == 0 else "pool", xt[x3][:], self.x[sq, tl, :], [], [("xt", x3)])
                    reg = self.PSD[b]
                    ks = ["ps%d" % (2 * b), "ps%d" % (2 * b + 1)]
                    for dh in range(2):
                        for k in range(8):
                            self.mm(reg[:, dh * 512:(dh + 1) * 512], self.m1[:, k, tl], wout[:, k, dh * 512:(dh + 1) * 512], k == 0, k == 7,
                                    [("m1", t), ("wout", dh)], [ks[dh]])
                    self.tt(xn[:, t, :], xt[x3][:], reg[:], ALU.add, [("xt", x3)] + ks, [("xn", t)])
                    self.act(junk[:], xn[:, t, :], AF.Square, [("xn", t)], ["junk2", ("ss2", t)], accum=ss[:, t:t + 1])
                    self.act(r1[:, t:t + 1], ss[:, t:t + 1], AF.Sqrt, [("ss2", t)], [("r12", t)], bias=self.epsc[:, 0:1], scale=1.0 / D)
                    self.recip(rstd[:, t:t + 1], r1[:, t:t + 1], [("r12", t)], [("rstd2", t)])
                    self.stt(h2[b][:], xn[:, t, :], rstd[:, t:t + 1], nwb[:], ALU.mult, ALU.mult,
                             [("xn", t), ("rstd2", t), "n2wb"], [("h2", b)])

                def partB(t):
                    tl = slice(t * 128, (t + 1) * 128)
                    b = t % 2
                    pb = self.bankb(4 + b)
                    for j in range(8):
                        self.tr(pb[:, j * 128:(j + 1) * 128], h2[b][:, j * 128:(j + 1) * 128], self.ident, [("h2", b), "cb"], ["ps%d" % (4 + b)])
                    self.evac(h2T[:, :, tl], pb.rearrange("p (k c) -> p k c", k=8), ["ps%d" % (4 + b)], [("m1", t)])

                partA(0)
                for t in range(1, NT):
                    partA(t)
                    partB(t - 1)
                partB(NT - 1)
                self.dbg("xn0", xn[:, 0:8, :], sq, [("xn", i) for i in range(8)])
                S.barrier()
            if self.upto < 5:
                return
            with ExitStack() as s2:
                sb2 = lambda n, s, d: s2.enter_context(nc.sbuf_tensor("%s_%d" % (n, sq), s, d))
                uT = sb2("uT", [128, 22, 1024], BF16)
                wa = [sb2("wa%d" % i, [128, 8, 256], BF16) for i in range(2)]
                wb_ = [sb2("wb%d" % i, [128, 8, 256], BF16) for i in range(2)]
                wd = [sb2("wd%d" % i, [128, 22, 256], BF16) for i in range(2)]
                nfwb = sb2("nfwb", [128, D], F32)
                araw = [sb2("araw%d" % i, [128, 1026], BF16) for i in range(2)]
                sa = [sb2("sa%d" % i, [128, 512], BF16) for i in range(2)]
                fdiag = [sb2("fdiag%d" % i, [128, 3, 128], BF16) for i in range(2)]
                outt = [sb2("outt%d" % i, [128, D], F32) for i in range(2)]
                junk = sb2("junk3", [128, D], BF16)
                ss = sb2("ss3", [128, 16], F32)
                r1 = sb2("r13", [128, 16], F32)
                rstd = sb2("rstd3", [128, 16], F32)
                w_up_r = self.w_up.rearrange("(k p) n -> p k n", p=128)
                w_dn_r = self.w_dn.rearrange("(k p) n -> p k n", p=128)
                self.dma("sp", nfwb[:], self.nw[2:3, :].partition_broadcast(128), [], ["nfwb"])
                wcnt = [0]

                def load_up(jg):
                    i = wcnt[0] % 2
                    wcnt[0] += 1
                    self.dma("pool", wa[i][:], w_up_r[:, :, jg * 256:(jg + 1) * 256], [], [("wa", i)])
                    self.dma("pool", wb_[i][:], w_up_r[:, :, 2816 + jg * 256:2816 + (jg + 1) * 256], [], [("wb", i)])
                    return i

                dcnt = [0]

                def load_dn(dq):
                    i = dcnt[0] % 2
                    dcnt[0] += 1
                    self.dma("pool", wd[i][:], w_dn_r[:, :, dq * 256:(dq + 1) * 256], [], [("wd", i)])
                    return i

                nxt_up = load_up(0)
                for hf in range(2):
                    tok0 = hf * 1024
                    dn_idx = {}
                    for jg in range(11):
                        wi = nxt_up
                        if jg + 1 < 11:
                            nxt_up = load_up(jg + 1)
                        else:
                            dn_idx[0] = load_dn(0)
                            dn_idx[1] = load_dn(1)
                            if hf == 0:
                                nxt_up = load_up(0)
                        for jj in range(2):
                            j = jg * 2 + jj
                            rb = j % 2
                            for k in range(3):
                                col = FW0 + j * 3 + k
                                self.S.op("act", (lambda o_, c_: lambda e: e.mul(out=o_, in_=self.ident, mul=self.pp[:, c_:c_ + 1]))(fdiag[rb][:, k, :], col),
                                          reads=["cb", "pp"], writes=[("fdiag", rb)])
                            if hf == 0:
                                self.memset(araw[rb][:, 0:2], 0.0, [], [("araw", rb)])
                            else:
                                self.vcopy(araw[rb][:, 0:2], self.carry[:, j, :], ["carry"], [("araw", rb)], eng="pool")
                            for tb in range(2):
                                ts_ = slice(tok0 + tb * 512, tok0 + (tb + 1) * 512)
                                for k in range(8):
                                    self.mm(self.bank(tb), wa[wi][:, k, jj * 128:(jj + 1) * 128], h2T[:, k, ts_], k == 0, k == 7,
                                            [("wa", wi), "h2T"], ["ps%d" % tb])
                                for k in range(8):
                                    self.mm(self.bank(2 + tb), wb_[wi][:, k, jj * 128:(jj + 1) * 128], h2T[:, k, ts_], k == 0, k == 7,
                                            [("wb", wi), "h2T"], ["ps%d" % (2 + tb)])
                                self.evac(araw[rb][:, 2 + tb * 512:2 + (tb + 1) * 512], self.bank(tb), ["ps%d" % tb], [("araw", rb)])
                            if hf == 0:
                                self.vcopy(self.carry[:, j, :], araw[rb][:, 1024:1026], [("araw", rb)], ["carry"], eng="pool")
                            for tb in range(2):
                                us_ = slice(tb * 512, (tb + 1) * 512)
                                for k in range(3):
                                    self.mm(self.bank(4 + tb), fdiag[rb][:, k, :], araw[rb][:, tb * 512 + k:tb * 512 + k + 512], k == 0, k == 2,
                                            [("fdiag", rb), ("araw", rb)], ["ps%d" % (4 + tb)])
                                self.act(sa[tb][:], self.bank(4 + tb), AF.Silu, ["ps%d" % (4 + tb), "pp"], [("sa", tb)],
                                         bias=self.pp[:, FB0 + j:FB0 + j + 1])
                                self.tt(uT[:, j, us_], sa[tb][:], self.bank(2 + tb), ALU.mult, [("sa", tb), "ps%d" % (2 + tb)], ["uT"])
                    for dq in range(4):
                        if dq >= 2:
                            dn_idx[dq] = load_dn(dq)
                        di = dn_idx[dq]
                        for tl_ in range(8):
                            t = hf * 8 + tl_
                            bk = 6 + tl_ % 2
                            for j in range(22):
                                self.mm(self.bank(bk)[:, 0:256], uT[:, j, tl_ * 128:(tl_ + 1) * 128], wd[di][:, j, :], j == 0, j == 21,
                                        ["uT", ("wd", di)], ["ps%d" % bk])
                            self.tt(xn[:, t, dq * 256:(dq + 1) * 256], xn[:, t, dq * 256:(dq + 1) * 256], self.bank(bk)[:, 0:256], ALU.add,
                                    [("xn", t), "ps%d" % bk], [("xn", t)])
                    for tl_ in range(8):
                        t = hf * 8 + tl_
                        b = t % 2
                        self.act(junk[:], xn[:, t, :], AF.Square, [("xn", t)], ["junk3", ("ss3", t)], accum=ss[:, t:t + 1])
                        self.act(r1[:, t:t + 1], ss[:, t:t + 1], AF.Sqrt, [("ss3", t)], [("r13", t)], bias=self.epsc[:, 0:1], scale=1.0 / D)
                        self.recip(rstd[:, t:t + 1], r1[:, t:t + 1], [("r13", t)], [("rstd3", t)])
                        self.stt(outt[b][:], xn[:, t, :], rstd[:, t:t + 1], nfwb[:], ALU.mult, ALU.mult,
                                 [("xn", t), ("rstd3", t), "nfwb"], [("outt", b)])
                        self.dma("sp", self.out[sq, t * 128:(t + 1) * 128, :], outt[b][:], [("outt", b)], [("out", t)])
                S.barrier()


class KB(PhasesA, PhasesB):
    def __init__(self, nc, nseq, dbg_names):
        self.nc = nc
        self.nseq = nseq
        self.S = Sched(nc)
        self.dbg_names = dbg_names
        self.dbg_t = {}
        self.evac_rr = 0

    def mm(self, out, lhsT, rhs, start, stop, r, w):
        self.S.op("pe", lambda e: e.matmul(out, lhsT=lhsT, rhs=rhs, start=start, stop=stop, skip_group_check=True), reads=r, writes=w)

    def tr(self, out, in_, ident, r, w):
        self.S.op("pe", lambda e: e.transpose(out=out, in_=in_, identity=ident), reads=r, writes=w)

    def act(self, out, in_, func, r, w, bias=None, scale=1.0, accum=None):
        kw = {}
        if bias is not None:
            kw["bias"] = bias
        if accum is not None:
            kw["accum_out"] = accum
        self.S.op("act", lambda e: e.activation(out=out, in_=in_, func=func, scale=scale, **kw), reads=r, writes=w)

    def acopy(self, out, in_, r, w):
        self.S.op("act", lambda e: e.copy(out=out, in_=in_), reads=r, writes=w)

    def vcopy(self, out, in_, r, w, eng="dve"):
        self.S.op(eng, lambda e: e.tensor_copy(out=out, in_=in_), reads=r, writes=w)

    def evac(self, out, in_, r, w):
        self.evac_rr += 1
        if self.evac_rr % 2:
            self.acopy(out, in_, r, w)
        else:
            self.vcopy(out, in_, r, w)

    def tt(self, out, in0, in1, op, r, w, eng="dve"):
        self.S.op(eng, lambda e: e.tensor_tensor(out=out, in0=in0, in1=in1, op=op), reads=r, writes=w)

    def ts(self, out, in0, s1, s2, op0, op1, r, w, eng="dve"):
        if s2 is None:
            self.S.op(eng, lambda e: e.tensor_scalar(out=out, in0=in0, scalar1=s1, scalar2=None, op0=op0), reads=r, writes=w)
        else:
            self.S.op(eng, lambda e: e.tensor_scalar(out=out, in0=in0, scalar1=s1, scalar2=s2, op0=op0, op1=op1), reads=r, writes=w)

    def stt(self, out, in0, scalar, in1, op0, op1, r, w):
        self.S.op("dve", lambda e: e.scalar_tensor_tensor(out=out, in0=in0, scalar=scalar, in1=in1, op0=op0, op1=op1), reads=r, writes=w)

    def recip(self, out, in_, r, w):
        self.S.op("dve", lambda e: e.reciprocal(out=out, in_=in_), reads=r, writes=w)

    def memset(self, ap, val, r, w, eng="pool"):
        self.S.op(eng, lambda e: e.memset(ap, val), reads=r, writes=w)

    def dma(self, q, out, in_, r, w):
        self.S.dma(q, lambda e: e.dma_start(out=out, in_=in_), reads=r, writes=w)

    def bank(self, i):
        return self.PSD[i // 2][:, (i % 2) * 512:(i % 2) * 512 + 512]

    def bankb(self, i):
        return self.PSD[i // 2][:].bitcast(BF16)[:, (i % 2) * 1024:(i % 2) * 1024 + 1024]

    def dbg(self, name, ap, sq, r):
        if sq != 0 or name not in self.dbg_names:
            return
        shape = list(ap.shape)
        dt_ = ap.dtype
        t = self.nc.dram_tensor("dbg_" + name, shape, dt_, kind="ExternalOutput").ap()
        self.dbg_t[name] = t
        self.dma("sp", t, ap, r, ["dbgout_" + name])

    def build(self):
        nc = self.nc
        S = self.S
        nseq = self.nseq
        self.x = nc.dram_tensor("x", [nseq, T, D], F32, kind="ExternalInput").ap()
        self.w_in = nc.dram_tensor("w_in", [D, 11296], F32, kind="ExternalInput").ap()
        self.w_ssd = nc.dram_tensor("w_ssd", [2048, D], F32, kind="ExternalInput").ap()
        self.w_att = nc.dram_tensor("w_att", [D, D], F32, kind="ExternalInput").ap()
        self.w_out = nc.dram_tensor("w_out", [D, D], F32, kind="ExternalInput").ap()
        self.w_up = nc.dram_tensor("w_up", [D, 5632], F32, kind="ExternalInput").ap()
        self.w_dn = nc.dram_tensor("w_dn", [2816, D], F32, kind="ExternalInput").ap()
        self.nw = nc.dram_tensor("nw", [3, D], F32, kind="ExternalInput").ap()
        self.pp_d = nc.dram_tensor("pp", [128, NPP], F32, kind="ExternalInput").ap()
        self.cf_d = nc.dram_tensor("cf", [128, NCF], F32, kind="ExternalInput").ap()
        self.cb_d = nc.dram_tensor("cb", [128, NCB], F32, kind="ExternalInput").ap()
        self.selg_d = nc.dram_tensor("selg", [128, 8192], F32, kind="ExternalInput").ap()
        self.out = nc.dram_tensor("out", [nseq, T, D], F32, kind="ExternalOutput").ap()
        self.yn_d = nc.dram_tensor("yn_scr", [16, 128, T], BF16, kind="Internal").ap()
        self.w_in_r = self.w_in.rearrange("(k p) n -> p k n", p=128)

        with ExitStack() as st:
            S.open(st)
            self.PSD = [st.enter_context(nc.psum_tensor("psd%d" % i, [128, 1024], F32)) for i in range(4)]
            sb = lambda n, s, d: st.enter_context(nc.sbuf_tensor(n, s, d))
            self.pp = sb("pp_sb", [128, NPP], F32)
            self.cf = sb("cf_sb", [128, NCF], F32)
            self.cbp = sb("cb_sb", [128, NCB], BF16)
            self.ones_bf = sb("ones_bf", [128, 128], BF16)
            self.ones_f = sb("ones_f", [128, 128], F32)
            self.dma("sp", self.pp[:], self.pp_d, [], ["pp"])
            self.dma("sp", self.cf[:], self.cf_d, [], ["cf"])
            self.dma("pool", self.cbp[:], self.cb_d, [], ["cb"])
            self.memset(self.ones_bf[:], 1.0, [], ["ones_bf"])
            self.memset(self.ones_f[:], 1.0, [], ["ones_f"])
            self.epsc = sb("epsc", [128, 1], F32)
            self.onec = sb("onec", [128, 1], F32)
            self.carry = sb("carry", [128, 22, 2], BF16)
            self.memset(self.epsc[:], EPS, [], ["epsc"])
            self.memset(self.onec[:], 1.0, [], ["onec"])
            self.ident = self.cbp[:, 0:128]
            self.neg4 = self.cbp[:, 128:640]
            self.ident_f = self.cf[:, 256:384]
            self.CONST = ["pp", "cf", "cb", "ones_bf", "ones_f"]

            for sq in range(nseq):
                with ExitStack() as sst:
                    self.m1 = sst.enter_context(nc.sbuf_tensor("m1_%d" % sq, [128, 8, T], BF16))
                    with ExitStack() as hst:
                        self.hnT = hst.enter_context(nc.sbuf_tensor("hnT%d" % sq, [128, 8, T], BF16))
                        self.phase_norm1(sq)
                        S.barrier()
                        if self.upto >= 2 and not SKIP_SSD:
                            self.phase_dt_ssd(sq)
                            S.barrier()
                            self.phase_ssdproj(sq)
                            S.barrier()
                        if self.upto >= 3:
                            self.phase_att(sq)
                            S.barrier()
                    if self.upto >= 4:
                        self.phase_ffn(sq)
                        S.barrier()
                S.barrier()
            S.emit()
        return nc


ATT_LVL = 2
ATT_SKIP = ""
SKIP_SSD = False
def _consts():
    cf = np.zeros((128, NCF), np.float32)
    inv_freq = (np.float32(500000.0) ** (-np.arange(0, 16, 2, dtype=np.float32) / np.float32(16))).astype(np.float32)
    pos = np.arange(T, dtype=np.float32)
    ang = (pos[:, None] * inv_freq[None, :]).astype(np.float32)
    cos = np.cos(ang).astype(np.float32)
    sin = np.sin(ang).astype(np.float32)
    cf[:, 0:128] = cos.reshape(16, 128, 8).transpose(1, 0, 2).reshape(128, 128)
    cf[:, 128:256] = sin.reshape(16, 128, 8).transpose(1, 0, 2).reshape(128, 128)
    cf[:, 256:384] = np.eye(128, dtype=np.float32)
    cf[0:64, 384] = 1.0
    cf[64:128, 385] = 1.0
    gm = np.zeros((8, 4, 8), np.float32)
    for t in range(8, 16):
        gm[t - 8, :, (t // 2):] = -1.0e30
    cf[:, 386:642] = gm.reshape(1, 256)
    cb = np.zeros((128, NCB), np.float32)
    cb[:, 0:128] = np.eye(128, dtype=np.float32)
    s = np.arange(128)[:, None]
    l = np.arange(128)[None, :]
    neg = np.where(s > l, NEGV, 0.0).astype(np.float32)
    cb[:, 128:640] = np.tile(neg, (1, 4))
    selg = np.zeros((128, 8192), np.float32)
    for j in range(32):
        selg[j, j * 128:(j + 1) * 128] = 1.0
        selg[32 + j, j * 128:(j + 1) * 128] = 1.0
    for g in range(8):
        for h in range(4):
            selg[4 * g + h, 4096 + g * 512 + h * 128:4096 + g * 512 + (h + 1) * 128] = 1.0
            selg[32 + 4 * g + h, 4096 + g * 512 + h * 128:4096 + g * 512 + (h + 1) * 128] = 1.0
    return cf, cb, selg


def _layout(inputs):
    f = lambda a: np.ascontiguousarray(np.asarray(a, dtype=np.float32))
    pp = np.zeros((128, NPP), np.float32)
    cw = f(inputs["ssd_conv_w"])[0]
    pp[:, CW0:CW0 + 128] = cw.reshape(4, 32, 128).transpose(2, 1, 0).reshape(128, 128)
    pp[:, CB0:CB0 + 32] = f(inputs["ssd_conv_b"])[0].reshape(32, 128).T
    fw = f(inputs["ffn_conv_w"])[0]
    pp[:, FW0:FW0 + 66] = fw.reshape(3, 22, 128).transpose(2, 1, 0).reshape(128, 66)
    pp[:, FB0:FB0 + 22] = f(inputs["ffn_conv_b"])[0].reshape(22, 128).T
    pp[:, BG0:BG0 + 16] = f(inputs["b_gate"])[0].reshape(16, 128).T
    pp[:, SN0:SN0 + 16] = f(inputs["ssd_norm_w"])[0].reshape(16, 128).T
    pp[:, DP0:DP0 + 16] = np.repeat(f(inputs["ssd_d"])[0], 64).reshape(16, 128).T
    pp[0:32, DTB] = f(inputs["ssd_dt_bias"])[0]
    pp[0:32, ALG] = f(inputs["ssd_a_log"])[0]
    pp[32:64, DTB] = f(inputs["ssd_dt_bias"])[0]
    pp[32:64, ALG] = f(inputs["ssd_a_log"])[0]
    nw = np.stack([f(inputs["norm1_w"])[0], f(inputs["norm2_w"])[0], f(inputs["norm_f_w"])])
    cf, cb, selg = _consts()
    base = {
        "w_in": f(inputs["w_in"])[0], "w_ssd": f(inputs["w_ssd_proj"])[0], "w_att": f(inputs["w_att_proj"])[0],
        "w_out": f(inputs["w_out"])[0], "w_up": f(inputs["w_ffn_up"])[0], "w_dn": f(inputs["w_ffn_down"])[0],
        "nw": nw, "pp": pp, "cf": cf, "cb": cb, "selg": selg,
    }
    return base


def build_program(nseq=2, dbg_names=(), upto=5):
    nc = bass.Bass("TRN2", target_bir_lowering=False)
    kb = KB(nc, nseq, dbg_names)
    kb.upto = upto
    kb.att_lvl = ATT_LVL
    kb.att_skip = ATT_SKIP
    kb.build()
    return nc, kb


def kernel(**inputs):
    n = 8
    x = np.ascontiguousarray(np.asarray(inputs["x"], dtype=np.float32))
    base = _layout(inputs)
    nseq = x.shape[0] // n
    nc, kb = build_program(nseq=nseq)
    in_maps = []
    for i in range(n):
        m = dict(base)
        m["x"] = np.ascontiguousarray(x[i * nseq:(i + 1) * nseq])
        in_maps.append(m)
    res = run_bass_kernel_spmd(nc, in_maps, core_ids=list(range(n)))
    return np.concatenate([np.asarray(r["out"], dtype=np.float32) for r in res.results], axis=0)
```

```python
import numpy as np
from contextlib import ExitStack
import concourse.bass as bass
import concourse.mybir as mybir
from concourse.bass_utils import run_bass_kernel_spmd

F32 = mybir.dt.float32
BF16 = mybir.dt.bfloat16
AF = mybir.ActivationFunctionType
ALU = mybir.AluOpType
AX = mybir.AxisListType

ENGS = ("pe", "act", "dve", "pool", "sp")
T = 2048
D = 1024
NT = 16
EPS = 1e-6
OZ, OX, OB, OC, ODT, OQ, OK_, OV, OG = 0, 2048, 4096, 5120, 6144, 6176, 7200, 8224, 9248
CW0, CB0, FW0, FB0, BG0, SN0, DP0, DTB, ALG, NPP = 0, 128, 160, 226, 248, 264, 280, 296, 297, 298
NCF = 642
NCB = 640
NEGV = -30000.0
DEBUG = {}


class Sched:
    def __init__(self, nc, n_dma_sems=32):
        self.nc = nc
        self.ops = {e: [] for e in ENGS}
        self.cnt = {e: 0 for e in ENGS}
        self.sem = {}
        self.last_w = {}
        self.readers = {}
        self.seen = {e: {} for e in ENGS}
        self.n_dma_sems = n_dma_sems
        self.dma_cnt = {}
        self.dma_rr = 0

    def open(self, stack):
        for e in ENGS:
            self.sem[e] = stack.enter_context(self.nc.semaphore("s_" + e))
        for i in range(self.n_dma_sems):
            nm = "dma%d" % i
            self.sem[nm] = stack.enter_context(self.nc.semaphore("s_" + nm))
            self.dma_cnt[nm] = 0

    def _deps(self, eng, reads, writes):
        deps = {}

        def add(tok, same_ok):
            if tok is None:
                return
            s, v = tok
            if s == eng and not same_ok:
                return
            if deps.get(s, 0) < v:
                deps[s] = v

        for k in reads:
            add(self.last_w.get(k), True)
        for k in writes:
            add(self.last_w.get(k), True)
            for t in self.readers.get(k, ()):
                add(t, False)
        out = []
        seen = self.seen[eng]
        for s, v in deps.items():
            if seen.get(s, 0) < v:
                seen[s] = v
                out.append((s, v))
        return out

    def _commit(self, tok, reads, writes):
        for k in writes:
            self.last_w[k] = tok
            self.readers[k] = []
        for k in reads:
            if k not in writes:
                self.readers.setdefault(k, []).append(tok)

    def op(self, eng, fn, reads=(), writes=()):
        waits = self._deps(eng, reads, writes)
        self.cnt[eng] += 1
        tok = (eng, self.cnt[eng])
        if eng == "pe":
            self.seen[eng][eng] = self.cnt[eng]
        self.ops[eng].append((waits, fn, (eng, 1)))
        self._commit(tok, reads, writes)
        return tok

    def dma(self, queue, fn, reads=(), writes=()):
        waits = self._deps(queue, reads, writes)
        name = "dma%d" % (self.dma_rr % self.n_dma_sems)
        self.dma_rr += 1
        self.dma_cnt[name] += 16
        tok = (name, self.dma_cnt[name])
        prev = self.dma_cnt[name] - 16
        if prev > 0 and self.seen[queue].get(name, 0) < prev:
            self.seen[queue][name] = prev
            waits = waits + [(name, prev)]
        self.ops[queue].append((waits, fn, (name, 16)))
        self._commit(tok, reads, writes)
        return tok

    def barrier(self):
        for e in ENGS:
            waits = []
            for s in ENGS:
                v = self.cnt[s]
                if v > 0 and self.seen[e].get(s, 0) < v:
                    self.seen[e][s] = v
                    waits.append((s, v))
            for nm, v in self.dma_cnt.items():
                if v > 0 and self.seen[e].get(nm, 0) < v:
                    self.seen[e][nm] = v
                    waits.append((nm, v))
            if waits:
                self.ops[e].append((waits, None, None))
        self.last_w = {}
        self.readers = {}

    def emit(self):
        nc = self.nc
        sem = self.sem

        def replay(e, name):
            for waits, fn, inc in self.ops[name]:
                for s, v in waits:
                    e.wait_ge(sem[s], v)
                if fn is None:
                    continue
                ins = fn(e)
                ins.then_inc(sem[inc[0]], inc[1])

        with nc.Block() as block:
            @block.tensor
            def _(e):
                replay(e, "pe")

            @block.scalar
            def _(e):
                replay(e, "act")

            @block.vector
            def _(e):
                replay(e, "dve")

            @block.gpsimd
            def _(e):
                replay(e, "pool")

            @block.sync
            def _(e):
                replay(e, "sp")


class PhasesA:
    def phase_norm1(self, sq):
        nc = self.nc
        with ExitStack() as ph:
            sb = lambda n, s, d: ph.enter_context(nc.sbuf_tensor("%s_%d" % (n, sq), s, d))
            nwb = sb("n1wb", [128, D], F32)
            xt = [sb("n1x%d" % i, [128, D], F32) for i in range(3)]
            xs = [sb("n1s%d" % i, [128, D], BF16) for i in range(3)]
            junk = sb("n1j", [128, D], BF16)
            ss = sb("n1ss", [128, 16], F32)
            r1 = sb("n1r1", [128, 16], F32)
            rstd = sb("n1rs", [128, 16], F32)
            self.dma("sp", nwb[:], self.nw[0:1, :].partition_broadcast(128), [], ["n1wb"])
            for t in range(NT):
                b = t % 3
                self.dma("sp", xt[b][:], self.x[sq, t * 128:(t + 1) * 128, :], [], [("n1x", b)])
                self.act(junk[:], xt[b][:], AF.Square, [("n1x", b)], ["n1j", ("n1ss", t)], accum=ss[:, t:t + 1])
                self.act(r1[:, t:t + 1], ss[:, t:t + 1], AF.Sqrt, [("n1ss", t)], [("n1r1", t)], bias=self.epsc[:, 0:1], scale=1.0 / D)
                self.recip(rstd[:, t:t + 1], r1[:, t:t + 1], [("n1r1", t)], [("n1rs", t)])
                self.stt(xs[b][:], xt[b][:], rstd[:, t:t + 1], nwb[:], ALU.mult, ALU.mult,
                         [("n1x", b), ("n1rs", t), "n1wb"], [("n1s", b)])
                pbk = t % 4
                pb = self.bankb(pbk)
                for j in range(8):
                    self.tr(pb[:, j * 128:(j + 1) * 128], xs[b][:, j * 128:(j + 1) * 128], self.ident,
                            [("n1s", b), "cb"], ["ps%d" % pbk])
                self.evac(self.hnT[:, :, t * 128:(t + 1) * 128], pb.rearrange("p (k c) -> p k c", k=8),
                          ["ps%d" % pbk], ["hnT"])
            self.dbg("hnT", self.hnT[:], sq, ["hnT"])

    def phase_dt_ssd(self, sq):
        nc = self.nc
        S = self.S
        with ExitStack() as ph:
            sb = lambda n, s, d: ph.enter_context(nc.sbuf_tensor("%s_%d" % (n, sq), s, d))
            selg = sb("selg", [128, 8192], BF16)
            self.dma("pool", selg[:], self.selg_d, [], ["selg"])
            cs_cat = sb("cs_cat", [128, T], BF16)
            f_cat = sb("f_cat", [128, T], BF16)
            self.memset(cs_cat[:], 0.0, [], ["cscat"])
            self.memset(f_cat[:], 0.0, [], ["fcat"])
            ecs_tok = sb("ecs_tok", [128, 16, 32], F32)
            w_tok = sb("w_tok", [128, 16, 32], F32)
            dec_rep = sb("dec_rep", [128, 16, 32], F32)
            with ExitStack() as t1s:
                sb1 = lambda n, s, d: t1s.enter_context(nc.sbuf_tensor("%s_%d" % (n, sq), s, d))
                A = [sb1("dtA%d" % i, [64, T], F32) for i in range(4)]
                mask = sb1("dtmask", [64, T], F32)
                hi64 = sb1("dthi64", [64, T], BF16)
                dec = sb1("dtdec", [32, 16], F32)
                decx = sb1("dtdecx", [32, 16, 32], F32)
                negA = sb1("dtnegA", [64, 1], F32)
                wdt = sb1("wdt", [128, 8, 64], BF16)
                self.dma("pool", wdt[:, :, 0:32], self.w_in_r[:, :, ODT:ODT + 32], [], ["wdt"])
                self.dma("pool", wdt[:, :, 32:64], self.w_in_r[:, :, ODT:ODT + 32], [], ["wdt"])
                for tb in range(4):
                    for k in range(8):
                        self.mm(self.bank(tb)[0:64, :], wdt[:, k, :], self.hnT[:, k, tb * 512:(tb + 1) * 512],
                                k == 0, k == 7, ["wdt", "hnT"], ["ps%d" % tb])
                    self.act(A[0][:, tb * 512:(tb + 1) * 512], self.bank(tb)[0:64, :], AF.Exp, ["ps%d" % tb, "pp"], ["A0"],
                             bias=self.pp[0:64, DTB:DTB + 1])
                self.act(A[1][:], A[0][:], AF.Ln, ["A0"], ["A1"], bias=self.onec[0:64, 0:1])
                self.act(A[2][:], A[1][:], AF.Ln, ["A1"], ["A2"])
                self.act(negA[:], self.pp[0:64, ALG:ALG + 1], AF.Exp, ["pp"], ["negA"])
                self.ts(A[0][:], A[1][:], negA[:, 0:1], -1.0, ALU.mult, ALU.mult, ["A1", "negA", "A0"], ["A0"])
                self.memset(mask[:], 1.0, [], ["mask"])
                self.memset(mask[:].rearrange("p (c l) -> p c l", l=128)[:, :, 0:1], 0.0, ["mask"], ["mask"])
                S.op("dve", lambda e: e.tensor_tensor_scan(out=A[3][:], data0=mask[:], data1=A[0][:], initial=0.0,
                                                           op0=ALU.mult, op1=ALU.add), reads=["mask", "A0"], writes=["A3"])
                self.tt(A[2][:], A[2][:], A[3][:], ALU.subtract, ["A2", "A3"], ["A2"])
                self.act(A[1][:], A[3][:], AF.Exp, ["A3", "A1"], ["A1"])
                A3v = A[3][:].rearrange("p (c l) -> p c l", l=128)
                self.tt(A[0][:].rearrange("p (c l) -> p c l", l=128), A[2][:].rearrange("p (c l) -> p c l", l=128),
                        A3v[:, :, 127:128].to_broadcast([64, 16, 128]), ALU.add, ["A2", "A3", "A0"], ["A0"])
                self.act(A[0][:], A[0][:], AF.Exp, ["A0"], ["A0"])
                self.act(dec[:], A3v[0:32, :, 127], AF.Exp, ["A3"], ["dec"])
                for (src, dst, nm) in ((A[3], cs_cat, "cscat"), (A[2], f_cat, "fcat")):
                    self.vcopy(hi64[:], src[:], ["A3", "A2", "hi64"], ["hi64"])
                    self.vcopy(dst[0:32, :], hi64[0:32, :], ["hi64"], [nm])
                    self.tt(dst[32:64, :], src[32:64, :], hi64[32:64, :], ALU.subtract, ["A3", "A2", "hi64"], [nm])
                id32 = self.ident_f[0:32, 0:32]
                for c in range(16):
                    self.tr(self.bank(4)[:, c * 32:(c + 1) * 32], A[1][0:32, c * 128:(c + 1) * 128], id32, ["A1", "cf"], ["ps4"])
                    self.tr(self.bank(5)[:, c * 32:(c + 1) * 32], A[0][0:32, c * 128:(c + 1) * 128], id32, ["A0", "cf"], ["ps5"])
                self.vcopy(ecs_tok[:].rearrange("p c h -> p (c h)"), self.bank(4), ["ps4"], ["ecs_tok"])
                self.vcopy(w_tok[:].rearrange("p c h -> p (c h)"), self.bank(5), ["ps5"], ["w_tok"])
                self.tt(decx[:], dec[:].unsqueeze(2).to_broadcast([32, 16, 32]),
                        id32.unsqueeze(1).to_broadcast([32, 16, 32]), ALU.mult, ["dec", "cf"], ["decx"])
                self.mm(self.bank(6), self.ones_f[0:32, :], decx[:].rearrange("p c h -> p (c h)"), True, True,
                        ["ones_f", "decx"], ["ps6"])
                self.vcopy(dec_rep[:].rearrange("p c h -> p (c h)"), self.bank(6), ["ps6"], ["dec_rep"])
                self.dbg("ecs_tok", ecs_tok[:], sq, ["ecs_tok"])
                self.dbg("w_tok", w_tok[:], sq, ["w_tok"])
                self.dbg("dec_rep", dec_rep[:], sq, ["dec_rep"])
                S.barrier()
            with ExitStack() as gs:
                sbg = lambda n, s, d: gs.enter_context(nc.sbuf_tensor("%s_%d" % (n, sq), s, d))
                wx = [sbg("wx%d" % i, [128, 8, 512], BF16) for i in range(2)]
                wz = [sbg("wz%d" % i, [128, 8, 256], BF16) for i in range(2)]
                raw = [sbg("raw%d" % i, [128, 3 + T], BF16) for i in range(2)]
                xc = sbg("xc", [128, 4, T], BF16)
                xbt = sbg("xbt", [128, 16, 384], BF16)
                yT = sbg("yT", [128, 2, T], BF16)
                diag = sbg("diag", [128, 16, 128], BF16)
                L2 = [sbg("L2_%d" % i, [128, 512], F32) for i in range(2)]
                M2 = [sbg("M2_%d" % i, [128, 512], BF16) for i in range(2)]
                Xw_all = sbg("Xw_all", [128, 16, 256], BF16)
                t1 = sbg("t1", [128, 256], F32)
                ych = [sbg("ych%d" % i, [128, 256], F32) for i in range(2)]
                Sst = sbg("Sst", [128, 256], F32)
                Sd = sbg("Sd", [128, 256], F32)
                STs = sbg("STs", [128, 256], F32)
                Sbf = sbg("Sbf", [128, 256], BF16)
                sz = [sbg("sz%d" % i, [128, 512], BF16) for i in range(2)]
                sq_ = [sbg("sqq%d" % i, [128, 512], BF16) for i in range(2)]
                rr = [sbg("rr%d" % i, [128, 512], F32) for i in range(4)]
                for i in range(2):
                    self.memset(raw[i][:, 0:3], 0.0, [], [("raw", i)])

                def load_w(g):
                    wb = g % 2
                    self.dma("pool", wx[wb][:, :, 0:256], self.w_in_r[:, :, OX + g * 256:OX + (g + 1) * 256], [], [("wx", wb, 0)])
                    self.dma("pool", wx[wb][:, :, 256:384], self.w_in_r[:, :, OB + g * 128:OB + (g + 1) * 128], [], [("wx", wb, 1)])
                    self.dma("pool", wx[wb][:, :, 384:512], self.w_in_r[:, :, OC + g * 128:OC + (g + 1) * 128], [], [("wx", wb, 2)])
                    self.dma("pool", wz[wb][:], self.w_in_r[:, :, OZ + g * 256:OZ + (g + 1) * 256], [], [("wz", wb)])

                load_w(0)
                load_w(1)
                def prologue_a(g):
                    wb = g % 2
                    chs = [2 * g, 2 * g + 1, 16 + g, 24 + g]
                    for cc in range(4):
                        for k in range(4):
                            col = CW0 + chs[cc] * 4 + k
                            self.S.op("act", (lambda o_, c_: lambda e: e.mul(out=o_, in_=self.ident, mul=self.pp[:, c_:c_ + 1]))(diag[:, cc * 4 + k, :], col),
                                      reads=["cb", "pp"], writes=[("diag", cc)])
                    for cc in range(4):
                        rb = cc % 2
                        ch = chs[cc]
                        for tb in range(4):
                            bk = tb
                            for k in range(8):
                                self.mm(self.bank(bk), wx[wb][:, k, cc * 128:(cc + 1) * 128],
                                        self.hnT[:, k, tb * 512:(tb + 1) * 512], k == 0, k == 7,
                                        [("wx", wb, max(0, cc - 1)), "hnT"], ["ps%d" % bk])
                            self.evac(raw[rb][:, 3 + tb * 512:3 + (tb + 1) * 512], self.bank(bk), ["ps%d" % bk], [("raw", rb)])
                        for tb in range(4):
                            bk = 4 + tb
                            for k in range(4):
                                self.mm(self.bank(bk), diag[:, cc * 4 + k, :], raw[rb][:, tb * 512 + k:tb * 512 + k + 512],
                                        k == 0, k == 3, [("diag", cc), ("raw", rb)], ["ps%d" % bk])
                            self.act(xc[:, cc, tb * 512:(tb + 1) * 512], self.bank(bk), AF.Silu, ["ps%d" % bk, "pp"],
                                     [("xc", cc)], bias=self.pp[:, CB0 + ch:CB0 + ch + 1])
                    if g == 0:
                        self.dbg("xc0", xc[:], sq, [("xc", i) for i in range(4)])
                def prologue_b(g):
                    wb = g % 2
                    chs = [2 * g, 2 * g + 1, 16 + g, 24 + g]
                    for c in range(16):
                        bk = c % 4
                        pb = self.bankb(bk)
                        for cc in range(3):
                            self.tr(pb[:, cc * 128:(cc + 1) * 128], xc[:, cc, c * 128:(c + 1) * 128], self.ident,
                                    [("xc", cc), "cb"], ["ps%d" % bk])
                        self.evac(xbt[:, c, :], pb[:, 0:384], ["ps%d" % bk], [("xbt", c // 4)])
                        if c % 4 == 3:
                            q4 = c // 4
                            self.tt(Xw_all[:, 4 * q4:4 * q4 + 4, :].rearrange("p c (h d) -> p c h d", h=4),
                                    xbt[:, 4 * q4:4 * q4 + 4, 0:256].rearrange("p c (h d) -> p c h d", h=4),
                                    w_tok[:, 4 * q4:4 * q4 + 4, 4 * g:4 * g + 4].unsqueeze(3).to_broadcast([128, 4, 4, 64]), ALU.mult,
                                    [("xbt", q4), "w_tok"], [("Xw", q4)], eng="pool")
                def chunkloop(g):
                    wb = g % 2
                    chs = [2 * g, 2 * g + 1, 16 + g, 24 + g]
                    def front(c):
                        p2 = c % 2
                        Db = self.bank(p2)
                        CBb = self.bank(2 + p2)
                        cs = slice(c * 128, (c + 1) * 128)
                        kD = "ps%d" % p2
                        kC = "ps%d" % (2 + p2)
                        self.mm(Db, self.ident, self.neg4, True, False, ["cb"], [kD])
                        for h in range(4):
                            j = 4 * g + h
                            self.mm(Db[:, h * 128:(h + 1) * 128], selg[:, j * 128:(j + 1) * 128], cs_cat[:, cs],
                                    False, False, ["selg", "cscat"], [kD])
                        self.mm(Db, f_cat[:, cs], selg[:, 4096 + g * 512:4096 + (g + 1) * 512], False, True, ["selg", "fcat"], [kD])
                        self.mm(CBb[:, 0:128], xc[:, 2, cs], xc[:, 3, cs], True, True, [("xc", 2), ("xc", 3)], [kC])
                        self.act(L2[p2][:], Db, AF.Exp, [kD], [("L2", p2)])
                        self.tt(M2[p2][:].rearrange("p (h l) -> p h l", h=4), L2[p2][:].rearrange("p (h l) -> p h l", h=4),
                                CBb[:, 0:128].unsqueeze(1).to_broadcast([128, 4, 128]), ALU.mult,
                                [("L2", p2), kC], [("M2", p2)])

                    def mid(c):
                        p2 = c % 2
                        cs = slice(c * 128, (c + 1) * 128)
                        Y1 = self.bank(4)
                        Y2 = self.bank(5)
                        STb = self.bank(6)
                        for h in range(4):
                            self.mm(Y1[:, h * 64:(h + 1) * 64], M2[p2][:, h * 128:(h + 1) * 128], xbt[:, c, h * 64:(h + 1) * 64],
                                    True, True, [("M2", p2), ("xbt", c // 4)], ["ps4"])
                        if c > 0:
                            self.mm(Y2[:, 0:256], xc[:, 3, cs], Sbf[:], True, True, [("xc", 3), "Sbf"], ["ps5"])
                        if c < 15:
                            self.mm(STb[:, 0:256], xbt[:, c, 256:384], Xw_all[:, c, :], True, True, [("xbt", c // 4), ("Xw", c // 4)], ["ps6"])
                        if c > 0:
                            self.tt(t1[:].rearrange("p (h d) -> p h d", h=4), Y2[:, 0:256].rearrange("p (h d) -> p h d", h=4),
                                    ecs_tok[:, c, 4 * g:4 * g + 4].unsqueeze(2).to_broadcast([128, 4, 64]), ALU.mult,
                                    ["ps5", "ecs_tok"], ["t1"])
                            self.tt(ych[p2][:], t1[:], Y1[:, 0:256], ALU.add, ["t1", "ps4"], [("ych", p2)])
                        else:
                            self.vcopy(ych[p2][:], Y1[:, 0:256], ["ps4"], [("ych", p2)])
                        if c < 15:
                            if c == 0:
                                self.acopy(Sst[:], STb[:, 0:256], ["ps6"], ["Sst"])
                            else:
                                self.acopy(STs[:], STb[:, 0:256], ["ps6"], ["STs"])
                                self.tt(Sd[:].rearrange("p (h d) -> p h d", h=4), Sst[:].rearrange("p (h d) -> p h d", h=4),
                                        dec_rep[:, c, 4 * g:4 * g + 4].unsqueeze(2).to_broadcast([128, 4, 64]), ALU.mult,
                                        ["Sst", "dec_rep"], ["Sd"], eng="pool")
                                self.tt(Sst[:], Sd[:], STs[:], ALU.add, ["Sd", "STs"], ["Sst"], eng="pool")
                            self.acopy(Sbf[:], Sst[:], ["Sst"], ["Sbf"])

                    def back(c):
                        p2 = c % 2
                        cs = slice(c * 128, (c + 1) * 128)
                        YT = self.bank(7)
                        for cc in range(2):
                            self.tr(YT[:, cc * 128:(cc + 1) * 128], ych[p2][:, cc * 128:(cc + 1) * 128], self.ident_f,
                                    [("ych", p2), "cf"], ["ps7"])
                        for cc in range(2):
                            ch = 2 * g + cc
                            self.stt(yT[:, cc, cs], xc[:, cc, cs], self.pp[:, DP0 + ch:DP0 + ch + 1], YT[:, cc * 128:(cc + 1) * 128],
                                     ALU.mult, ALU.add, [("xc", cc), "pp", "ps7"], [("yT", cc)])

                    front(0)
                    for c in range(16):
                        if c + 1 < 16:
                            front(c + 1)
                        mid(c)
                        if c >= 1:
                            back(c - 1)
                    back(15)
                    if g == 0:
                        self.dbg("yT0", yT[:], sq, [("yT", 0), ("yT", 1)])
                def zgate(g):
                    wb = g % 2
                    chs = [2 * g, 2 * g + 1, 16 + g, 24 + g]
                    for tb in range(4):
                        ts_ = slice(tb * 512, (tb + 1) * 512)
                        for cc in range(2):
                            bk = (tb * 2 + cc) % 4
                            i2 = (tb * 2 + cc) % 2
                            for k in range(8):
                                self.mm(self.bank(bk), wz[wb][:, k, cc * 128:(cc + 1) * 128], self.hnT[:, k, ts_], k == 0, k == 7,
                                        [("wz", wb), "hnT"], ["ps%d" % bk])
                            self.act(sz[i2][:], self.bank(bk), AF.Silu, ["ps%d" % bk], [("sz", i2)])
                            self.tt(yT[:, cc, ts_], yT[:, cc, ts_], sz[i2][:], ALU.mult, [("yT", cc), ("sz", i2)], [("yT", cc)], eng="pool")
                sqbufs = [(sq_[0][:], ("sq", 0)), (sq_[1][:], ("sq", 1)), (M2[0][:], ("M2", 0)), (M2[1][:], ("M2", 1)),
                          (sz[0][:], ("sz", 0)), (sz[1][:], ("sz", 1)),
                          (ych[0][:].bitcast(BF16), ("ych", 0)), (ych[1][:].bitcast(BF16), ("ych", 1))]

                def normpart(g):
                    wb = g % 2
                    chs = [2 * g, 2 * g + 1, 16 + g, 24 + g]
                    for tb in range(4):
                        ts_ = slice(tb * 512, (tb + 1) * 512)
                        bk = 4 + tb
                        for cc in range(2):
                            sqb, sqk = sqbufs[tb * 2 + cc]
                            self.tt(sqb, yT[:, cc, ts_], yT[:, cc, ts_], ALU.mult, [("yT", cc)], [sqk], eng="pool")
                            self.mm(self.bank(bk), self.ones_bf[:], sqb, cc == 0, cc == 1, ["ones_bf", sqk], ["ps%d" % bk])
                    for tb in range(4):
                        ts_ = slice(tb * 512, (tb + 1) * 512)
                        self.act(rr[tb][:], self.bank(4 + tb), AF.Ln, ["ps%d" % (4 + tb)], [("rr", tb)], bias=self.epsc[:, 0:1], scale=1.0 / 256)
                        self.act(rr[tb][:], rr[tb][:], AF.Exp, [("rr", tb)], [("rr", tb)], scale=-0.5)
                        for cc in range(2):
                            ch = 2 * g + cc
                            self.stt(yT[:, cc, ts_], yT[:, cc, ts_], self.pp[:, SN0 + ch:SN0 + ch + 1], rr[tb][:], ALU.mult, ALU.mult,
                                     [("yT", cc), "pp", ("rr", tb)], [("yT", cc)])
                    if g == 0:
                        self.dbg("yn0", yT[:], sq, [("yT", 0), ("yT", 1)])
                    for cc in range(2):
                        self.dma("sp", self.yn_d[2 * g + cc], yT[:, cc, :], [("yT", cc)], [("yn_d", 2 * g + cc)])
                prologue_a(0)
                prologue_b(0)
                for g in range(8):
                    chunkloop(g)
                    zgate(g)
                    if g + 1 < 8:
                        prologue_a(g + 1)
                    if g + 2 < 8:
                        load_w(g + 2)
                    normpart(g)
                    if g + 1 < 8:
                        prologue_b(g + 1)
                S.barrier()

    def phase_ssdproj(self, sq):
        nc = self.nc
        with ExitStack() as ph:
            sb = lambda n, s, d: ph.enter_context(nc.sbuf_tensor("%s_%d" % (n, sq), s, d))
            ynT = sb("ynT", [128, 16, T], BF16)
            wsp = [sb("wsp%d" % i, [128, 16, 512], BF16) for i in range(2)]
            wg = [sb("wgs%d" % i, [128, 8, 512], BF16) for i in range(2)]
            gsb = [sb("gsb%d" % i, [128, 512], BF16) for i in range(2)]
            for j in range(16):
                self.dma("sp", ynT[:, j, :], self.yn_d[j], [], [("ynT", j)])
            w_ssd_r = self.w_ssd.rearrange("(k p) n -> p k n", p=128)
            for dh in range(2):
                self.dma("pool", wsp[dh][:, 0:8, :], w_ssd_r[:, 0:8, dh * 512:(dh + 1) * 512], [], [("wsp", dh)])
                self.dma("pool", wsp[dh][:, 8:16, :], w_ssd_r[:, 8:16, dh * 512:(dh + 1) * 512], [], [("wsp", dh)])
                self.dma("pool", wg[dh][:], self.w_in_r[:, :, OG + dh * 512:OG + (dh + 1) * 512], [], [("wgs", dh)])
            it = 0
            for dh in range(2):
                for j in range(4):
                    dch = dh * 4 + j
                    for tb in range(4):
                        ts_ = slice(tb * 512, (tb + 1) * 512)
                        bP = (it % 4) * 2
                        bG = bP + 1
                        i2 = it % 2
                        it += 1
                        if dh == 0 and j == 0:
                            if tb == 0:
                                for tb2 in range(4):
                                    ts2 = slice(tb2 * 512, (tb2 + 1) * 512)
                                    for k in range(8):
                                        self.mm(self.bank(tb2 * 2 + 1), wg[dh][:, k, 0:128], self.hnT[:, k, ts2], k == 0, k == 7,
                                                [("wgs", dh), "hnT"], ["ps%d" % (tb2 * 2 + 1)])
                                for k in range(16):
                                    for tb2 in range(4):
                                        ts2 = slice(tb2 * 512, (tb2 + 1) * 512)
                                        self.mm(self.bank(tb2 * 2), wsp[dh][:, k, 0:128], ynT[:, k, ts2], k == 0, k == 15,
                                                [("wsp", dh), ("ynT", k)], ["ps%d" % (tb2 * 2)])
                        else:
                            for k in range(16):
                                self.mm(self.bank(bP), wsp[dh][:, k, j * 128:(j + 1) * 128], ynT[:, k, ts_], k == 0, k == 15,
                                        [("wsp", dh), ("ynT", k)], ["ps%d" % bP])
                            for k in range(8):
                                self.mm(self.bank(bG), wg[dh][:, k, j * 128:(j + 1) * 128], self.hnT[:, k, ts_], k == 0, k == 7,
                                        [("wgs", dh), "hnT"], ["ps%d" % bG])
                        self.act(gsb[i2][:], self.bank(bG), AF.Sigmoid, ["ps%d" % bG, "pp"], [("gsb", i2)],
                                 bias=self.pp[:, BG0 + dch:BG0 + dch + 1])
                        self.tt(self.m1[:, dch, ts_], gsb[i2][:], self.bank(bP), ALU.mult, [("gsb", i2), "ps%d" % bP], ["m1"])
            self.dbg("m1s", self.m1[:], sq, ["m1"])


class PhasesB:
    def phase_att(self, sq):
        nc = self.nc
        S = self.S
        with ExitStack() as ph:
            sb = lambda n, s, d: ph.enter_context(nc.sbuf_tensor("%s_%d" % (n, sq), s, d))
            yattT = sb("yattT", [128, 8, T], BF16)
            with ExitStack() as a1:
                sba = lambda n, s, d: a1.enter_context(nc.sbuf_tensor("%s_%d" % (n, sq), s, d))
                wqk = [sba("wqk%d" % i, [128, 8, 512], BF16) for i in range(2)]
                wv = [sba("wv%d" % i, [128, 8, 256], BF16) for i in range(2)]
                qk_tok = sba("qk_tok", [128, 16, 512], BF16)
                v_aug = sba("v_aug", [128, 16, 4, 66], BF16)
                qTm = sba("qTm", [128, 2, 2, T], BF16)
                kT = sba("kT", [128, 2, T], BF16)
                kms = sba("kms", [128, 2, 8], F32)
                km_hi = sba("km_hi", [128, 2, 8], BF16)
                km_lo = sba("km_lo", [128, 2, 8], BF16)
                PT = [sba("PT%d" % i, [128, 512], BF16) for i in range(5)]
                yat = sba("yat", [128, 16, 256], BF16)
                g8all = sba("g8all", [128, 256], F32)
                m8all = sba("m8all", [128, 32, 8], F32)
                selA = sba("selA", [128, 16, 4, 8], F32)
                tmp = [sba("atmp%d" % i, [128, 7 * 65], F32) for i in range(2)]
                red = [sba("ared%d" % i, [128, 65], F32) for i in range(2)]
                tot = [sba("atot%d" % i, [128, 65], F32) for i in range(2)]
                rinv = [sba("arinv%d" % i, [128, 1], F32) for i in range(2)]
                rt = [sba("rt%d" % i, [128, 8, 8], F32) for i in range(4)]
                self.memset(v_aug[:].rearrange("p a b c -> p (a b c)"), 1.0, [], ["v_aug"])

                def load_w(hg):
                    wb = hg % 2
                    self.dma("pool", wqk[wb][:, :, 0:256], self.w_in_r[:, :, OQ + hg * 256:OQ + (hg + 1) * 256], [], [("wqk", wb)])
                    self.dma("pool", wqk[wb][:, :, 256:512], self.w_in_r[:, :, OK_ + hg * 256:OK_ + (hg + 1) * 256], [], [("wqk", wb)])
                    self.dma("pool", wv[wb][:], self.w_in_r[:, :, OV + hg * 256:OV + (hg + 1) * 256], [], [("wv", wb)])

                load_w(0)
                pti = 0
                sri = 0
                aci = 0
                for hg in range(4):
                    wb = hg % 2
                    if hg + 1 < 4:
                        load_w(hg + 1)
                    def emit_tr(t):
                        tl = slice(t * 128, (t + 1) * 128)
                        bk = 4 + t % 4
                        pb = self.bankb(bk)
                        for i in range(4):
                            self.tr(pb[:, i * 128:(i + 1) * 128], qk_tok[:, t, i * 128:(i + 1) * 128], self.ident,
                                    [("qk_tok", t), "cb"], ["ps%d" % bk])
                        qsrc = pb[:, 0:256].rearrange("p (c l) -> p c l", c=2)
                        ksrc = pb[:, 256:512].rearrange("p (c l) -> p c l", c=2)
                        if bk in (4, 5):
                            for par in range(2):
                                self.S.op("act", (lambda o_, i_, c_: lambda e: e.mul(out=o_, in_=i_, mul=self.cf[:, c_:c_ + 1]))(qTm[:, par, :, tl], qsrc, 384 + par),
                                          reads=["ps%d" % bk, "cf"], writes=[("qT", par)])
                            self.acopy(kT[:, :, tl], ksrc, ["ps%d" % bk], ["kT"])
                        else:
                            for par in range(2):
                                self.ts(qTm[:, par, :, tl], qsrc, self.cf[:, 384 + par:385 + par], None, ALU.mult, None, ["ps%d" % bk, "cf"], [("qT", par)])
                            self.vcopy(kT[:, :, tl], ksrc, ["ps%d" % bk], ["kT"])
                    for t in range(NT):
                        tl = slice(t * 128, (t + 1) * 128)
                        bq = (t % 2) * 2
                        bv = bq + 1
                        for k in range(8):
                            self.mm(self.bank(bq), self.hnT[:, k, tl], wqk[wb][:, k, :], k == 0, k == 7, ["hnT", ("wqk", wb)], ["ps%d" % bq])
                        for k in range(8):
                            self.mm(self.bank(bv)[:, 0:256], self.hnT[:, k, tl], wv[wb][:, k, :], k == 0, k == 7, ["hnT", ("wv", wb)], ["ps%d" % bv])
                        Pv = self.bank(bq).rearrange("p (h d) -> p h d", h=8)
                        O = qk_tok[:, t, :].rearrange("p (h d) -> p h d", h=8)
                        cosb = self.cf[:, t * 8:(t + 1) * 8].unsqueeze(1).to_broadcast([128, 8, 8])
                        sinb = self.cf[:, 128 + t * 8:128 + (t + 1) * 8].unsqueeze(1).to_broadcast([128, 8, 8])
                        kq = "ps%d" % bq
                        self.vcopy(O[:, :, 16:64], Pv[:, :, 16:64], [kq], [("qk_tok", t)])
                        self.tt(rt[0][:], Pv[:, :, 0:8], cosb, ALU.mult, [kq, "cf"], ["rt0"])
                        self.tt(rt[1][:], Pv[:, :, 8:16], sinb, ALU.mult, [kq, "cf"], ["rt1"])
                        self.tt(rt[2][:], Pv[:, :, 8:16], cosb, ALU.mult, [kq, "cf"], ["rt2"])
                        self.tt(rt[3][:], Pv[:, :, 0:8], sinb, ALU.mult, [kq, "cf"], ["rt3"])
                        self.tt(O[:, :, 0:8], rt[0][:], rt[1][:], ALU.subtract, ["rt0", "rt1"], [("qk_tok", t)])
                        self.tt(O[:, :, 8:16], rt[2][:], rt[3][:], ALU.add, ["rt2", "rt3"], [("qk_tok", t)])
                        self.acopy(v_aug[:, t, :, 0:64], self.bank(bv)[:, 0:256].rearrange("p (h d) -> p h d", h=4),
                                   ["ps%d" % bv], ["v_aug"])
                        if t >= 1:
                            emit_tr(t - 1)
                    emit_tr(NT - 1)
                    if self.att_lvl == -1:
                        S.barrier()
                        return
                    if self.att_lvl == -2:
                        continue
                    if hg == 0:
                        self.dbg("kT0", kT[:], sq, ["kT"])
                    if "K" not in self.att_skip:
                        S.op("dve", lambda e: e.tensor_reduce(out=kms[:], in_=kT[:].rearrange("p c (n j) -> p c n j", j=256),
                                                              axis=AX.X, op=ALU.add), reads=["kT"], writes=["kms"])
                        self.vcopy(km_hi[:], kms[:], ["kms"], ["km_hi"])
                        self.tt(km_lo[:], kms[:], km_hi[:], ALU.subtract, ["kms", "km_hi"], ["km_lo"])
                    if self.att_lvl >= 1:
                        Gps = self.bank(7)
                        first = True
                        for t in range(8, NT):
                            tl = slice(t * 128, (t + 1) * 128)
                            for h in range(4):
                                c = h // 2
                                o_ = Gps[:, (t - 8) * 32 + h * 8:(t - 8) * 32 + (h + 1) * 8]
                                self.mm(o_, qTm[:, h % 2, c, tl], km_hi[:, c, :], first, False, [("qT", h % 2), "km_hi"], ["ps7"])
                                first = False
                                self.mm(o_, qTm[:, h % 2, c, tl], km_lo[:, c, :], False, (t == NT - 1 and h == 3), [("qT", h % 2), "km_lo"], ["ps7"])
                        self.tt(g8all[:], Gps[:, 0:256], self.cf[:, 386:642], ALU.add, ["ps7", "cf"], ["g8"])
                        for idx in range(32):
                            S.op("dve", (lambda i_: (lambda e: e.max(out=m8all[:, i_, :], in_=g8all[:, i_ * 8:(i_ + 1) * 8])))(idx), reads=["g8"], writes=["m8"])
                        self.tt(selA[:, 8:16, :, :].rearrange("p t h n -> p (t h) n"), g8all[:].rearrange("p (g n) -> p g n", n=8),
                                m8all[:, :, 2:3].to_broadcast([128, 32, 8]), ALU.is_ge, ["g8", "m8"], ["selA"])
                    if hg == 0:
                        self.dbg("selA", selA[:], sq, ["selA"])
                    chunks_l = []
                    t_order = []
                    for i_ in range(NT // 2):
                        t_order += [i_, NT - 1 - i_]
                    for h in (range(4) if self.att_lvl >= 2 else []):
                        for t in t_order:
                            nk = t + 1
                            cl = [(k0_, min(k0_ + 4, nk)) for k0_ in range(0, nk, 4)]
                            for ci, (k0, k1) in enumerate(cl):
                                chunks_l.append((h, t, k0, k1, ci == len(cl) - 1))
                    unit_ai = {}

                    def emit_qk(idx):
                        h, t, k0, k1, last = chunks_l[idx]
                        c = h // 2
                        tl = slice(t * 128, (t + 1) * 128)
                        si = idx % 4
                        pi = idx % 5
                        sreg = self.bank(si)
                        ks = ["ps%d" % si]
                        for kt in range(k0, k1):
                            col = (kt - k0) * 128
                            self.mm(sreg[:, col:col + 128], kT[:, c, kt * 128:(kt + 1) * 128], qTm[:, h % 2, c, tl],
                                    True, kt != t, ["kT", ("qT", h % 2)], ks)
                            if kt == t:
                                self.mm(sreg[:, col:col + 128], self.ident, self.neg4[:, 0:128], False, True, ["cb"], ks)
                        n = (k1 - k0) * 128
                        self.act(PT[pi][:, 0:n], sreg[:, 0:n], AF.Exp, ks, [("PT", pi)], scale=0.125)

                    def emit_pv(idx):
                        h, t, k0, k1, last = chunks_l[idx]
                        blk = t // 2
                        pi = idx % 5
                        if (h, t) not in unit_ai:
                            unit_ai[(h, t)] = len(unit_ai) % 2
                        ai = unit_ai[(h, t)]
                        acc0 = self.bank(4 + 2 * ai)
                        acc1 = self.bank(5 + 2 * ai)
                        ka0 = "ps%d" % (4 + 2 * ai)
                        ka1 = "ps%d" % (5 + 2 * ai)
                        for kt in range(k0, k1):
                            col = (kt - k0) * 128
                            if blk <= 3:
                                o, st_, sp_, kk = acc0[:, 0:65], kt == 0, kt == t, ka0
                            else:
                                n_ = kt // 2
                                if n_ < blk:
                                    o, st_, sp_, kk = acc0[:, n_ * 65:(n_ + 1) * 65], kt % 2 == 0, kt % 2 == 1, ka0
                                else:
                                    o, st_, sp_, kk = acc1[:, 0:65], kt == 2 * blk, kt == t, ka1
                            self.mm(o, PT[pi][:, col:col + 128], v_aug[:, kt, h, 0:65], st_, sp_, [("PT", pi), "v_aug"], [kk])
                        if not last:
                            return
                        yo = yat[:, t, h * 64:(h + 1) * 64]
                        if blk <= 3:
                            self.recip(rinv[ai][:], acc0[:, 64:65], [ka0], [("rinv", ai)])
                            self.ts(yo, acc0[:, 0:64], rinv[ai][:, 0:1], None, ALU.mult, None, [ka0, ("rinv", ai)], ["yat"])
                        else:
                            self.tt(tmp[ai][:, 0:blk * 65].rearrange("p (n d) -> p n d", n=blk),
                                    acc0[:, 0:blk * 65].rearrange("p (n d) -> p n d", n=blk),
                                    selA[:, t, h, 0:blk].unsqueeze(2).to_broadcast([128, blk, 65]), ALU.mult,
                                    [ka0, "selA"], [("atmp", ai)])
                            S.op("dve", (lambda a_, b_: (lambda e: e.tensor_reduce(
                                out=red[a_][:], in_=tmp[a_][:, 0:b_ * 65].rearrange("p (n d) -> p d n", n=b_), axis=AX.X, op=ALU.add)))(ai, blk),
                                reads=[("atmp", ai)], writes=[("ared", ai)])
                            self.tt(tot[ai][:], red[ai][:], acc1[:, 0:65], ALU.add, [("ared", ai), ka1], [("atot", ai)])
                            self.recip(rinv[ai][:], tot[ai][:, 64:65], [("atot", ai)], [("rinv", ai)])
                            self.ts(yo, tot[ai][:, 0:64], rinv[ai][:, 0:1], None, ALU.mult, None, [("atot", ai), ("rinv", ai)], ["yat"])

                    DEP = 3
                    for idx in range(len(chunks_l)):
                        emit_qk(idx)
                        if idx >= DEP:
                            emit_pv(idx - DEP)
                    for idx in range(max(0, len(chunks_l) - DEP), len(chunks_l)):
                        emit_pv(idx)
                    for t in (range(NT) if "Y" not in self.att_skip else []):
                        tl = slice(t * 128, (t + 1) * 128)
                        bk = t % 4
                        pb = self.bankb(bk)
                        for i in range(2):
                            self.tr(pb[:, i * 128:(i + 1) * 128], yat[:, t, i * 128:(i + 1) * 128], self.ident, ["yat", "cb"], ["ps%d" % bk])
                        self.evac(yattT[:, 2 * hg:2 * hg + 2, tl], pb[:, 0:256].rearrange("p (c l) -> p c l", c=2), ["ps%d" % bk], ["yattT"])
                S.barrier()
            self.dbg("yattT", yattT[:], sq, ["yattT"])
            if "M" in self.att_skip:
                return
            with ExitStack() as a2:
                sbm = lambda n, s, d: a2.enter_context(nc.sbuf_tensor("%s_%d" % (n, sq), s, d))
                watt = [sbm("watt%d" % i, [128, 8, 512], BF16) for i in range(2)]
                wg = [sbm("wga%d" % i, [128, 8, 512], BF16) for i in range(2)]
                gsb = [sbm("gsa%d" % i, [128, 512], BF16) for i in range(2)]
                tmpm = [sbm("tmpm%d" % i, [128, 512], BF16) for i in range(2)]
                w_att_r = self.w_att.rearrange("(k p) n -> p k n", p=128)
                for dh in range(2):
                    self.dma("pool", watt[dh][:], w_att_r[:, :, dh * 512:(dh + 1) * 512], [], [("watt", dh)])
                    self.dma("pool", wg[dh][:], self.w_in_r[:, :, OG + 1024 + dh * 512:OG + 1024 + (dh + 1) * 512], [], [("wga", dh)])
                it = 0
                for dh in range(2):
                    for j in range(4):
                        dch = dh * 4 + j
                        for tb in range(4):
                            ts_ = slice(tb * 512, (tb + 1) * 512)
                            bP = (it % 4) * 2
                            bG = bP + 1
                            i2 = it % 2
                            it += 1
                            for k in range(8):
                                self.mm(self.bank(bP), watt[dh][:, k, j * 128:(j + 1) * 128], yattT[:, k, ts_], k == 0, k == 7,
                                        [("watt", dh), "yattT"], ["ps%d" % bP])
                            for k in range(8):
                                self.mm(self.bank(bG), wg[dh][:, k, j * 128:(j + 1) * 128], self.hnT[:, k, ts_], k == 0, k == 7,
                                        [("wga", dh), "hnT"], ["ps%d" % bG])
                            self.act(gsb[i2][:], self.bank(bG), AF.Sigmoid, ["ps%d" % bG, "pp"], [("gsa", i2)],
                                     bias=self.pp[:, BG0 + 8 + dch:BG0 + 8 + dch + 1])
                            self.tt(tmpm[i2][:], gsb[i2][:], self.bank(bP), ALU.mult, [("gsa", i2), "ps%d" % bP], [("tmpm", i2)])
                            self.tt(self.m1[:, dch, ts_], self.m1[:, dch, ts_], tmpm[i2][:], ALU.add, ["m1", ("tmpm", i2)], ["m1"], eng="pool")
                self.dbg("m1", self.m1[:], sq, ["m1"])
                S.barrier()

    def phase_ffn(self, sq):
        nc = self.nc
        S = self.S
        h2T = self.m1
        with ExitStack() as ph:
            sb = lambda n, s, d: ph.enter_context(nc.sbuf_tensor("%s_%d" % (n, sq), s, d))
            xn = sb("xn", [128, 16, D], F32)
            with ExitStack() as s1:
                sb1 = lambda n, s, d: s1.enter_context(nc.sbuf_tensor("%s_%d" % (n, sq), s, d))
                wout = sb1("wout", [128, 8, D], BF16)
                nwb = sb1("n2wb", [128, D], F32)
                xt = [sb1("xt%d" % i, [128, D], F32) for i in range(3)]
                h2 = [sb1("h2_%d" % i, [128, D], BF16) for i in range(2)]
                junk = sb1("junk2", [128, D], BF16)
                ss = sb1("ss2", [128, 16], F32)
                r1 = sb1("r12", [128, 16], F32)
                rstd = sb1("rstd2", [128, 16], F32)
                w_out_r = self.w_out.rearrange("(k p) n -> p k n", p=128)
                for dh in range(2):
                    self.dma("pool", wout[:, :, dh * 512:(dh + 1) * 512], w_out_r[:, :, dh * 512:(dh + 1) * 512], [], [("wout", dh)])
                self.dma("sp", nwb[:], self.nw[1:2, :].partition_broadcast(128), [], ["n2wb"])
                def partA(t):
                    tl = slice(t * 128, (t + 1) * 128)
                    b = t % 2
                    x3 = t % 3
                    self.dma("sp" if t % 2 == 0 else "pool", xt[x3][:], self.x[sq, tl, :], [], [("xt", x3)])
                    reg = self.PSD[b]
                    ks = ["ps%d" % (2 * b), "ps%d" % (2 * b + 1)]
                    for dh in range(2):
                        for k in range(8):
                            self.mm(reg[:, dh * 512:(dh + 1) * 512], self.m1[:, k, tl], wout[:, k, dh * 512:(dh + 1) * 512], k == 0, k == 7,
                                    [("m1", t), ("wout", dh)], [ks[dh]])
                    self.tt(xn[:, t, :], xt[x3][:], reg[:], ALU.add, [("xt", x3)] + ks, [("xn", t)])
                    self.act(junk[:], xn[:, t, :], AF.Square, [("xn", t)], ["junk2", ("ss2", t)], accum=ss[:, t:t + 1])
                    self.act(r1[:, t:t + 1], ss[:, t:t + 1], AF.Sqrt, [("ss2", t)], [("r12", t)], bias=self.epsc[:, 0:1], scale=1.0 / D)
                    self.recip(rstd[:, t:t + 1], r1[:, t:t + 1], [("r12", t)], [("rstd2", t)])
                    self.stt(h2[b][:], xn[:, t, :], rstd[:, t:t + 1], nwb[:], ALU.mult, ALU.mult,
                             [("xn", t), ("rstd2", t), "n2wb"], [("h2", b)])

                def partB(t):
                    tl = slice(t * 128, (t + 1) * 128)
                    b = t % 2
                    pb = self.bankb(4 + b)
                    for j in range(8):
                        self.tr(pb[:, j * 128:(j + 1) * 128], h2[b][:, j * 128:(j + 1) * 128], self.ident, [("h2", b), "cb"], ["ps%d" % (4 + b)])
                    self.evac(h2T[:, :, tl], pb.rearrange("p (k c) -> p k c", k=8), ["ps%d" % (4 + b)], [("m1", t)])

                partA(0)
                for t in range(1, NT):
                    partA(t)
                    partB(t - 1)
                partB(NT - 1)
                self.dbg("xn0", xn[:, 0:8, :], sq, [("xn", i) for i in range(8)])
                S.barrier()
            if self.upto < 5:
                return
            with ExitStack() as s2:
                sb2 = lambda n, s, d: s2.enter_context(nc.sbuf_tensor("%s_%d" % (n, sq), s, d))
                uT = sb2("uT", [128, 22, 1024], BF16)
                wa = [sb2("wa%d" % i, [128, 8, 256], BF16) for i in range(2)]
                wb_ = [sb2("wb%d" % i, [128, 8, 256], BF16) for i in range(2)]
                wd = [sb2("wd%d" % i, [128, 22, 256], BF16) for i in range(2)]
                nfwb = sb2("nfwb", [128, D], F32)
                araw = [sb2("araw%d" % i, [128, 1026], BF16) for i in range(2)]
                sa = [sb2("sa%d" % i, [128, 512], BF16) for i in range(2)]
                fdiag = [sb2("fdiag%d" % i, [128, 3, 128], BF16) for i in range(2)]
                outt = [sb2("outt%d" % i, [128, D], F32) for i in range(2)]
                junk = sb2("junk3", [128, D], BF16)
                ss = sb2("ss3", [128, 16], F32)
                r1 = sb2("r13", [128, 16], F32)
                rstd = sb2("rstd3", [128, 16], F32)
                w_up_r = self.w_up.rearrange("(k p) n -> p k n", p=128)
                w_dn_r = self.w_dn.rearrange("(k p) n -> p k n", p=128)
                self.dma("sp", nfwb[:], self.nw[2:3, :].partition_broadcast(128), [], ["nfwb"])
                wcnt = [0]

                def load_up(jg):
                    i = wcnt[0] % 2
                    wcnt[0] += 1
                    self.dma("pool", wa[i][:], w_up_r[:, :, jg * 256:(jg + 1) * 256], [], [("wa", i)])
                    self.dma("pool", wb_[i][:], w_up_r[:, :, 2816 + jg * 256:2816 + (jg + 1) * 256], [], [("wb", i)])
                    return i

                dcnt = [0]

                def load_dn(dq):
                    i = dcnt[0] % 2
                    dcnt[0] += 1
                    self.dma("pool", wd[i][:], w_dn_r[:, :, dq * 256:(dq + 1) * 256], [], [("wd", i)])
                    return i

                nxt_up = load_up(0)
                for hf in range(2):
                    tok0 = hf * 1024
                    dn_idx = {}
                    for jg in range(11):
                        wi = nxt_up
                        if jg + 1 < 11:
                            nxt_up = load_up(jg + 1)
                        else:
                            dn_idx[0] = load_dn(0)
                            dn_idx[1] = load_dn(1)
                            if hf == 0:
                                nxt_up = load_up(0)
                        for jj in range(2):
                            j = jg * 2 + jj
                            rb = j % 2
                            for k in range(3):
                                col = FW0 + j * 3 + k
                                self.S.op("act", (lambda o_, c_: lambda e: e.mul(out=o_, in_=self.ident, mul=self.pp[:, c_:c_ + 1]))(fdiag[rb][:, k, :], col),
                                          reads=["cb", "pp"], writes=[("fdiag", rb)])
                            if hf == 0:
                                self.memset(araw[rb][:, 0:2], 0.0, [], [("araw", rb)])
                            else:
                                self.vcopy(araw[rb][:, 0:2], self.carry[:, j, :], ["carry"], [("araw", rb)], eng="pool")
                            for tb in range(2):
                                ts_ = slice(tok0 + tb * 512, tok0 + (tb + 1) * 512)
                                for k in range(8):
                                    self.mm(self.bank(tb), wa[wi][:, k, jj * 128:(jj + 1) * 128], h2T[:, k, ts_], k == 0, k == 7,
                                            [("wa", wi), "h2T"], ["ps%d" % tb])
                                for k in range(8):
                                    self.mm(self.bank(2 + tb), wb_[wi][:, k, jj * 128:(jj + 1) * 128], h2T[:, k, ts_], k == 0, k == 7,
                                            [("wb", wi), "h2T"], ["ps%d" % (2 + tb)])
                                self.evac(araw[rb][:, 2 + tb * 512:2 + (tb + 1) * 512], self.bank(tb), ["ps%d" % tb], [("araw", rb)])
                            if hf == 0:
                                self.vcopy(self.carry[:, j, :], araw[rb][:, 1024:1026], [("araw", rb)], ["carry"], eng="pool")
                            for tb in range(2):
                                us_ = slice(tb * 512, (tb + 1) * 512)
                                for k in range(3):
                                    self.mm(self.bank(4 + tb), fdiag[rb][:, k, :], araw[rb][:, tb * 512 + k:tb * 512 + k + 512], k == 0, k == 2,
                                            [("fdiag", rb), ("araw", rb)], ["ps%d" % (4 + tb)])
                                self.act(sa[tb][:], self.bank(4 + tb), AF.Silu, ["ps%d" % (4 + tb), "pp"], [("sa", tb)],
                                         bias=self.pp[:, FB0 + j:FB0 + j + 1])
                                self.tt(uT[:, j, us_], sa[tb][:], self.bank(2 + tb), ALU.mult, [("sa", tb), "ps%d" % (2 + tb)], ["uT"])
                    for dq in range(4):
                        if dq >= 2:
                            dn_idx[dq] = load_dn(dq)
                        di = dn_idx[dq]
                        for tl_ in range(8):
                            t = hf * 8 + tl_
                            bk = 6 + tl_ % 2
                            for j in range(22):
                                self.mm(self.bank(bk)[:, 0:256], uT[:, j, tl_ * 128:(tl_ + 1) * 128], wd[di][:, j, :], j == 0, j == 21,
                                        ["uT", ("wd", di)], ["ps%d" % bk])
                            self.tt(xn[:, t, dq * 256:(dq + 1) * 256], xn[:, t, dq * 256:(dq + 1) * 256], self.bank(bk)[:, 0:256], ALU.add,
                                    [("xn", t), "ps%d" % bk], [("xn", t)])
                    for tl_ in range(8):
                        t = hf * 8 + tl_
                        b = t % 2
                        self.act(junk[:], xn[:, t, :], AF.Square, [("xn", t)], ["junk3", ("ss3", t)], accum=ss[:, t:t + 1])
                        self.act(r1[:, t:t + 1], ss[:, t:t + 1], AF.Sqrt, [("ss3", t)], [("r13", t)], bias=self.epsc[:, 0:1], scale=1.0 / D)
                        self.recip(rstd[:, t:t + 1], r1[:, t:t + 1], [("r13", t)], [("rstd3", t)])
                        self.stt(outt[b][:], xn[:, t, :], rstd[:, t:t + 1], nfwb[:], ALU.mult, ALU.mult,
                                 [("xn", t), ("rstd3", t), "nfwb"], [("outt", b)])
                        self.dma("sp", self.out[sq, t * 128:(t + 1) * 128, :], outt[b][:], [("outt", b)], [("out", t)])
                S.barrier()


class KB(PhasesA, PhasesB):
    def __init__(self, nc, nseq, dbg_names):
        self.nc = nc
        self.nseq = nseq
        self.S = Sched(nc)
        self.dbg_names = dbg_names
        self.dbg_t = {}
        self.evac_rr = 0

    def mm(self, out, lhsT, rhs, start, stop, r, w):
        self.S.op("pe", lambda e: e.matmul(out, lhsT=lhsT, rhs=rhs, start=start, stop=stop, skip_group_check=True), reads=r, writes=w)

    def tr(self, out, in_, ident, r, w):
        self.S.op("pe", lambda e: e.transpose(out=out, in_=in_, identity=ident), reads=r, writes=w)

    def act(self, out, in_, func, r, w, bias=None, scale=1.0, accum=None):
        kw = {}
        if bias is not None:
            kw["bias"] = bias
        if accum is not None:
            kw["accum_out"] = accum
        self.S.op("act", lambda e: e.activation(out=out, in_=in_, func=func, scale=scale, **kw), reads=r, writes=w)

    def acopy(self, out, in_, r, w):
        self.S.op("act", lambda e: e.copy(out=out, in_=in_), reads=r, writes=w)

    def vcopy(self, out, in_, r, w, eng="dve"):
        self.S.op(eng, lambda e: e.tensor_copy(out=out, in_=in_), reads=r, writes=w)

    def evac(self, out, in_, r, w):
        self.evac_rr += 1
        if self.evac_rr % 2:
            self.acopy(out, in_, r, w)
        else:
            self.vcopy(out, in_, r, w)

    def tt(self, out, in0, in1, op, r, w, eng="dve"):
        self.S.op(eng, lambda e: e.tensor_tensor(out=out, in0=in0, in1=in1, op=op), reads=r, writes=w)

    def ts(self, out, in0, s1, s2, op0, op1, r, w, eng="dve"):
        if s2 is None:
            self.S.op(eng, lambda e: e.tensor_scalar(out=out, in0=in0, scalar1=s1, scalar2=None, op0=op0), reads=r, writes=w)
        else:
            self.S.op(eng, lambda e: e.tensor_scalar(out=out, in0=in0, scalar1=s1, scalar2=s2, op0=op0, op1=op1), reads=r, writes=w)

    def stt(self, out, in0, scalar, in1, op0, op1, r, w):
        self.S.op("dve", lambda e: e.scalar_tensor_tensor(out=out, in0=in0, scalar=scalar, in1=in1, op0=op0, op1=op1), reads=r, writes=w)

    def recip(self, out, in_, r, w):
        self.S.op("dve", lambda e: e.reciprocal(out=out, in_=in_), reads=r, writes=w)

    def memset(self, ap, val, r, w, eng="pool"):
        self.S.op(eng, lambda e: e.memset(ap, val), reads=r, writes=w)

    def dma(self, q, out, in_, r, w):
        self.S.dma(q, lambda e: e.dma_start(out=out, in_=in_), reads=r, writes=w)

    def bank(self, i):
        return self.PSD[i // 2][:, (i % 2) * 512:(i % 2) * 512 + 512]

    def bankb(self, i):
        return self.PSD[i // 2][:].bitcast(BF16)[:, (i % 2) * 1024:(i % 2) * 1024 + 1024]

    def dbg(self, name, ap, sq, r):
        if sq != 0 or name not in self.dbg_names:
            return
        shape = list(ap.shape)
        dt_ = ap.dtype
        t = self.nc.dram_tensor("dbg_" + name, shape, dt_, kind="ExternalOutput").ap()
        self.dbg_t[name] = t
        self.dma("sp", t, ap, r, ["dbgout_" + name])

    def build(self):
        nc = self.nc
        S = self.S
        nseq = self.nseq
        self.x = nc.dram_tensor("x", [nseq, T, D], F32, kind="ExternalInput").ap()
        self.w_in = nc.dram_tensor("w_in", [D, 11296], F32, kind="ExternalInput").ap()
        self.w_ssd = nc.dram_tensor("w_ssd", [2048, D], F32, kind="ExternalInput").ap()
        self.w_att = nc.dram_tensor("w_att", [D, D], F32, kind="ExternalInput").ap()
        self.w_out = nc.dram_tensor("w_out", [D, D], F32, kind="ExternalInput").ap()
        self.w_up = nc.dram_tensor("w_up", [D, 5632], F32, kind="ExternalInput").ap()
        self.w_dn = nc.dram_tensor("w_dn", [2816, D], F32, kind="ExternalInput").ap()
        self.nw = nc.dram_tensor("nw", [3, D], F32, kind="ExternalInput").ap()
        self.pp_d = nc.dram_tensor("pp", [128, NPP], F32, kind="ExternalInput").ap()
        self.cf_d = nc.dram_tensor("cf", [128, NCF], F32, kind="ExternalInput").ap()
        self.cb_d = nc.dram_tensor("cb", [128, NCB], F32, kind="ExternalInput").ap()
        self.selg_d = nc.dram_tensor("selg", [128, 8192], F32, kind="ExternalInput").ap()
        self.out = nc.dram_tensor("out", [nseq, T, D], F32, kind="ExternalOutput").ap()
        self.yn_d = nc.dram_tensor("yn_scr", [16, 128, T], BF16, kind="Internal").ap()
        self.w_in_r = self.w_in.rearrange("(k p) n -> p k n", p=128)

        with ExitStack() as st:
            S.open(st)
            self.PSD = [st.enter_context(nc.psum_tensor("psd%d" % i, [128, 1024], F32)) for i in range(4)]
            sb = lambda n, s, d: st.enter_context(nc.sbuf_tensor(n, s, d))
            self.pp = sb("pp_sb", [128, NPP], F32)
            self.cf = sb("cf_sb", [128, NCF], F32)
            self.cbp = sb("cb_sb", [128, NCB], BF16)
            self.ones_bf = sb("ones_bf", [128, 128], BF16)
            self.ones_f = sb("ones_f", [128, 128], F32)
            self.dma("sp", self.pp[:], self.pp_d, [], ["pp"])
            self.dma("sp", self.cf[:], self.cf_d, [], ["cf"])
            self.dma("pool", self.cbp[:], self.cb_d, [], ["cb"])
            self.memset(self.ones_bf[:], 1.0, [], ["ones_bf"])
            self.memset(self.ones_f[:], 1.0, [], ["ones_f"])
            self.epsc = sb("epsc", [128, 1], F32)
            self.onec = sb("onec", [128, 1], F32)
            self.carry = sb("carry", [128, 22, 2], BF16)
            self.memset(self.epsc[:], EPS, [], ["epsc"])
            self.memset(self.onec[:], 1.0, [], ["onec"])
            self.ident = self.cbp[:, 0:128]
            self.neg4 = self.cbp[:, 128:640]
            self.ident_f = self.cf[:, 256:384]
            self.CONST = ["pp", "cf", "cb", "ones_bf", "ones_f"]

            for sq in range(nseq):
                with ExitStack() as sst:
                    self.m1 = sst.enter_context(nc.sbuf_tensor("m1_%d" % sq, [128, 8, T], BF16))
                    with ExitStack() as hst:
                        self.hnT = hst.enter_context(nc.sbuf_tensor("hnT%d" % sq, [128, 8, T], BF16))
                        self.phase_norm1(sq)
                        S.barrier()
                        if self.upto >= 2 and not SKIP_SSD:
                            self.phase_dt_ssd(sq)
                            S.barrier()
                            self.phase_ssdproj(sq)
                            S.barrier()
                        if self.upto >= 3:
                            self.phase_att(sq)
                            S.barrier()
                    if self.upto >= 4:
                        self.phase_ffn(sq)
                        S.barrier()
                S.barrier()
            S.emit()
        return nc


ATT_LVL = 2
ATT_SKIP = ""
SKIP_SSD = False
def _consts():
    cf = np.zeros((128, NCF), np.float32)
    inv_freq = (np.float32(500000.0) ** (-np.arange(0, 16, 2, dtype=np.float32) / np.float32(16))).astype(np.float32)
    pos = np.arange(T, dtype=np.float32)
    ang = (pos[:, None] * inv_freq[None, :]).astype(np.float32)
    cos = np.cos(ang).astype(np.float32)
    sin = np.sin(ang).astype(np.float32)
    cf[:, 0:128] = cos.reshape(16, 128, 8).transpose(1, 0, 2).reshape(128, 128)
    cf[:, 128:256] = sin.reshape(16, 128, 8).transpose(1, 0, 2).reshape(128, 128)
    cf[:, 256:384] = np.eye(128, dtype=np.float32)
    cf[0:64, 384] = 1.0
    cf[64:128, 385] = 1.0
    gm = np.zeros((8, 4, 8), np.float32)
    for t in range(8, 16):
        gm[t - 8, :, (t // 2):] = -1.0e30
    cf[:, 386:642] = gm.reshape(1, 256)
    cb = np.zeros((128, NCB), np.float32)
    cb[:, 0:128] = np.eye(128, dtype=np.float32)
    s = np.arange(128)[:, None]
    l = np.arange(128)[None, :]
    neg = np.where(s > l, NEGV, 0.0).astype(np.float32)
    cb[:, 128:640] = np.tile(neg, (1, 4))
    selg = np.zeros((128, 8192), np.float32)
    for j in range(32):
        selg[j, j * 128:(j + 1) * 128] = 1.0
        selg[32 + j, j * 128:(j + 1) * 128] = 1.0
    for g in range(8):
        for h in range(4):
            selg[4 * g + h, 4096 + g * 512 + h * 128:4096 + g * 512 + (h + 1) * 128] = 1.0
            selg[32 + 4 * g + h, 4096 + g * 512 + h * 128:4096 + g * 512 + (h + 1) * 128] = 1.0
    return cf, cb, selg


def _layout(inputs):
    f = lambda a: np.ascontiguousarray(np.asarray(a, dtype=np.float32))
    pp = np.zeros((128, NPP), np.float32)
    cw = f(inputs["ssd_conv_w"])[0]
    pp[:, CW0:CW0 + 128] = cw.reshape(4, 32, 128).transpose(2, 1, 0).reshape(128, 128)
    pp[:, CB0:CB0 + 32] = f(inputs["ssd_conv_b"])[0].reshape(32, 128).T
    fw = f(inputs["ffn_conv_w"])[0]
    pp[:, FW0:FW0 + 66] = fw.reshape(3, 22, 128).transpose(2, 1, 0).reshape(128, 66)
    pp[:, FB0:FB0 + 22] = f(inputs["ffn_conv_b"])[0].reshape(22, 128).T
    pp[:, BG0:BG0 + 16] = f(inputs["b_gate"])[0].reshape(16, 128).T
    pp[:, SN0:SN0 + 16] = f(inputs["ssd_norm_w"])[0].reshape(16, 128).T
    pp[:, DP0:DP0 + 16] = np.repeat(f(inputs["ssd_d"])[0], 64).reshape(16, 128).T
    pp[0:32, DTB] = f(inputs["ssd_dt_bias"])[0]
    pp[0:32, ALG] = f(inputs["ssd_a_log"])[0]
    pp[32:64, DTB] = f(inputs["ssd_dt_bias"])[0]
    pp[32:64, ALG] = f(inputs["ssd_a_log"])[0]
    nw = np.stack([f(inputs["norm1_w"])[0], f(inputs["norm2_w"])[0], f(inputs["norm_f_w"])])
    cf, cb, selg = _consts()
    base = {
        "w_in": f(inputs["w_in"])[0], "w_ssd": f(inputs["w_ssd_proj"])[0], "w_att": f(inputs["w_att_proj"])[0],
        "w_out": f(inputs["w_out"])[0], "w_up": f(inputs["w_ffn_up"])[0], "w_dn": f(inputs["w_ffn_down"])[0],
        "nw": nw, "pp": pp, "cf": cf, "cb": cb, "selg": selg,
    }
    return base


def build_program(nseq=2, dbg_names=(), upto=5):
    nc = bass.Bass("TRN2", target_bir_lowering=False)
    kb = KB(nc, nseq, dbg_names)
    kb.upto = upto
    kb.att_lvl = ATT_LVL
    kb.att_skip = ATT_SKIP
    kb.build()
    return nc, kb


def kernel(**inputs):
    n = 8
    x = np.ascontiguousarray(np.asarray(inputs["x"], dtype=np.float32))
    base = _layout(inputs)
    nseq = x.shape[0] // n
    nc, kb = build_program(nseq=nseq)
    in_maps = []
    for i in range(n):
        m = dict(base)
        m["x"] = np.ascontiguousarray(x[i * nseq:(i + 1) * nseq])
        in_maps.append(m)
    res = run_bass_kernel_spmd(nc, in_maps, core_ids=list(range(n)))
    return np.concatenate([np.asarray(r["out"], dtype=np.float32) for r in res.results], axis=0)
```
